# Optimizing a Trainium2 kernel written in Bass

```python
import jax
import jax.numpy as jnp
from jax import lax
import numpy as np

D_MODEL = 1024
BATCH = 16
SEQ = 4096
DEPTH = 1

CHUNK = 64
Q_BLOCK = 2 * CHUNK
FOX_HEADS = 8
FOX_HEAD_DIM = 64
FOX_WIDTH = FOX_HEADS * FOX_HEAD_DIM
RWKV_HEADS = 8
RWKV_HEAD_DIM = 64
RWKV_WIDTH = RWKV_HEADS * RWKV_HEAD_DIM
DECAY_LORA = 64
AAA_LORA = 64
GATE_LORA = 128
D_FF = 2816
CONV_WIDTH = 3
N_BRANCHES = 2
N_MOD = 6
RMS_EPS = 1e-6
GN_EPS = RWKV_HEAD_DIM * 1e-5
L2_EPS = 1e-12

FOX_SPLITS = (FOX_WIDTH, FOX_WIDTH, FOX_WIDTH, FOX_HEADS)
RWKV_SPLITS = (RWKV_WIDTH, RWKV_WIDTH, RWKV_WIDTH, DECAY_LORA, AAA_LORA, GATE_LORA)
FOX_COLS = 3 * FOX_WIDTH + FOX_HEADS
RWKV_COLS = 3 * RWKV_WIDTH + DECAY_LORA + AAA_LORA + GATE_LORA
GATE_COLS = N_BRANCHES * D_MODEL
IN_COLS = FOX_COLS + RWKV_COLS + GATE_COLS

kernel_name = 'fox_rwkv7_hybrid_block'


def _split(t, sizes):
    offs = np.cumsum(sizes)[:-1].tolist()
    return jnp.split(t, offs, axis=-1)


def rms_norm(x, gain):
    xf = x.astype(jnp.float32)
    y = xf * lax.rsqrt(jnp.mean(xf * xf, axis=-1, keepdims=True) + RMS_EPS)
    return (y * gain).astype(x.dtype)


def modulate(xn, shift, scale):
    return xn * (1 + scale[:, None, :]) + shift[:, None, :]


def token_shift(t):
    return jnp.pad(t[:, :-1], ((0, 0), (1, 0), (0, 0)))


def fox_attention(q, k, v, log_f):
    seq = q.shape[2]
    cum = jnp.cumsum(log_f, axis=-1)
    scale = FOX_HEAD_DIM ** -0.5
    outs = []
    for blk in range(seq // Q_BLOCK):
        q0 = blk * Q_BLOCK
        q1 = q0 + Q_BLOCK
        logits = jnp.einsum('bhqd,bhkd->bhqk', q[:, :, q0:q1], k[:, :, :q1]).astype(jnp.float32) * scale
        logits = logits + cum[:, :, q0:q1, None] - cum[:, :, None, :q1]
        causal = jnp.arange(q1)[None, :] <= (q0 + jnp.arange(Q_BLOCK))[:, None]
        logits = jnp.where(causal, logits, -jnp.inf)
        probs = jax.nn.softmax(logits, axis=-1).astype(v.dtype)
        outs.append(jnp.einsum('bhqk,bhkd->bhqd', probs, v[:, :, :q1]))
    return jnp.concatenate(outs, axis=2)


def rwkv7_time_mix(p, mu, w0, w2, a0, a2, g2, k_k, k_a, r_k, ln_w, ln_b):
    bsz, seq, _ = p.shape
    p = p + (token_shift(p) - p) * mu
    r, k, v, xw, xa, xg = _split(p, RWKV_SPLITS)
    w_log = -jax.nn.softplus(-(w0 + jnp.tanh(xw) @ w2).astype(jnp.float32)) - 0.5
    decay = jnp.exp(-jnp.exp(w_log))
    a = jax.nn.sigmoid((a0 + xa @ a2).astype(jnp.float32))
    g = jax.nn.sigmoid(xg) @ g2

    def heads(t):
        return t.astype(jnp.float32).reshape(bsz, seq, RWKV_HEADS, RWKV_HEAD_DIM)

    kk = heads(k * k_k)
    kk = kk / jnp.maximum(jnp.sqrt(jnp.sum(kk * kk, axis=-1, keepdims=True)), L2_EPS)
    k_mod = k.astype(jnp.float32) * (1 + (a - 1) * k_a)
    r_h, k_h, v_h, w_h, a_h = heads(r), heads(k_mod), heads(v), heads(decay), heads(a)
    a_vec = -kk
    b_vec = kk * a_h

    def step(state, inp):
        r_t, w_t, k_t, v_t, a_t, b_t = inp
        sa = jnp.einsum('bhvk,bhk->bhv', state, a_t)
        state = state * w_t[:, :, None, :] + sa[..., None] * b_t[:, :, None, :] + v_t[..., None] * k_t[:, :, None, :]
        y_t = jnp.einsum('bhvk,bhk->bhv', state, r_t)
        return state, y_t

    tm = lambda t: jnp.moveaxis(t, 1, 0)
    state0 = jnp.zeros((bsz, RWKV_HEADS, RWKV_HEAD_DIM, RWKV_HEAD_DIM), jnp.float32)
    _, y = lax.scan(step, state0, (tm(r_h), tm(w_h), tm(k_h), tm(v_h), tm(a_vec), tm(b_vec)))
    y = jnp.moveaxis(y, 0, 1)
    mean = jnp.mean(y, axis=-1, keepdims=True)
    var = jnp.mean(jnp.square(y - mean), axis=-1, keepdims=True)
    y = ((y - mean) * lax.rsqrt(var + GN_EPS)).reshape(bsz, seq, RWKV_WIDTH) * ln_w + ln_b
    bonus = (jnp.sum(r_h * k_h * r_k, axis=-1, keepdims=True) * v_h).reshape(bsz, seq, RWKV_WIDTH)
    return ((y + bonus) * g).astype(p.dtype)


def token_mixer(u, w_in, fox_b_f, fox_w_out, rwkv_mu, rwkv_w0, rwkv_w2, rwkv_a0, rwkv_a2, rwkv_g2,
                rwkv_k_k, rwkv_k_a, rwkv_r_k, rwkv_ln_w, rwkv_ln_b, rwkv_w_out, w_o):
    bsz, seq, _ = u.shape
    proj = u @ w_in
    p_fox, p_rwkv, p_gate = _split(proj, (FOX_COLS, RWKV_COLS, GATE_COLS))
    q, k, v, f = _split(p_fox, FOX_SPLITS)
    to_heads = lambda t: t.reshape(bsz, seq, FOX_HEADS, FOX_HEAD_DIM).transpose(0, 2, 1, 3)
    log_f = jax.nn.log_sigmoid((f + fox_b_f).astype(jnp.float32)).transpose(0, 2, 1)
    o_fox = fox_attention(to_heads(q), to_heads(k), to_heads(v), log_f)
    o_fox = o_fox.transpose(0, 2, 1, 3).reshape(bsz, seq, FOX_WIDTH)
    o_rwkv = rwkv7_time_mix(p_rwkv, rwkv_mu, rwkv_w0, rwkv_w2, rwkv_a0, rwkv_a2, rwkv_g2,
                            rwkv_k_k, rwkv_k_a, rwkv_r_k, rwkv_ln_w, rwkv_ln_b)
    gate_fox, gate_rwkv = jnp.split(jax.nn.sigmoid(p_gate), N_BRANCHES, axis=-1)
    merged = gate_fox * (o_fox @ fox_w_out) + gate_rwkv * (o_rwkv @ rwkv_w_out)
    return merged @ w_o


def conv_ffn(u, w_up, conv_w, conv_b, w_down):
    a, b = jnp.split(u @ w_up, 2, axis=-1)
    a = lax.conv_general_dilated(a, conv_w[:, None, :], window_strides=(1,), padding=[(CONV_WIDTH - 1, 0)],
                                 dimension_numbers=('NWC', 'WIO', 'NWC'), feature_group_count=D_FF) + conv_b
    return (jax.nn.silu(a) * b) @ w_down


def setup_inputs(seed: int = 0) -> dict:
    key = jax.random.key(seed)
    ks = jax.random.split(key, 32)
    L = DEPTH
    f32 = jnp.float32
    nrm = lambda kk, shape, s: jax.random.normal(kk, shape, f32) * s
    return {
        'x': nrm(ks[0], (BATCH, SEQ, D_MODEL), 1.0),
        'c': nrm(ks[1], (BATCH, D_MODEL), 1.0),
        'w_mod': nrm(ks[2], (L, D_MODEL, N_MOD * D_MODEL), D_MODEL ** -0.5),
        'b_mod': nrm(ks[3], (L, N_MOD * D_MODEL), 0.02),
        'ln1_g': 1.0 + nrm(ks[4], (L, D_MODEL), 0.02),
        'w_in': nrm(ks[5], (L, D_MODEL, IN_COLS), D_MODEL ** -0.5),
        'fox_b_f': 2.0 + nrm(ks[6], (L, FOX_HEADS), 0.1),
        'fox_w_out': nrm(ks[7], (L, FOX_WIDTH, D_MODEL), FOX_WIDTH ** -0.5),
        'rwkv_mu': jax.random.uniform(ks[8], (L, RWKV_COLS), f32, 0.0, 1.0),
        'rwkv_w0': jax.random.uniform(ks[9], (L, RWKV_WIDTH), f32, -4.0, 0.0),
        'rwkv_w2': nrm(ks[10], (L, DECAY_LORA, RWKV_WIDTH), 0.1 * DECAY_LORA ** -0.5),
        'rwkv_a0': nrm(ks[11], (L, RWKV_WIDTH), 0.1),
        'rwkv_a2': nrm(ks[12], (L, AAA_LORA, RWKV_WIDTH), AAA_LORA ** -0.5),
        'rwkv_g2': nrm(ks[13], (L, GATE_LORA, RWKV_WIDTH), GATE_LORA ** -0.5),
        'rwkv_k_k': 0.85 + nrm(ks[14], (L, RWKV_WIDTH), 0.05),
        'rwkv_k_a': 1.0 + nrm(ks[15], (L, RWKV_WIDTH), 0.05),
        'rwkv_r_k': nrm(ks[16], (L, RWKV_HEADS, RWKV_HEAD_DIM), 0.1),
        'rwkv_ln_w': 1.0 + nrm(ks[17], (L, RWKV_WIDTH), 0.02),
        'rwkv_ln_b': nrm(ks[18], (L, RWKV_WIDTH), 0.02),
        'rwkv_w_out': nrm(ks[19], (L, RWKV_WIDTH, D_MODEL), RWKV_WIDTH ** -0.5),
        'w_o': nrm(ks[20], (L, D_MODEL, D_MODEL), D_MODEL ** -0.5),
        'ln2_g': 1.0 + nrm(ks[21], (L, D_MODEL), 0.02),
        'w_up': nrm(ks[22], (L, D_MODEL, 2 * D_FF), D_MODEL ** -0.5),
        'conv_w': nrm(ks[23], (L, CONV_WIDTH, D_FF), CONV_WIDTH ** -0.5),
        'conv_b': nrm(ks[24], (L, D_FF), 0.02),
        'w_down': nrm(ks[25], (L, D_FF, D_MODEL), D_FF ** -0.5),
        'final_g': 1.0 + nrm(ks[26], (D_MODEL,), 0.02),
    }


def reference(x, c, w_mod, b_mod, ln1_g, w_in, fox_b_f, fox_w_out, rwkv_mu, rwkv_w0, rwkv_w2, rwkv_a0,
              rwkv_a2, rwkv_g2, rwkv_k_k, rwkv_k_a, rwkv_r_k, rwkv_ln_w, rwkv_ln_b, rwkv_w_out, w_o,
              ln2_g, w_up, conv_w, conv_b, w_down, final_g):
    h = x
    for l in range(DEPTH):
        mod = jax.nn.silu(c) @ w_mod[l] + b_mod[l]
        shift1, scale1, gate1, shift2, scale2, gate2 = jnp.split(mod, N_MOD, axis=-1)
        u = modulate(rms_norm(h, ln1_g[l]), shift1, scale1)
        mix = token_mixer(u, w_in[l], fox_b_f[l], fox_w_out[l], rwkv_mu[l], rwkv_w0[l], rwkv_w2[l],
                          rwkv_a0[l], rwkv_a2[l], rwkv_g2[l], rwkv_k_k[l], rwkv_k_a[l], rwkv_r_k[l],
                          rwkv_ln_w[l], rwkv_ln_b[l], rwkv_w_out[l], w_o[l])
        h = h + gate1[:, None, :] * mix
        u2 = modulate(rms_norm(h, ln2_g[l]), shift2, scale2)
        h = h + gate2[:, None, :] * conv_ffn(u2, w_up[l], conv_w[l], conv_b[l], w_down[l])
    return rms_norm(h, final_g)
```

```python
import numpy as np
import concourse.bass as bass
import concourse.mybir as mybir
from concourse.bass_utils import run_bass_kernel_spmd
from contextlib import ExitStack

F32 = mybir.dt.float32
BF16 = mybir.dt.bfloat16
AF = mybir.ActivationFunctionType
ALU = mybir.AluOpType
AX = mybir.AxisListType


class Buf:
    __slots__ = ("name", "w", "r")

    def __init__(self, name):
        self.name = name
        self.w = None
        self.r = []


class Sched:
    ENGS = ("pe", "act", "dve", "pool", "sp")

    def __init__(self, nc, stack, n_dma_sems=32):
        self.nc = nc
        self.q = {e: [] for e in self.ENGS}
        self.sems = {}
        self.val = {}
        for e in ("pe", "act", "dve", "pool"):
            self.sems[e] = stack.enter_context(nc.semaphore("s_" + e))
            self.val[e] = 0
        self.free_dma = []
        for i in range(n_dma_sems):
            k = "d%d" % i
            self.sems[k] = stack.enter_context(nc.semaphore("s_" + k))
            self.val[k] = 0
            self.free_dma.append(k)
        self.dma_rr = 0
        self.sw_dma = []
        for i in range(8):
            k = "e%d" % i
            self.sems[k] = stack.enter_context(nc.semaphore("s_" + k))
            self.val[k] = 0
            self.sw_dma.append(k)
        self.sw_rr = 0
        self.waited = {e: {} for e in self.ENGS}
        self.n_wait = 0
        self.n_op = 0

    def _need(self, eng, reads, writes):
        need = {}

        def add(tok):
            if tok is None:
                return
            k, v, e = tok
            if e == eng and eng == "pe":
                return
            if need.get(k, 0) < v:
                need[k] = v
        for b in reads:
            add(b.w)
        for b in writes:
            add(b.w)
            for t in b.r:
                add(t)
        out = []
        for k, v in need.items():
            if self.waited[eng].get(k, 0) < v:
                self.waited[eng][k] = v
                out.append((k, v))
        return out

    def _emit_waits(self, eng, waits):
        for k, v in waits:
            sem = self.sems[k]
            self.q[eng].append(lambda E, sem=sem, v=v: E.wait_ge(sem, v))
            self.n_wait += 1

    def op(self, eng, fn, reads=(), writes=()):
        waits = self._need(eng, reads, writes)
        self._emit_waits(eng, waits)
        self.val[eng] += 1
        v = self.val[eng]
        sem = self.sems[eng]
        self.q[eng].append(lambda E, fn=fn, sem=sem: fn(E).then_inc(sem, 1))
        tok = (eng, v, eng)
        for b in reads:
            b.r.append(tok)
            if len(b.r) > 24:
                b.r = b.r[-24:] if False else self._compact(b.r)
        for b in writes:
            b.w = tok
            b.r = []
        self.n_op += 1

    @staticmethod
    def _compact(toks):
        best = {}
        for k, v, e in toks:
            if k not in best or best[k][1] < v:
                best[k] = (k, v, e)
        return list(best.values())

    def dma(self, fn, reads=(), writes=(), queue="sp"):
        waits = self._need(queue, reads, writes)
        self._emit_waits(queue, waits)
        if queue == "pool":
            k = self.sw_dma[self.sw_rr % len(self.sw_dma)]
            self.sw_rr += 1
        else:
            k = self.free_dma[self.dma_rr % len(self.free_dma)]
            self.dma_rr += 1
        if self.val[k] > 0 and self.waited[queue].get(k, 0) < self.val[k]:
            self.waited[queue][k] = self.val[k]
            self._emit_waits(queue, [(k, self.val[k])])
        self.val[k] += 16
        v = self.val[k]
        sem = self.sems[k]
        self.q[queue].append(lambda E, fn=fn, sem=sem: fn(E).then_inc(sem, 16))
        tok = (k, v, "dma")
        for b in reads:
            b.r.append(tok)
        for b in writes:
            b.w = tok
            b.r = []
        self.n_op += 1

    def barrier(self):
        for eng in self.ENGS:
            waits = []
            for k, v in self.val.items():
                if v > 0 and self.waited[eng].get(k, 0) < v and not (k == eng):
                    self.waited[eng][k] = v
                    waits.append((k, v))
            self._emit_waits(eng, waits)

    def final_wait(self, bufs, eng="sp"):
        waits = self._need(eng, bufs, bufs)
        self._emit_waits(eng, waits)

    def emit(self):
        nc = self.nc
        with nc.Block() as block:
            @block.sync
            def _(E):
                for f in self.q["sp"]:
                    f(E)

            @block.tensor
            def _(E):
                for f in self.q["pe"]:
                    f(E)

            @block.scalar
            def _(E):
                for f in self.q["act"]:
                    f(E)

            @block.vector
            def _(E):
                for f in self.q["dve"]:
                    f(E)

            @block.gpsimd
            def _(E):
                for f in self.q["pool"]:
                    f(E)
D = 1024; NH = 8; DH = 64; DFF = 2816; NRW = 15
TT = 512
P3DT = BF16
P3D2 = BF16


def build(NB, S, debug=False):
    nc = bass.Bass("TRN2", target_bir_lowering=False)
    TOK = NB * S
    NTL = S // TT
    dk = "ExternalOutput" if debug else "Internal"
    def din(name, shape, dt=F32):
        return nc.dram_tensor(name, list(shape), dt, kind="ExternalInput").ap()
    def dsc(name, shape, dt=BF16):
        return nc.dram_tensor(name, list(shape), dt, kind=dk).ap()
    x_d = din("x", [TOK, D]); cT_d = din("cT", [128, 8, NB]); wmod_d = din("w_mod", [D, 6 * D])
    bmodc_d = din("bmod_col", [128, 48]); bmodg_d = din("bmod_g", [NB, 2, D])
    ln1_d = din("ln1_col", [128, 8]); ln2_d = din("ln2_col", [128, 8]); fing_d = din("fin_g", [D])
    wqkv_d = din("w_qkv", [D, 1536]); wf_d = din("w_f", [D, 8]); wrw_d = din("w_rw", [D, NRW * 128]); wgate_d = din("w_gate", [D, 2048])
    bf_d = din("fox_bf", [8, 1]); mu_d = din("mu_col", [128, NRW])
    rwp_d = din("rw_par", [128, 7, 4])
    w2_d = din("w2", [64, 512]); a2_d = din("a2", [64, 512]); g2_d = din("g2", [128, 512])
    foxo_d = din("fox_w_out", [512, D]); rwo_d = din("rwkv_w_out", [512, D]); wo_d = din("w_o", [D, D])
    wup_d = din("w_up", [D, 2 * DFF]); wdn_d = din("w_down", [DFF, D])
    convw_d = din("conv_col", [128, 22, 3]); convb_d = din("convb_col", [128, 22])
    out_d = nc.dram_tensor("out", [TOK, D], F32, kind="ExternalOutput").ap()
    q_s = dsc("q_s", [NB, NH, 68, S]); k_s = dsc("k_s", [NB, NH, 68, S]); v_s = dsc("v_s", [NB, S, NH * 65])
    gate_s = dsc("gate_s", [NB, 16, 128, S]); rw_s = dsc("rw_s", [NB, 7, NH, 64, S]); e_s = dsc("e_s", [NB, 2, NH, 64, S])
    pc_s = dsc("pc_s", [NB, NH, 64, S // 64], F32); gates_s = dsc("gates_s", [NB, 2, D], F32)
    ofox_s = dsc("ofox_s", [NB, 4, 128, S]); orw_s = dsc("orw_s", [NB, 4, 128, S]); h_s = dsc("h_s", [TOK, D], F32)

    with ExitStack() as top:
        SC = Sched(nc, top)
        B = {}
        def buf(n):
            if n not in B:
                B[n] = Buf(n)
            return B[n]
        def bl(names):
            return [buf(n) for n in names.split()] if isinstance(names, str) else [buf(n) for n in names]
        from collections import deque
        jobs = deque(); jstate = {"in": False, "n": 0, "K": 14}
        _op, _dma = SC.op, SC.dma
        def tick():
            if jstate["in"] or not jobs:
                return
            jstate["n"] += 1
            if jstate["n"] % jstate["K"] == 0:
                jstate["in"] = True
                jobs.popleft()()
                jstate["in"] = False
        def flush_jobs():
            jstate["in"] = True
            while jobs:
                jobs.popleft()()
            jstate["in"] = False
        def op_t(*a, **k):
            _op(*a, **k); tick()
        def dma_t(*a, **k):
            _dma(*a, **k); tick()
        SC.op = op_t; SC.dma = dma_t
        def mm(out, lhsT, rhs, start, stop, R, W, skip=False):
            _op("pe", lambda E: E.matmul(out, lhsT=lhsT, rhs=rhs, start=start, stop=stop, skip_group_check=skip), bl(R), bl(W))
        def act(out, in_, func, R, W, bias=None, scale=None, accum=None):
            kw = {}
            if bias is not None: kw["bias"] = bias
            if scale is not None: kw["scale"] = scale
            if accum is not None: kw["accum_out"] = accum
            SC.op("act", lambda E: E.activation(out=out, in_=in_, func=func, **kw), bl(R), bl(W))
        def tt(eng, out, in0, in1, op, R, W):
            SC.op(eng, lambda E: E.tensor_tensor(out=out, in0=in0, in1=in1, op=op), bl(R), bl(W))
        def ts(eng, out, in0, s1, s2, op0, op1, R, W):
            if op1 is None:
                SC.op(eng, lambda E: E.tensor_scalar(out=out, in0=in0, scalar1=s1, scalar2=None, op0=op0), bl(R), bl(W))
            else:
                SC.op(eng, lambda E: E.tensor_scalar(out=out, in0=in0, scalar1=s1, scalar2=s2, op0=op0, op1=op1), bl(R), bl(W))
        def stt(out, in0, sc, in1, op0, op1, R, W):
            SC.op("dve", lambda E: E.scalar_tensor_tensor(out=out, in0=in0, scalar=sc, in1=in1, op0=op0, op1=op1), bl(R), bl(W))
        def cp(eng, out, in_, R, W):
            if eng == "act":
                SC.op("act", lambda E: E.activation(out=out, in_=in_, func=AF.Identity), bl(R), bl(W))
            else:
                SC.op(eng, lambda E: E.tensor_copy(out=out, in_=in_), bl(R), bl(W))
        def mset(eng, ap, val, W):
            SC.op(eng, lambda E: E.memset(ap, val), [], bl(W))
        def dma(out, in_, R, W, queue="sp"):
            SC.dma(lambda E: E.dma_start(out=out, in_=in_), bl(R), bl(W), queue=queue)

        gsb = lambda n, shp, dt=F32: top.enter_context(nc.sbuf_tensor(n, shp, dt))
        ident = gsb("ident", [128, 128], BF16)
        cst = gsb("cst", [128, 8])
        modc = gsb("modc", [128, 6, 8, NB])
        G1 = gsb("G1", [128, 8, NB]); G2 = gsb("G2", [128, 8, NB])
        mset("pool", ident[:], 1.0, "ident")
        SC.op("pool", lambda E: E.affine_select(out=ident[:], in_=ident[:], pattern=[[-1, 128]], compare_op=ALU.is_equal, fill=0.0, base=0, channel_multiplier=1), bl("ident"), bl("ident"))
        for i, v in enumerate([1e-6, 1.0, -0.5, 0.0, 64e-5, 1e-24]):
            mset("pool", cst[:, i:i + 1], v, "cst")
        EPS, ONE, MHALF, ZERO, GNEPS = (cst[:, i:i + 1] for i in range(5))

        pw = ExitStack()
        pwsb = lambda n, shp, dt=F32: pw.enter_context(nc.sbuf_tensor(n, shp, dt))
        wqkv = pwsb("wqkv", [128, 8, 1536], BF16); wfb = pwsb("wfb", [128, 8, 8], BF16)
        wrw = pwsb("wrw", [128, 8, NRW * 128], BF16); wgate = pwsb("wgate", [128, 8, 2048], BF16)
        w2b = pwsb("w2b", [64, 512], BF16); a2b = pwsb("a2b", [64, 512], BF16); g2b = pwsb("g2b", [128, 512], BF16)
        vw = lambda d: d.rearrange("(kc p) n -> p kc n", p=128)
        dma(wqkv[:], vw(wqkv_d), "", "wqkv", queue="pool"); dma(wfb[:], vw(wf_d), "", "wfb", queue="pool")
        dma(wrw[:], vw(wrw_d), "", "wrw", queue="pool"); dma(wgate[:], vw(wgate_d), "", "wgate", queue="pool")
        dma(w2b[:], w2_d, "", "w2b", queue="pool"); dma(a2b[:], a2_d, "", "a2b", queue="pool"); dma(g2b[:], g2_d, "", "g2b", queue="pool")
        with ExitStack() as ph:
            sb = lambda n, shp, dt=F32: ph.enter_context(nc.sbuf_tensor(n, shp, dt))
            ps = lambda n, shp, dt=F32: ph.enter_context(nc.psum_tensor(n, shp, dt))
            cTt = sb("cTt", [128, 8, NB]); scT = sb("scT", [128, 8, NB]); bmodc = sb("bmodc", [128, 48]); bmodg = sb("bmodg", [NB, 2, D])
            ln1c = sb("ln1c", [128, 8]); ln2c = sb("ln2c", [128, 8]); grow = sb("grow", [NB, 2, D])
            wm = [sb("wm%d" % i, [128, 8, D]) for i in range(2)]
            pcol = ps("pcol", [128, 8, NB]); prow = [ps("prow%d" % i, [NB, 512]) for i in range(2)]
            dma(cTt[:], cT_d, "", "cTt"); dma(bmodc[:], bmodc_d, "", "bmodc"); dma(bmodg[:], bmodg_d, "", "bmodg")
            dma(ln1c[:], ln1_d, "", "ln1c"); dma(ln2c[:], ln2_d, "", "ln2c")
            act(scT[:], cTt[:], AF.Silu, "cTt", "scT")
            wmv = wmod_d.rearrange("(kc p) n -> p kc n", p=128)
            for m in range(6):
                w_ = wm[m % 2]; wn = "wm%d" % (m % 2)
                dma(w_[:], wmv[:, :, m * D:(m + 1) * D], "", wn)
                if m in (2, 5):
                    g = 0 if m == 2 else 1
                    for hf in range(2):
                        pr = prow[hf]; pn = "prow%d" % hf
                        for kc in range(8):
                            mm(pr[:, :], scT[:, kc, :], w_[:, kc, hf * 512:(hf + 1) * 512], kc == 0, kc == 7, [wn, "scT"], [pn])
                        tt("dve", grow[:, g, hf * 512:(hf + 1) * 512], pr[:, :], bmodg[:, g, hf * 512:(hf + 1) * 512], ALU.add, [pn, "bmodg"], ["grow"])
                else:
                    for j in range(8):
                        for kc in range(8):
                            mm(pcol[:, j, :], w_[:, kc, j * 128:(j + 1) * 128], scT[:, kc, :], kc == 0, kc == 7, [wn, "scT"], ["pcol"])
                    for b in range(NB):
                        tt("dve", modc[:, m, :, b], pcol[:, :, b], bmodc[:, m * 8:(m + 1) * 8], ALU.add, ["pcol", "bmodc"], ["modc"])
            dma(gates_s, grow[:], "grow", "gates_s")
            for b in range(NB):
                stt(G1[:, :, b], modc[:, 1, :, b], 1.0, ln1c[:], ALU.add, ALU.mult, ["modc", "ln1c"], ["G1"])
                stt(G2[:, :, b], modc[:, 4, :, b], 1.0, ln2c[:], ALU.add, ALU.mult, ["modc", "ln2c"], ["G2"])

        def norm_T(xt, xn, nblk, Gc, shc, b, uT, un, tmp, tmpn, ssq, rstd, xs, pT, pfx, lnexp=False):
            for blk in range(nblk):
                act(tmp, xt[:, blk, :], AF.Square, [xn], [tmpn, pfx + "ssq"], accum=ssq[:, blk:blk + 1])
            if lnexp:
                act(rstd[:, 0:nblk], ssq[:, 0:nblk], AF.Ln, [pfx + "ssq"], [pfx + "rstd"], bias=EPS, scale=1.0 / D)
                act(rstd[:, 0:nblk], rstd[:, 0:nblk], AF.Exp, [pfx + "rstd"], [pfx + "rstd"], scale=-0.5)
            else:
                act(rstd[:, 0:nblk], ssq[:, 0:nblk], AF.Sqrt, [pfx + "ssq"], [pfx + "rstd"], bias=EPS, scale=1.0 / D)
                SC.op("dve", lambda E: E.reciprocal(out=rstd[:, 0:nblk], in_=rstd[:, 0:nblk]), bl([pfx + "rstd"]), bl([pfx + "rstd"]))
            for blk in range(nblk):
                ts("dve", xs[:, blk, :], xt[:, blk, :], rstd[:, blk:blk + 1], None, ALU.mult, None, [xn, pfx + "rstd"], [pfx + "xs"])
                p_ = pT[blk % 2]; pn = pfx + "pT%d" % (blk % 2)
                for j in range(8):
                    SC.op("pe", lambda E, p_=p_, j=j, blk=blk: E.transpose(out=p_[:, j * 128:(j + 1) * 128], in_=xs[:, blk, j * 128:(j + 1) * 128], identity=ident[:]), bl([pfx + "xs", "ident"]), bl([pn]))
                for j in range(8):
                    act(uT[:, j, blk * 128:(blk + 1) * 128], p_[:, j * 128:(j + 1) * 128], AF.Identity, [pn, "G1", "G2", "modc"], [un], bias=shc[:, j, b:b + 1], scale=Gc[:, j, b:b + 1])

        SC.barrier()
        with ExitStack() as ph:
            sb = lambda n, shp, dt=F32: ph.enter_context(nc.sbuf_tensor(n, shp, dt))
            ps = lambda n, shp, dt=F32: ph.enter_context(nc.psum_tensor(n, shp, dt))
            muc = sb("muc", [128, NRW]); ommu = sb("ommu", [128, NRW]); rwp = sb("rwp", [128, 7, 4]); omka = sb("omka", [128, 4])
            nw0 = sb("nw0", [128, 4]); na0 = sb("na0", [128, 4]); bfc = sb("bfc", [8, 1]); nbf = sb("nbf", [8, 1]); blk1 = sb("blk1", [128, 128])
            dma(muc[:], mu_d, "", "muc"); dma(rwp[:], rwp_d, "", "rwp"); dma(bfc[:], bf_d, "", "bfc")
            ts("dve", ommu[:], muc[:], -1.0, 1.0, ALU.mult, ALU.add, "muc", "ommu")
            ts("dve", omka[:], rwp[:, 3, :], -1.0, 1.0, ALU.mult, ALU.add, "rwp", "omka")
            ts("dve", nw0[:], rwp[:, 0, :], -1.0, None, ALU.mult, None, "rwp", "nw0")
            ts("dve", na0[:], rwp[:, 1, :], -1.0, None, ALU.mult, None, "rwp", "na0")
            ts("dve", nbf[:], bfc[:], -1.0, None, ALU.mult, None, "bfc", "nbf")
            mset("pool", blk1[:], 0.0, "blk1"); mset("pool", blk1[0:64, 0:64], 1.0, "blk1"); mset("pool", blk1[64:128, 64:128], 1.0, "blk1")
            W0, A0, KK_, KA, RK, LNW, LNB = (lambda c4, i=i: rwp[:, i, c4:c4 + 1] for i in range(7))
            fcar = sb("fcar", [8, 1]); pcar = sb("pcar", [128, NRW])
            xt = sb("xt", [128, 4, D]); xs = sb("xs", [128, 4, D], BF16); uTl = [sb("uT%d" % i, [128, 8, TT], BF16) for i in range(2)]; uT = uTl[0]; un = "uT0"
            ssq = sb("ssq", [128, 4]); rstd = sb("rstd", [128, 4])
            qst2 = [sb("qst0", [128, TT], BF16)] * 2; vt = sb("vt", [128, 2, NH, 65], BF16)
            fe = sb("fe", [8, TT]); cum = sb("cum", [8, TT])
            hi = sb("hi", [8, TT], BF16); lo = sb("lo", [8, TT], BF16); nhi = sb("nhi", [8, TT], BF16); nlo = sb("nlo", [8, TT], BF16)
            one8 = sb("one8", [8, 2, TT], BF16)
            Tsets = [[sb("t%d_%d" % (i, s_), [128, TT]) for i in range(9)] for s_ in range(2)]; T_ = Tsets[0]
            praws = [sb("praw_%d" % s_, [128, TT + 1]) for s_ in range(2)]; praw0 = praws[0]; onesf = sb("onesf", [128, 64])
            twb = sb("twb", [64, TT], BF16); xab = sb("xab", [64, TT], BF16); sgb = sb("sgb", [128, TT], BF16)
            rwts = [sb("rwt_%d" % s_, [128, 7, TT], BF16) for s_ in range(2)]; ets = [sb("et_%d" % s_, [128, 2, TT], BF16) for s_ in range(2)]; pcts = [sb("pct_%d" % s_, [128, TT // 64]) for s_ in range(2)]
            gtl = sb("gt", [128, 2, TT], BF16)
            pT = [ps("pT%d" % i, [128, D], BF16) for i in range(2)]
            pm = [ps("pm%d" % i, [128, TT]) for i in range(3)]
            psss = [ps("pss_%d" % s_, [128, TT]) for s_ in range(2)]; pf = ps("pf", [8, TT])
            mset("pool", one8[:], 1.0, "one8"); mset("pool", onesf[:], 1.0, "onesf")
            mset("pool", vt[:], 1.0, "vt")
            pmi = [0]
            def nextpm():
                i = pmi[0] % 3; pmi[0] += 1
                return pm[i], "pm%d" % i

            def proj_fm(wt, wn, c0, M):
                p_, pn = nextpm()
                for kc in range(8):
                    mm(p_[0:M, :], wt[:, kc, c0:c0 + M], uT[:, kc, :], kc == 0, kc == 7, [wn, un], [pn])
                return p_, pn

            def lerp(cc, dst, dn, praw=None, pn_="praw_0"):
                praw = praw0 if praw is None else praw
                p_, pn = proj_fm(wrw, "wrw", cc * 128, 128)
                cp("act", praw[:, 1:TT + 1], p_[:, :], [pn], [pn_])
                cp("pool", praw[:, 0:1], pcar[:, cc:cc + 1], ["pcar"], [pn_])
                ts("dve", dst[:], praw[:, 1:TT + 1], ommu[:, cc:cc + 1], None, ALU.mult, None, [pn_, "ommu"], [dn])
                stt(dst[:], praw[:, 0:TT], muc[:, cc:cc + 1], dst[:], ALU.mult, ALU.add, [pn_, "muc", dn], [dn])
                cp("pool", pcar[:, cc:cc + 1], praw[:, TT:TT + 1], [pn_], ["pcar"])

            def prologue(gt):
                norm_T(xt, "xt", 4, G1, modc[:, 0], gt // NTL, uTl[gt % 2], "uT%d" % (gt % 2), xs[:, 0, :], "xs", ssq, rstd, xs, pT, "", lnexp=True)
                if gt + 1 < NB * NTL:
                    t1 = (gt + 1) * TT
                    dma(xt[:], x_d[t1:t1 + TT, :].rearrange("(k p) d -> p k d", p=128), "", "xt")
            dma(xt[:], x_d[0:TT, :].rearrange("(k p) d -> p k d", p=128), "", "xt")
            prologue(0)
            for b in range(NB):
                mset("pool", fcar[:], 0.0, "fcar"); mset("pool", pcar[:], 0.0, "pcar")
                for tl in range(NTL):
                    t0 = b * S + tl * TT
                    tsl = slice(tl * TT, (tl + 1) * TT)
                    gt = b * NTL + tl
                    uT = uTl[gt % 2]; un = "uT%d" % (gt % 2)
                    def qk_job(qk, hp, b=b, tsl=tsl):
                        p_, pn = proj_fm(wqkv, "wqkv", qk * 512 + hp * 128, 128)
                        st_ = qst2[(qk * 4 + hp) % 2]; sn = "qst0"
                        if qk == 0:
                            act(st_[:], p_[:, :], AF.Identity, [pn], [sn], scale=0.125)
                        else:
                            cp("dve", st_[:], p_[:, :], [pn], [sn])
                        for hh in range(2):
                            dma((q_s if qk == 0 else k_s)[b, 2 * hp + hh, 0:64, tsl], st_[hh * 64:(hh + 1) * 64, :], [sn], "qk_s")
                    def v_job(blk, b=b, tsl=tsl, tl=tl):
                        p_, pn = nextpm()
                        for kc in range(8):
                            mm(p_[:, :], uT[:, kc, blk * 128:(blk + 1) * 128], wqkv[:, kc, 1024:1536], kc == 0, kc == 7, ["wqkv", un], [pn])
                        cp("act", vt[:, blk % 2, :, 0:64], p_[:, :].rearrange("p (h d) -> p h d", h=NH), [pn], ["vt"])
                        if blk % 2 == 1:
                            hs = slice(tl * TT + (blk - 1) * 128, tl * TT + (blk + 1) * 128)
                            dma(v_s[b, hs, :].rearrange("(k p) c -> p k c", p=128), vt[:].rearrange("p k h c -> p k (h c)"), "vt", "v_s")
                    def g_job(gc, b=b, tsl=tsl):
                        gq, gg = gc // 2, gc % 2
                        p_, pn = proj_fm(wgate, "wgate", gc * 128, 128)
                        act(gtl[:, gg, :], p_[:, :], AF.Sigmoid, [pn], ["gt"])
                        if gg == 1:
                            dma(gate_s[b, gq * 2:(gq + 1) * 2, :, tsl].rearrange("g p t -> p g t"), gtl[:], "gt", "gate_s")
                    jstate["in"] = True
                    lerp(12, T_[0], "t0_0"); act(twb[:], T_[0][0:64, :], AF.Tanh, ["t0_0"], ["twb"])
                    lerp(14, T_[0], "t0_0"); act(sgb[:], T_[0][:, :], AF.Sigmoid, ["t0_0"], ["sgb"])
                    for gc in range(16):
                        g_job(gc)
                    lerp(13, T_[0], "t0_0"); cp("dve", xab[:], T_[0][0:64, :], ["t0_0"], ["xab"])
                    jstate["in"] = False
                    for qk in range(2):
                        for hp in range(4):
                            jobs.append(lambda qk=qk, hp=hp: qk_job(qk, hp))
                    for blk in range(4):
                        jobs.append(lambda blk=blk: v_job(blk))
                    if gt + 1 < NB * NTL:
                        jobs.append(lambda gt=gt: prologue(gt + 1))
                    for kc in range(8):
                        mm(pf[:, :], wfb[:, kc, :], uT[:, kc, :], kc == 0, kc == 7, ["wfb", un], ["pf"])
                    act(fe[:], pf[:, :], AF.Exp, ["pf", "nbf"], ["fe"], bias=nbf[:], scale=-1.0)
                    act(fe[:], fe[:], AF.Ln, ["fe", "cst"], ["fe"], bias=ONE[0:8, :])
                    SC.op("dve", lambda E: E.tensor_tensor_scan(out=cum[:], data0=one8[:, 0, :], data1=fe[:], initial=fcar[:], op0=ALU.mult, op1=ALU.subtract), bl("one8 fe fcar"), bl("cum"))
                    cp("pool", fcar[:], cum[:, TT - 1:TT], ["cum"], ["fcar"])
                    cp("dve", hi[:], cum[:], ["cum"], ["hi"])
                    tt("dve", lo[:], cum[:], hi[:], ALU.subtract, ["cum", "hi"], ["lo"])
                    ts("pool", nhi[:], hi[:], -1.0, None, ALU.mult, None, ["hi"], ["nhi"])
                    ts("pool", nlo[:], lo[:], -1.0, None, ALU.mult, None, ["lo"], ["nlo"])
                    dma(q_s[b, :, 64, tsl], hi[:], "hi", "qk_s"); dma(q_s[b, :, 65, tsl], lo[:], "lo", "qk_s")
                    dma(q_s[b, :, 66:68, tsl], one8[:], "one8", "qk_s"); dma(k_s[b, :, 64:66, tsl], one8[:], "one8", "qk_s")
                    dma(k_s[b, :, 66, tsl], nhi[:], "nhi", "qk_s"); dma(k_s[b, :, 67, tsl], nlo[:], "nlo", "qk_s")
                    def chain(c4, si):
                        T_s = Tsets[si]; praw = praws[si]; rwt = rwts[si]; et = ets[si]; pct = pcts[si]; pss = psss[si]
                        N = lambda s: s + '_%d' % si
                        r_, k_, v_, e2, cl, a_, g_, kk, t8 = T_s
                        lerp(c4, r_, N("t0"), praw, N("praw")); lerp(4 + c4, k_, N("t1"), praw, N("praw")); lerp(8 + c4, v_, N("t2"), praw, N("praw"))
                        yield
                        cs = slice(c4 * 128, (c4 + 1) * 128)
                        p_, pn = nextpm(); mm(p_[:, :], w2b[:, cs], twb[:], True, True, ["w2b", "twb"], [pn])
                        act(e2[:], p_[:, :], AF.Exp, [pn, "nw0"], [N("t3")], bias=nw0[:, c4:c4 + 1], scale=-1.0)
                        yield
                        act(e2[:], e2[:], AF.Ln, [N("t3"), "cst"], [N("t3")], bias=ONE)
                        yield
                        act(e2[:], e2[:], AF.Exp, [N("t3"), "cst"], [N("t3")], bias=MHALF, scale=-1.0)
                        yield
                        for c in range(TT // 64):
                            sl = slice(c * 64, (c + 1) * 64)
                            SC.op("dve", lambda E, sl=sl: E.tensor_tensor_scan(out=cl[:, sl], data0=onesf[:], data1=e2[:, sl], initial=0.0, op0=ALU.mult, op1=ALU.subtract), bl(["onesf", N("t3")]), bl([N("t4")]))
                        p_, pn = nextpm(); mm(p_[:, :], a2b[:, cs], xab[:], True, True, ["a2b", "xab"], [pn])
                        act(a_[:], p_[:, :], AF.Exp, [pn, "na0"], [N("t5")], bias=na0[:, c4:c4 + 1], scale=-1.0); act(a_[:], a_[:], AF.Ln, [N("t5"), "cst"], [N("t5")], bias=ONE); act(a_[:], a_[:], AF.Exp, [N("t5")], [N("t5")], scale=-1.0)
                        yield
                        p_, pn = nextpm(); mm(p_[:, :], g2b[:, cs], sgb[:], True, True, ["g2b", "sgb"], [pn])
                        cp("act", g_[:], p_[:, :], [pn], [N("t6")])
                        yield
                        ts("dve", kk[:], k_[:], KK_(c4), None, ALU.mult, None, [N("t1"), "rwp"], [N("t7")])
                        yield
                        tt("dve", t8[:], kk[:], kk[:], ALU.mult, [N("t7")], [N("t8")])
                        yield
                        mm(pss[:, :], blk1[:], t8[:], True, True, ["blk1", N("t8")], [N("pss")])
                        ts("dve", t8[:], pss[:, :], 1e-24, None, ALU.max, None, [N("pss")], [N("t8")])
                        yield
                        act(t8[:], t8[:], AF.Ln, [N("t8")], [N("t8")])
                        yield
                        act(t8[:], t8[:], AF.Exp, [N("t8")], [N("t8")], scale=-0.5)
                        yield
                        tt("dve", kk[:], kk[:], t8[:], ALU.mult, [N("t7"), N("t8")], [N("t7")])
                        yield
                        ts("dve", t8[:], a_[:], KA(c4), omka[:, c4:c4 + 1], ALU.mult, ALU.add, [N("t5"), "rwp", "omka"], [N("t8")])
                        yield
                        tt("dve", k_[:], k_[:], t8[:], ALU.mult, [N("t1"), N("t8")], [N("t1")])
                        yield
                        stt(t8[:], r_[:], RK(c4), k_[:], ALU.mult, ALU.mult, [N("t0"), "rwp", N("t1")], [N("t8")])
                        yield
                        mm(pss[:, :], blk1[:], t8[:], True, True, ["blk1", N("t8")], [N("pss")])
                        tt("dve", t8[:], pss[:, :], v_[:], ALU.mult, [N("pss"), N("t2")], [N("t8")])
                        yield
                        stt(et[:, 0, :], t8[:], LNB(c4), g_[:], ALU.add, ALU.mult, [N("t8"), "rwp", N("t6")], [N("et")])
                        yield
                        ts("dve", et[:, 1, :], g_[:], LNW(c4), None, ALU.mult, None, [N("t6"), "rwp"], [N("et")])
                        yield
                        act(t8[:], cl[:], AF.Exp, [N("t4")], [N("t8")])
                        yield
                        tt("dve", rwt[:, 3, :], r_[:], t8[:], ALU.mult, [N("t0"), N("t8")], [N("rwt")])
                        yield
                        tt("dve", t8[:], cl[:], e2[:], ALU.add, [N("t4"), N("t3")], [N("t8")])
                        yield
                        act(t8[:], t8[:], AF.Exp, [N("t8")], [N("t8")])
                        yield
                        stt(rwt[:, 0, :], kk[:], -1.0, t8[:], ALU.mult, ALU.mult, [N("t7"), N("t8")], [N("rwt")])
                        yield
                        tt("dve", a_[:], kk[:], a_[:], ALU.mult, [N("t7"), N("t5")], [N("t5")])
                        yield
                        act(t8[:], cl[:], AF.Exp, [N("t4")], [N("t8")], scale=-1.0)
                        yield
                        tt("dve", rwt[:, 1, :], a_[:], t8[:], ALU.mult, [N("t5"), N("t8")], [N("rwt")])
                        yield
                        tt("dve", rwt[:, 2, :], k_[:], t8[:], ALU.mult, [N("t1"), N("t8")], [N("rwt")])
                        yield
                        for c in range(TT // 64):
                            sl = slice(c * 64, (c + 1) * 64)
                            act(t8[:, sl], cl[:, sl], AF.Exp, [N("t4")], [N("t8")], bias=cl[:, c * 64 + 63:c * 64 + 64], scale=-1.0)
                        tt("dve", rwt[:, 4, :], a_[:], t8[:], ALU.mult, [N("t5"), N("t8")], [N("rwt")])
                        yield
                        tt("dve", rwt[:, 5, :], k_[:], t8[:], ALU.mult, [N("t1"), N("t8")], [N("rwt")])
                        yield
                        cp("pool", rwt[:, 6, :], v_[:], [N("t2")], [N("rwt")])
                        yield
                        act(pct[:], cl[:, 63:TT:64], AF.Exp, [N("t4")], [N("pct")])
                        yield
                        dma(rw_s[b, :, 2 * c4:2 * c4 + 2, :, tsl].rearrange("q h k t -> (h k) q t"), rwt[:], N("rwt"), "rw_s")
                        yield
                        dma(e_s[b, :, 2 * c4:2 * c4 + 2, :, tsl].rearrange("q h k t -> (h k) q t"), et[:], N("et"), "e_s")
                        yield
                        dma(pc_s[b, 2 * c4:2 * c4 + 2, :, tl * 8:(tl + 1) * 8].rearrange("h k c -> (h k) c"), pct[:], N("pct"), "pc_s")
                        yield
                    for pair in ((0, 1), (2, 3)):
                        gens = [chain(pair[0], 0), chain(pair[1], 1)]
                        live = [True, True]
                        while any(live):
                            for gi in range(2):
                                if live[gi]:
                                    try:
                                        next(gens[gi])
                                    except StopIteration:
                                        live[gi] = False
                    flush_jobs()

        SC.barrier()
        pw.close()
        with ExitStack() as ph:
            sb = lambda n, shp, dt=F32: ph.enter_context(nc.sbuf_tensor(n, shp, dt))
            ps = lambda n, shp, dt=F32: ph.enter_context(nc.psum_tensor(n, shp, dt))
            NKB = S // 128
            kT = sb("kT", [68, NH, S], BF16); vS = sb("vS", [128, NKB, NH * 65], BF16); qTl = [sb("qq%d" % i, [68, NH, TT], BF16) for i in range(2)]
            PT = [sb("PT%d" % i, [128, TT], BF16) for i in range(3)]
            msk = sb("msk", [128, 4, TT], BF16); otok = sb("otok", [128, 4, NH * 64], BF16); rec = sb("rec", [128, 4])
            ofT = sb("ofT", [128, 4, TT], BF16); clampS = sb("clampS", [128, TT])
            pS = [ps("pS%d" % i, [128, TT]) for i in range(3)]
            pO = [ps("pO%d" % i, [128, 4, 128]) for i in range(2)]; hcnt = [0]
            pTr = ps("pTr", [128, D], BF16)
            mset("pool", msk[:], 1.0, "msk")
            for jj in range(4):
                SC.op("pool", lambda E, jj=jj: E.affine_select(out=msk[:, jj, :], in_=msk[:, jj, :], pattern=[[1, TT]], compare_op=ALU.is_ge, fill=0.0, base=-jj * 128, channel_multiplier=-1), bl("msk"), bl("msk"))
            for b in range(NB):
                dma(kT[:], k_s[b].rearrange("h k t -> k h t"), "qk_s", "kT")
                dma(vS[:], v_s[b].rearrange("(j p) c -> p j c", p=128), "v_s", "vS")
                def qload(tl_, b=b):
                    dma(qTl[tl_ % 2][:], q_s[b, :, :, tl_ * TT:(tl_ + 1) * TT].rearrange("h k t -> k h t"), "qk_s", "qq%d" % (tl_ % 2))
                qload(0)
                for tl in range(NTL):
                    tsl = slice(tl * TT, (tl + 1) * TT)
                    qT = qTl[tl % 2]; qn = "qq%d" % (tl % 2)
                    if tl + 1 < NTL:
                        qload(tl + 1)
                    for h in range(NH):
                        nj = 4 * tl + 4
                        ob = hcnt[0] % 2; hcnt[0] += 1
                        pOh = pO[ob]; pOn = "pO%d" % ob
                        def QK(j):
                            mm(pS[j % 3][:, :], kT[:, h, j * 128:(j + 1) * 128], qT[:, h, :], True, True, ["kT", qn], ["pS%d" % (j % 3)])
                        QK(0)
                        if nj > 1:
                            QK(1)
                        for j in range(nj):
                            p_ = pS[j % 3]; pn = "pS%d" % (j % 3); P_ = PT[j % 3]; Pn = "PT%d" % (j % 3)
                            jj = j - 4 * tl
                            if jj >= 0:
                                c0 = jj * 128
                                ts("dve", clampS[:, c0:], p_[:, c0:], 60.0, None, ALU.min, None, [pn], ["clampS"])
                                act(P_[:, c0:], clampS[:, c0:], AF.Exp, ["clampS"], [Pn])
                                tt("dve", P_[:, c0:], P_[:, c0:], msk[:, jj, c0:], ALU.mult, [Pn, "msk"], [Pn])
                            else:
                                act(P_[:], p_[:, :], AF.Exp, [pn], [Pn])
                            if j + 2 < nj:
                                QK(j + 2)
                            for qb in range(4):
                                if j <= 4 * tl + qb:
                                    mm(pOh[:, qb, 0:65], P_[:, qb * 128:(qb + 1) * 128], vS[:, j, h * 65:(h + 1) * 65], j == 0 and qb == 0, j == 4 * tl + qb, [Pn, "vS"], [pOn], skip=True)
                        for qb in range(4):
                            SC.op("dve", lambda E, qb=qb, pOh=pOh: E.reciprocal(out=rec[:, qb:qb + 1], in_=pOh[:, qb, 64:65]), bl([pOn]), bl("rec"))
                            ts("dve", otok[:, qb, h * 64:(h + 1) * 64], pOh[:, qb, 0:64], rec[:, qb:qb + 1], None, ALU.mult, None, [pOn, "rec"], ["otok"])
                    for qb in range(4):
                        for c in range(4):
                            SC.op("pe", lambda E, qb=qb, c=c: E.transpose(out=pTr[:, c * 128:(c + 1) * 128], in_=otok[:, qb, c * 128:(c + 1) * 128], identity=ident[:]), bl("otok ident"), bl("pTr"))
                        cp("act", ofT[:, :, qb * 128:(qb + 1) * 128], pTr[:, 0:512].rearrange("p (c t) -> p c t", c=4), ["pTr"], ["ofT"])
                    dma(ofox_s[b, :, :, tsl].rearrange("c p t -> p c t"), ofT[:], "ofT", "ofox_s")

        SC.barrier()
        with ExitStack() as ph:
            sb = lambda n, shp, dt=F32: ph.enter_context(nc.sbuf_tensor(n, shp, dt))
            ps = lambda n, shp, dt=F32: ph.enter_context(nc.psum_tensor(n, shp, dt))
            PD = P3DT
            D2 = P3D2
            UC = 4 if PD == BF16 else 2
            UT = UC * 64
            NU = S // UT
            RW = [sb("RW%d" % i, [64, 7, NH, UT], PD) for i in range(2)]
            ET = [sb("ET%d" % i, [128, 2, 4, UT], BF16) for i in range(2)]
            PC = [sb("PC%d" % i, [64, NH, UC]) for i in range(2)]
            Hf = sb("Hf", [64, NH, 64]); Hhi = sb("Hhi", [64, NH, 64], PD); Hlo = sb("Hlo", [64, NH, 64], PD)
            mLT = sb("mLT", [64, NH, 64]); mL = sb("mL", [64, NH, 64]); mGT = sb("mGT", [64, NH, 64]); idF = sb("idF", [64, NH, 64]); idP = sb("idP", [64, 64], PD)
            tok = [[sb("tok%d_%d" % (p, c), [64, 4, NH, 64], PD) for c in range(UC)] for p in range(2)]
            TTt = [[sb("TT%d_%d" % (p, c), [64, NH, 64], D2) for c in range(UC)] for p in range(2)]
            LakT = [[sb("LakT%d_%d" % (p, c), [64, NH, 64], PD) for c in range(UC)] for p in range(2)]
            GrbT = [[sb("GrbT%d_%d" % (p, c), [64, NH, 64], PD) for c in range(UC)] for p in range(2)]
            GrkT = [[sb("GrkT%d_%d" % (p, c), [64, NH, 64], PD) for c in range(UC)] for p in range(2)]
            A_ = [[sb("A%d_%d" % (i, c), [64, NH, 64], BF16) for c in range(UC)] for i in range(2)]
            Tb = [sb("Tb_%d" % c, [64, NH, 64], BF16) for c in range(UC)]; TTb = [sb("TTb_%d" % c, [64, NH, 64], BF16) for c in range(UC)]
            L0b = [sb("L0b_%d" % c, [64, NH, 64], BF16) for c in range(UC)]; R2T = [sb("R2T_%d" % c, [64, NH, 64], BF16) for c in range(UC)]
            AT_ = [[sb("AT%d_%d" % (i, c), [64, NH, 64], BF16) for c in range(UC)] for i in range(2)]
            R2f = sb("R2f", [64, NH, 64]); RHS = sb("RHS", [64, NH, 64], D2); U = sb("U", [64, NH, 64], PD)
            ytl = [sb("yt%d" % i, [64, NH, 64]) for i in range(2)]; ysq = sb("ysq", [64, NH, 64]); yn = sb("yn", [64, NH, 64], BF16)
            st8 = sb("st8", [64, NH]); rs8 = sb("rs8", [64, NH]); orT = [sb("orT%d" % i, [128, 4, UT], BF16) for i in range(2)]; o1 = sb("o1", [128, 4, 64])
            pp = [ps("pp%d" % i, [64, NH, 64]) for i in range(4)]
            pq = [ps("pq%d" % i, [64, NH, 64]) for i in range(2)]
            ptrA = ps("ptrA", [64, 2, NH * 64], BF16); ptr = ps("ptr", [128, 256], BF16)
            for t_, op_, cm, st in ((mLT, ALU.is_gt, -1, 1), (mL, ALU.is_gt, 1, -1), (mGT, ALU.is_ge, -1, 1), (idF, ALU.is_equal, 1, -1)):
                SC.op("pool", lambda E, t_=t_: E.memset(t_[:], 1.0), [], bl("masks"))
                SC.op("pool", lambda E, t_=t_, op_=op_, cm=cm, st=st: E.affine_select(out=t_[:], in_=t_[:], pattern=[[0, NH], [st, 64]], compare_op=op_, fill=0.0, base=0, channel_multiplier=cm), bl("masks"), bl("masks"))
            cp("dve", idP[:], idF[:, 0, :], ["masks"], ["idP"])
            ppi = [0]; pqi = [0]; pti = [0]
            def grp(pool, idx, lhs_fn, rhs_fn, R, start=True, stop=True, same=None):
                if same is None:
                    i = idx[0] % len(pool); idx[0] += 1
                else:
                    i = same
                nm = ("pp%d" if pool is pp else "pq%d") % i
                multi = not (start and stop)
                for h in range(NH):
                    mm(pool[i][:, h, :], lhs_fn(h), rhs_fn(h), start and (h == 0 or not multi), stop, R, [nm], skip=multi)
                return pool[i], nm, i

            def A_groups(b, u):
                p = u % 2
                rw = RW[p]; rwn = "RW%d" % p
                usl = slice(u * UT, (u + 1) * UT)
                G = []
                def load():
                    q_ = "pool" if PD == F32 else "sp"
                    dma(rw[:], rw_s[b, :, :, :, usl].rearrange("q h k t -> k q h t"), "rw_s", rwn, queue=q_)
                    dma(ET[p][:], e_s[b, :, :, :, usl].rearrange("q (c hh) k t -> (hh k) q c t", hh=2), "e_s", "ET%d" % p)
                    dma(PC[p][:], pc_s[b, :, :, u * UC:(u + 1) * UC].rearrange("h k c -> k h c"), "pc_s", "PC%d" % p)
                G.append(load)
                X = lambda q, c: (lambda h: rw[:, q, h, c * 64:(c + 1) * 64])
                def tr_stage(i, q):
                    def f(c):
                        tkn = "tok%d_%d" % (p, c)
                        if PD == BF16:
                            k = pti[0] % 2; pti[0] += 1
                            for h in range(NH):
                                SC.op("pe", lambda E, h=h, k=k, c=c: E.transpose(out=ptrA[:, k, h * 64:(h + 1) * 64], in_=rw[:, q, h, c * 64:(c + 1) * 64], identity=idP[:]), bl([rwn, "idP"]), bl(["ptrA"]))
                            cp("act", tok[p][c][:, i, :, :], ptrA[:, k, :].rearrange("p (h d) -> p h d", h=NH), ["ptrA"], [tkn])
                        else:
                            j = ppi[0] % 4; ppi[0] += 1
                            for h in range(NH):
                                SC.op("pe", lambda E, h=h, j=j, c=c: E.transpose(out=pp[j][:, h, :], in_=rw[:, q, h, c * 64:(c + 1) * 64], identity=idP[:]), bl([rwn, "idP"]), bl(["pp%d" % j]))
                            cp("act" if i % 2 else "dve", tok[p][c][:, i, :, :], pp[j][:], ["pp%d" % j], [tkn])
                    return f
                def prod_stage(ql, qr, dst_fn, dn_fn, mask, eng):
                    def f(c):
                        t_, nm, _ = grp(pp, ppi, X(ql, c), X(qr, c), [rwn])
                        tt(eng, dst_fn(c)[:], t_[:], mask[:], ALU.mult, [nm, "masks"], [dn_fn(c)])
                    return f
                stages = [tr_stage(i, q) for i, q in enumerate((0, 4, 5, 6))]
                stages.append(prod_stage(1, 0, lambda c: AT_[0][c], lambda c: "AT0_%d" % c, mLT, "dve"))
                def a0_stage(c):
                    t_, nm, _ = grp(pp, ppi, X(0, c), X(1, c), [rwn])
                    tt("dve", A_[0][c][:], t_[:], mL[:], ALU.mult, [nm, "masks"], ["A0_%d" % c])
                    cp("act", L0b[c][:], A_[0][c][:], ["A0_%d" % c], ["L0b_%d" % c])
                stages.append(a0_stage)
                stages.append(prod_stage(2, 0, lambda c: LakT[p][c], lambda c: "LakT%d_%d" % (p, c), mLT, "dve"))
                stages.append(prod_stage(1, 3, lambda c: GrbT[p][c], lambda c: "GrbT%d_%d" % (p, c), mGT, "dve"))
                stages.append(prod_stage(2, 3, lambda c: GrkT[p][c], lambda c: "GrkT%d_%d" % (p, c), mGT, "dve"))
                def st_init(c):
                    tt("dve", TTb[c][:], AT_[0][c][:], idF[:], ALU.add, ["AT0_%d" % c, "masks"], ["TTb_%d" % c])
                stages.append(st_init)
                for r in range(6):
                    cur = r % 2; nxt = 1 - cur
                    def sq1(c, cur=cur, nxt=nxt):
                        t_, nm, _ = grp(pp, ppi, lambda h: A_[cur][c][:, h, :], lambda h: AT_[cur][c][:, h, :], ["A%d_%d" % (cur, c), "AT%d_%d" % (cur, c)])
                        cp("act", AT_[nxt][c][:], t_[:], [nm], ["AT%d_%d" % (nxt, c)])
                    def sq2(c, cur=cur, nxt=nxt):
                        t_, nm, _ = grp(pp, ppi, lambda h: AT_[cur][c][:, h, :], lambda h: A_[cur][c][:, h, :], ["A%d_%d" % (cur, c), "AT%d_%d" % (cur, c)])
                        cp("act", A_[nxt][c][:], t_[:], [nm], ["A%d_%d" % (nxt, c)])
                    def supd(c, cur=cur):
                        t_, nm, _ = grp(pp, ppi, lambda h: A_[cur][c][:, h, :], lambda h: TTb[c][:, h, :], ["A%d_%d" % (cur, c), "TTb_%d" % c])
                        tt("dve", TTb[c][:], TTb[c][:], t_[:], ALU.add, ["TTb_%d" % c, nm], ["TTb_%d" % c])
                    def supd2(c, cur=cur):
                        t_, nm, _ = grp(pp, ppi, lambda h: AT_[cur][c][:, h, :], lambda h: Tb[c][:, h, :], ["AT%d_%d" % (cur, c), "Tb_%d" % c])
                        tt("dve", Tb[c][:], Tb[c][:], t_[:], ALU.add, ["Tb_%d" % c, nm], ["Tb_%d" % c])
                    if r >= 1:
                        stages.append(supd)
                    if r <= 4:
                        stages.append(sq1); stages.append(sq2)
                def nwT(c):
                    k = pti[0] % 2; pti[0] += 1
                    for h in range(NH):
                        SC.op("pe", lambda E, h=h, k=k, c=c: E.transpose(out=ptrA[:, k, h * 64:(h + 1) * 64], in_=TTb[c][:, h, :], identity=idP[:]), bl(["TTb_%d" % c, "idP"]), bl(["ptrA"]))
                    cp("act", Tb[c][:], ptrA[:, k, :].rearrange("p (h d) -> p h d", h=NH), ["ptrA"], ["Tb_%d" % c])
                def nw1(c):
                    t_, nm, _ = grp(pp, ppi, lambda h: L0b[c][:, h, :], lambda h: TTb[c][:, h, :], ["L0b_%d" % c, "TTb_%d" % c])
                    stt(R2f[:], TTb[c][:], -1.0, t_[:], ALU.mult, ALU.add, ["TTb_%d" % c, nm], ["R2f"])
                    tt("dve", R2T[c][:], R2f[:], idF[:], ALU.add, ["R2f", "masks"], ["R2T_%d" % c])
                def nw2(c):
                    t_, nm, _ = grp(pp, ppi, lambda h: Tb[c][:, h, :], lambda h: R2T[c][:, h, :], ["Tb_%d" % c, "R2T_%d" % c])
                    tt("dve", TTt[p][c][:], TTb[c][:], t_[:], ALU.add, ["TTb_%d" % c, nm], ["TT%d_%d" % (p, c)])
                stages += [nwT, nw1, nw2]
                for stg in stages:
                    for c in range(UC):
                        G.append(lambda stg=stg, c=c: stg(c))
                return G

            def B_steps(b, u):
                p = u % 2
                rw = RW[p]; rwn = "RW%d" % p
                usl = slice(u * UT, (u + 1) * UT)
                Bs = []
                pend = []
                for c in range(UC):
                    sl = slice(c * 64, (c + 1) * 64)
                    X = lambda q, sl=sl: (lambda h: rw[:, q, h, sl])
                    tk = tok[p][c]; tkn = "tok%d_%d" % (p, c)
                    TK = lambda i, tk=tk: (lambda h: tk[:, i, h, :])
                    Hn = ["Hhi", "Hlo"] if PD == BF16 else ["Hf"]
                    def s1(c=c, X=X, TK=TK, tkn=tkn):
                        if PD == BF16:
                            t_, nm, i = grp(pq, pqi, X(0), lambda h: Hhi[:, h, :], [rwn, "Hhi"], True, False)
                            grp(pq, pqi, X(0), lambda h: Hlo[:, h, :], [rwn, "Hlo"], False, False, same=i)
                        else:
                            t_, nm, i = grp(pq, pqi, X(0), lambda h: Hf[:, h, :], [rwn, "Hf"], True, False)
                        grp(pq, pqi, lambda h: LakT[p][c][:, h, :], TK(3), ["LakT%d_%d" % (p, c), tkn], False, True, same=i)
                        cp("act", RHS[:], t_[:], [nm], ["RHS"])
                    def s2(c=c):
                        t_, nm, i = grp(pq, pqi, lambda h: TTt[p][c][:, h, :], lambda h: RHS[:, h, :], ["TT%d_%d" % (p, c), "RHS"])
                        cp("act", U[:], t_[:], [nm], ["U"])
                    def s3(c=c, X=X, TK=TK, tkn=tkn):
                        t_, nm, i = grp(pq, pqi, TK(1), lambda h: U[:, h, :], [tkn, "U"], True, False)
                        grp(pq, pqi, TK(2), TK(3), [tkn], False, True, same=i)
                        for h in range(NH):
                            stt(Hf[:, h, :], Hf[:, h, :], PC[p][:, h, c:c + 1], t_[:, h, :], ALU.mult, ALU.add, ["Hf", "PC%d" % p, nm], ["Hf"])
                        if PD == BF16:
                            cp("dve", Hhi[:], Hf[:], ["Hf"], ["Hhi"])
                            tt("dve", Hlo[:], Hf[:], Hhi[:], ALU.subtract, ["Hf", "Hhi"], ["Hlo"])
                    def s3y(c=c, X=X, TK=TK, tkn=tkn):
                        if PD == BF16:
                            t_, nm, i = grp(pq, pqi, X(3), lambda h: Hhi[:, h, :], [rwn, "Hhi"], True, False)
                            grp(pq, pqi, X(3), lambda h: Hlo[:, h, :], [rwn, "Hlo"], False, False, same=i)
                        else:
                            t_, nm, i = grp(pq, pqi, X(3), lambda h: Hf[:, h, :], [rwn, "Hf"], True, False)
                        grp(pq, pqi, lambda h: GrbT[p][c][:, h, :], lambda h: U[:, h, :], ["GrbT%d_%d" % (p, c), "U"], False, False, same=i)
                        grp(pq, pqi, lambda h: GrkT[p][c][:, h, :], TK(3), ["GrkT%d_%d" % (p, c), tkn], False, True, same=i)
                        cp("act", ytl[c % 2][:], t_[:], [nm], ["yt%d" % (c % 2)])
                    def s4(c=c, sl=sl):
                        yt_ = ytl[c % 2]; ytn = "yt%d" % (c % 2)
                        bc = lambda t: t[:, :].unsqueeze(2).broadcast_to([64, NH, 64])
                        SC.op("dve", lambda E: E.tensor_reduce(out=st8[:], in_=yt_[:], axis=AX.X, op=ALU.add), bl([ytn]), bl("st8"))
                        ts("dve", st8[:], st8[:], 1.0 / 64, None, ALU.mult, None, ["st8"], ["st8"])
                        tt("dve", yt_[:], yt_[:], bc(st8), ALU.subtract, [ytn, "st8"], [ytn])
                        act(ysq[:], yt_[:], AF.Square, [ytn], ["ysq"])
                        SC.op("dve", lambda E: E.tensor_reduce(out=rs8[:], in_=ysq[:], axis=AX.X, op=ALU.add), bl("ysq"), bl("rs8"))
                        act(rs8[:], rs8[:], AF.Sqrt, ["rs8", "cst"], ["rs8"], bias=GNEPS[0:64, :], scale=1.0 / 64)
                        SC.op("dve", lambda E: E.reciprocal(out=rs8[:], in_=rs8[:]), bl("rs8"), bl("rs8"))
                        tt("dve", yn[:], yt_[:], bc(rs8), ALU.mult, [ytn, "rs8"], ["yn"])
                        for c4 in range(4):
                            SC.op("pe", lambda E, c4=c4: E.transpose(out=ptr[:, c4 * 64:(c4 + 1) * 64], in_=yn[:, 2 * c4:2 * c4 + 2, :].rearrange("p h d -> p (h d)"), identity=ident[0:64, 0:64]), bl("yn ident"), bl("ptr"))
                        tt("dve", o1[:], ptr[:, :].rearrange("p (c t) -> p c t", c=4), ET[p][:, 1, :, sl], ALU.mult, ["ptr", "ET%d" % p], ["o1"])
                        tt("dve", orT[p][:, :, sl], o1[:], ET[p][:, 0, :, sl], ALU.add, ["o1", "ET%d" % p], ["orT%d" % p])
                    if pend:
                        Bs += [s1, s2, pend.pop(), s3y, s3]
                    else:
                        Bs += [s1, s2, s3y, s3]
                    pend.append(s4)
                Bs.append(pend.pop())
                def store():
                    dma(orw_s[b, :, :, usl].rearrange("c p t -> p c t"), orT[p][:], "orT%d" % p, "orw_s")
                Bs.append(store)
                return Bs

            for b in range(NB):
                mset("pool", Hf[:], 0.0, "Hf")
                if PD == BF16:
                    mset("pool", Hhi[:], 0.0, "Hhi"); mset("pool", Hlo[:], 0.0, "Hlo")
                for g in A_groups(b, 0):
                    g()
                for u in range(NU):
                    Bl = B_steps(b, u)
                    Al = A_groups(b, u + 1) if u + 1 < NU else []
                    na, nb_ = len(Al), len(Bl)
                    ia = 0
                    for ib, st in enumerate(Bl):
                        st()
                        tgt = (ib + 1) * na // nb_
                        while ia < tgt:
                            Al[ia](); ia += 1
                    while ia < na:
                        Al[ia](); ia += 1
        SC.barrier()
        with ExitStack() as ph:
            sb = lambda n, shp, dt=F32: ph.enter_context(nc.sbuf_tensor(n, shp, dt))
            ps = lambda n, shp, dt=F32: ph.enter_context(nc.psum_tensor(n, shp, dt))
            foxo = sb("foxo", [128, 4, D], BF16); rwo = sb("rwo", [128, 4, D], BF16); wo = sb("wo", [128, 8, D], BF16)
            vw = lambda d: d.rearrange("(kc p) n -> p kc n", p=128)
            dma(foxo[:], vw(foxo_d), "", "foxo", queue="pool"); dma(rwo[:], vw(rwo_d), "", "rwo", queue="pool"); dma(wo[:], vw(wo_d), "", "wo", queue="pool")
            ofTl = [sb("ofT4_%d" % i, [128, 4, TT], BF16) for i in range(2)]; orTl = [sb("orT4_%d" % i, [128, 4, TT], BF16) for i in range(2)]; gtsl = [sb("gts_%d" % i, [128, 16, TT], BF16) for i in range(2)]
            xtl = [sb("xt4_%d" % i, [128, 4, D]) for i in range(2)]; g1b = sb("g1b", [128, D]); mgl = [sb("mg%d" % i, [128, 8, TT], BF16) for i in range(2)]
            m1l = [sb("m1_%d" % i, [128, TT]) for i in range(2)]; m2l = [sb("m2_%d" % i, [128, TT]) for i in range(2)]
            pF = [ps("pF%d" % i, [128, TT]) for i in range(2)]; pR = [ps("pR%d" % i, [128, TT]) for i in range(2)]; pM = [ps("pM%d" % i, [128, TT]) for i in range(2)]
            def loads(gt):
                i = gt % 2
                b_, tl_ = gt // NTL, gt % NTL
                t0_ = b_ * S + tl_ * TT
                tsl_ = slice(tl_ * TT, (tl_ + 1) * TT)
                dma(ofTl[i][:], ofox_s[b_, :, :, tsl_].rearrange("c p t -> p c t"), "ofox_s", "ofT4_%d" % i)
                dma(orTl[i][:], orw_s[b_, :, :, tsl_].rearrange("c p t -> p c t"), "orw_s", "orT4_%d" % i)
                dma(gtsl[i][:], gate_s[b_, :, :, tsl_].rearrange("g p t -> p g t"), "gate_s", "gts_%d" % i)
                dma(xtl[i][:], x_d[t0_:t0_ + TT, :].rearrange("(k p) d -> p k d", p=128), "", "xt4_%d" % i)
            loads(0)
            for b in range(NB):
                dma(g1b[:], gates_s[b, 0, :].partition_broadcast(128), "gates_s", "g1b")
                for tl in range(NTL):
                    t0 = b * S + tl * TT
                    gt = b * NTL + tl
                    i4 = gt % 2
                    ofT = ofTl[i4]; orT = orTl[i4]; gts = gtsl[i4]; xt = xtl[i4]
                    ofn, orn, gtn, xtn = "ofT4_%d" % i4, "orT4_%d" % i4, "gts_%d" % i4, "xt4_%d" % i4
                    mg = mgl[i4]; mgn = "mg%d" % i4
                    if gt + 1 < NB * NTL:
                        loads(gt + 1)
                    for oc in range(8):
                        f_ = pF[oc % 2]; fn = "pF%d" % (oc % 2); r_ = pR[oc % 2]; rn = "pR%d" % (oc % 2)
                        for kc in range(4):
                            mm(f_[:, :], foxo[:, kc, oc * 128:(oc + 1) * 128], ofT[:, kc, :], kc == 0, kc == 3, ["foxo", ofn], [fn])
                        for kc in range(4):
                            mm(r_[:, :], rwo[:, kc, oc * 128:(oc + 1) * 128], orT[:, kc, :], kc == 0, kc == 3, ["rwo", orn], [rn])
                        m1 = m1l[oc % 2]; m2 = m2l[oc % 2]; m1n = "m1_%d" % (oc % 2); m2n = "m2_%d" % (oc % 2)
                        tt("dve", m1[:], f_[:, :], gts[:, oc, :], ALU.mult, [fn, gtn], [m1n])
                        tt("dve", m2[:], r_[:, :], gts[:, 8 + oc, :], ALU.mult, [rn, gtn], [m2n])
                        tt("pool", mg[:, oc, :], m1[:], m2[:], ALU.add, [m1n, m2n], [mgn])
                    for blk in range(4):
                        for hf in range(2):
                            i = (blk * 2 + hf) % 2
                            for kc in range(8):
                                mm(pM[i][:, :], mg[:, kc, blk * 128:(blk + 1) * 128], wo[:, kc, hf * 512:(hf + 1) * 512], kc == 0, kc == 7, [mgn, "wo"], ["pM%d" % i])
                            m1 = m1l[i]; m1n = "m1_%d" % i
                            tt("dve", m1[:], pM[i][:, :], g1b[:, hf * 512:(hf + 1) * 512], ALU.mult, ["pM%d" % i, "g1b"], [m1n])
                            tt("pool", xt[:, blk, hf * 512:(hf + 1) * 512], xt[:, blk, hf * 512:(hf + 1) * 512], m1[:], ALU.add, [xtn, m1n], [xtn])
                    dma(h_s[t0:t0 + TT, :].rearrange("(k p) d -> p k d", p=128), xt[:], xtn, "h_s")

        SC.barrier()
        with ExitStack() as ph:
            sb = lambda n, shp, dt=F32: ph.enter_context(nc.sbuf_tensor(n, shp, dt))
            ps = lambda n, shp, dt=F32: ph.enter_context(nc.psum_tensor(n, shp, dt))
            T2 = 256
            wup = sb("wup", [128, 8, 2 * DFF], BF16); wdn = sb("wdn", [128, 22, D], BF16)
            vw = lambda d: d.rearrange("(kc p) n -> p kc n", p=128)
            dma(wup[:], vw(wup_d), "", "wup", queue="pool"); dma(wdn[:], vw(wdn_d), "", "wdn", queue="pool")
            cw = sb("cw", [128, 22, 3]); cb = sb("cb", [128, 22]); g2b_ = sb("g2bc", [128, D]); fgb = sb("fgb", [128, D])
            dma(cw[:], convw_d, "", "cw"); dma(cb[:], convb_d, "", "cb"); dma(fgb[:], fing_d.partition_broadcast(128), "", "fgb")
            htl = [sb("ht%d" % i, [128, 2, D]) for i in range(2)]; hx = sb("hx", [128, 2, D], BF16); u2l = [sb("u2T%d" % i, [128, 8, T2], BF16) for i in range(2)]; ssq3 = sb("ssq3", [128, 2]); rstd3 = sb("rstd3", [128, 2])
            ssq = sb("ssq2", [128, 2]); rstd = sb("rstd2", [128, 2]); acar = sb("acar", [128, 22, 2])
            araw_ = [sb("araw%d" % i, [128, T2 + 2]) for i in range(2)]; c1_ = [sb("c1%d" % i, [128, T2]) for i in range(2)]; gT = sb("gT", [128, 22, T2], BF16)
            m1bl = [sb("m1b_%d" % i, [128, 512]) for i in range(2)]; junk = sb("junk", [128, D], BF16); junk2 = sb("junk2", [128, D], BF16)
            pT = [ps("qT%d" % i, [128, D], BF16) for i in range(2)]
            pa = [ps("pa%d" % i, [128, T2]) for i in range(2)]; pb = [ps("pb%d" % i, [128, T2]) for i in range(2)]
            pD = [ps("pD%d" % i, [128, 512]) for i in range(2)]
            def hload(b, tl):
                i = tl % 2
                t0 = b * S + tl * T2
                dma(htl[i][:], h_s[t0:t0 + T2, :].rearrange("(k p) d -> p k d", p=128), "h_s", "ht%d" % i)
            def prologue(b, tl):
                i = tl % 2
                norm_T(htl[i], "ht%d" % i, 2, G2, modc[:, 3], b, u2l[i], "u2T%d" % i, junk[:], "junk", ssq, rstd, hx, pT, "b_")
            def main(b, tl):
                i = tl % 2
                u2T = u2l[i]; un = "u2T%d" % i
                for fc in range(22):
                    a_ = pa[fc % 2]; an = "pa%d" % (fc % 2); b_ = pb[fc % 2]; bn = "pb%d" % (fc % 2)
                    araw = araw_[fc % 2]; arn = "araw%d" % (fc % 2); c1 = c1_[fc % 2]; c1n = "c1%d" % (fc % 2)
                    for kc in range(8):
                        mm(a_[:, :], wup[:, kc, fc * 128:(fc + 1) * 128], u2T[:, kc, :], kc == 0, kc == 7, ["wup", un], [an])
                    for kc in range(8):
                        mm(b_[:, :], wup[:, kc, DFF + fc * 128:DFF + (fc + 1) * 128], u2T[:, kc, :], kc == 0, kc == 7, ["wup", un], [bn])
                    cp("act", araw[:, 2:T2 + 2], a_[:, :], [an], [arn])
                    cp("pool", araw[:, 0:2], acar[:, fc, :], ["acar"], [arn])
                    ts("dve", c1[:], araw[:, 2:T2 + 2], cw[:, fc, 2:3], cb[:, fc:fc + 1], ALU.mult, ALU.add, [arn, "cw", "cb"], [c1n])
                    stt(c1[:], araw[:, 1:T2 + 1], cw[:, fc, 1:2], c1[:], ALU.mult, ALU.add, [arn, "cw", c1n], [c1n])
                    stt(c1[:], araw[:, 0:T2], cw[:, fc, 0:1], c1[:], ALU.mult, ALU.add, [arn, "cw", c1n], [c1n])
                    cp("pool", acar[:, fc, :], araw[:, T2:T2 + 2], [arn], ["acar"])
                    act(c1[:], c1[:], AF.Silu, [c1n], [c1n])
                    tt("dve", gT[:, fc, :], c1[:], b_[:, :], ALU.mult, [c1n, bn], ["gT"])
            def epilogue(b, tl):
                i = tl % 2
                ht = htl[i]; hn = "ht%d" % i
                t0 = b * S + tl * T2
                for blk in range(2):
                    for hf in range(2):
                        for fc in range(22):
                            mm(pD[hf][:, :], gT[:, fc, blk * 128:(blk + 1) * 128], wdn[:, fc, hf * 512:(hf + 1) * 512], fc == 0, fc == 21, ["gT", "wdn"], ["pD%d" % hf])
                        m1 = m1bl[hf]; m1n = "m1b_%d" % hf
                        tt("dve", m1[:], pD[hf][:, :], g2b_[:, hf * 512:(hf + 1) * 512], ALU.mult, ["pD%d" % hf, "g2bc"], [m1n])
                        tt("pool", ht[:, blk, hf * 512:(hf + 1) * 512], ht[:, blk, hf * 512:(hf + 1) * 512], m1[:], ALU.add, [hn, m1n], [hn])
                for blk in range(2):
                    act(junk2[:], ht[:, blk, :], AF.Square, [hn], ["junk2", "ssq3"], accum=ssq3[:, blk:blk + 1])
                act(rstd3[:], ssq3[:], AF.Sqrt, ["ssq3", "cst"], ["rstd3"], bias=EPS, scale=1.0 / D)
                SC.op("dve", lambda E: E.reciprocal(out=rstd3[:], in_=rstd3[:]), bl("rstd3"), bl("rstd3"))
                for blk in range(2):
                    stt(ht[:, blk, :], ht[:, blk, :], rstd3[:, blk:blk + 1], fgb[:], ALU.mult, ALU.mult, [hn, "rstd3", "fgb"], [hn])
                dma(out_d[t0:t0 + T2, :].rearrange("(k p) d -> p k d", p=128), ht[:], hn, "out")
            NT2 = S // T2
            for b in range(NB):
                dma(g2b_[:], gates_s[b, 1, :].partition_broadcast(128), "gates_s", "g2bc")
                mset("pool", acar[:], 0.0, "acar")
                hload(b, 0)
                prologue(b, 0)
                for tl in range(NT2):
                    if tl + 1 < NT2:
                        hload(b, tl + 1)
                    main(b, tl)
                    if tl + 1 < NT2:
                        prologue(b, tl + 1)
                    epilogue(b, tl)
        SC.final_wait(bl("out"))
        if debug:
            SC.final_wait(list(B.values()))
        SC.emit()
    return nc


def host_inputs(inp, b0, NB):
    f = lambda a: np.ascontiguousarray(np.asarray(a, dtype=np.float32))
    S = inp["x"].shape[1]
    w_in = np.asarray(inp["w_in"])[0]
    rw = w_in[:, 1544:3336]
    z64 = np.zeros((D, 64), np.float32)
    w_rw = np.concatenate([rw[:, 0:1600], z64, rw[:, 1600:1664], z64, rw[:, 1664:1792]], axis=1)
    mu = np.asarray(inp["rwkv_mu"])[0]
    zz = np.zeros((64,), np.float32)
    mu_p = np.concatenate([mu[0:1600], zz, mu[1600:1664], zz, mu[1664:1792]])
    col = lambda v, n: f(np.asarray(v).reshape(n, 128).T)
    bm = np.asarray(inp["b_mod"])[0]
    par = [inp["rwkv_w0"], inp["rwkv_a0"], inp["rwkv_k_k"], inp["rwkv_k_a"], inp["rwkv_r_k"], inp["rwkv_ln_w"], inp["rwkv_ln_b"]]
    rw_par = np.stack([np.asarray(p)[0].reshape(4, 128).T for p in par], axis=1)
    return {
        "x": f(np.asarray(inp["x"])[b0:b0 + NB].reshape(NB * S, D)),
        "cT": f(np.asarray(inp["c"])[b0:b0 + NB].reshape(NB, 8, 128).transpose(2, 1, 0)),
        "w_mod": f(np.asarray(inp["w_mod"])[0]),
        "bmod_col": col(bm, 48),
        "bmod_g": f(np.broadcast_to(np.stack([bm[2048:3072], bm[5120:6144]])[None], (NB, 2, D))),
        "ln1_col": col(np.asarray(inp["ln1_g"])[0], 8), "ln2_col": col(np.asarray(inp["ln2_g"])[0], 8),
        "fin_g": f(inp["final_g"]),
        "w_qkv": f(w_in[:, 0:1536]), "w_f": f(w_in[:, 1536:1544]), "w_rw": f(w_rw), "w_gate": f(w_in[:, 3336:5384]),
        "fox_bf": f(np.asarray(inp["fox_b_f"])[0].reshape(8, 1)), "mu_col": col(mu_p, NRW),
        "rw_par": f(rw_par),
        "w2": f(np.asarray(inp["rwkv_w2"])[0]), "a2": f(np.asarray(inp["rwkv_a2"])[0]), "g2": f(np.asarray(inp["rwkv_g2"])[0]),
        "fox_w_out": f(np.asarray(inp["fox_w_out"])[0]), "rwkv_w_out": f(np.asarray(inp["rwkv_w_out"])[0]), "w_o": f(np.asarray(inp["w_o"])[0]),
        "w_up": f(np.asarray(inp["w_up"])[0]), "w_down": f(np.asarray(inp["w_down"])[0]),
        "conv_col": f(np.asarray(inp["conv_w"])[0].reshape(3, 22, 128).transpose(2, 1, 0)),
        "convb_col": col(np.asarray(inp["conv_b"])[0], 22),
    }


_NC_CACHE = {}


def kernel(**inputs):
    x = np.asarray(inputs["x"])
    Bt, S, _ = x.shape
    ncores = 8
    NB = Bt // ncores
    key = (NB, S)
    if key not in _NC_CACHE:
        _NC_CACHE[key] = build(NB, S)
    nc = _NC_CACHE[key]
    in_maps = [host_inputs(inputs, i * NB, NB) for i in range(ncores)]
    res = run_bass_kernel_spmd(nc, in_maps, core_ids=list(range(ncores)))
    out = np.concatenate([np.asarray(r["out"]).reshape(NB, S, D) for r in res.results], axis=0)
    return out.astype(np.float32)
```

```python
import numpy as np
import concourse.bass as bass
import concourse.mybir as mybir
from concourse.bass_utils import run_bass_kernel_spmd
from contextlib import ExitStack

F32 = mybir.dt.float32
BF16 = mybir.dt.bfloat16
AF = mybir.ActivationFunctionType
ALU = mybir.AluOpType
AX = mybir.AxisListType


class Buf:
    __slots__ = ("name", "w", "r")

    def __init__(self, name):
        self.name = name
        self.w = None
        self.r = []


class Sched:
    ENGS = ("pe", "act", "dve", "pool", "sp")

    def __init__(self, nc, stack, n_dma_sems=32):
        self.nc = nc
        self.q = {e: [] for e in self.ENGS}
        self.sems = {}
        self.val = {}
        for e in ("pe", "act", "dve", "pool"):
            self.sems[e] = stack.enter_context(nc.semaphore("s_" + e))
            self.val[e] = 0
        self.free_dma = []
        for i in range(n_dma_sems):
            k = "d%d" % i
            self.sems[k] = stack.enter_context(nc.semaphore("s_" + k))
            self.val[k] = 0
            self.free_dma.append(k)
        self.dma_rr = 0
        self.sw_dma = []
        for i in range(8):
            k = "e%d" % i
            self.sems[k] = stack.enter_context(nc.semaphore("s_" + k))
            self.val[k] = 0
            self.sw_dma.append(k)
        self.sw_rr = 0
        self.waited = {e: {} for e in self.ENGS}
        self.n_wait = 0
        self.n_op = 0

    def _need(self, eng, reads, writes):
        need = {}

        def add(tok):
            if tok is None:
                return
            k, v, e = tok
            if e == eng and eng == "pe":
                return
            if need.get(k, 0) < v:
                need[k] = v
        for b in reads:
            add(b.w)
        for b in writes:
            add(b.w)
            for t in b.r:
                add(t)
        out = []
        for k, v in need.items():
            if self.waited[eng].get(k, 0) < v:
                self.waited[eng][k] = v
                out.append((k, v))
        return out

    def _emit_waits(self, eng, waits):
        for k, v in waits:
            sem = self.sems[k]
            self.q[eng].append(lambda E, sem=sem, v=v: E.wait_ge(sem, v))
            self.n_wait += 1

    def op(self, eng, fn, reads=(), writes=()):
        waits = self._need(eng, reads, writes)
        self._emit_waits(eng, waits)
        self.val[eng] += 1
        v = self.val[eng]
        sem = self.sems[eng]
        self.q[eng].append(lambda E, fn=fn, sem=sem: fn(E).then_inc(sem, 1))
        tok = (eng, v, eng)
        for b in reads:
            b.r.append(tok)
            if len(b.r) > 24:
                b.r = b.r[-24:] if False else self._compact(b.r)
        for b in writes:
            b.w = tok
            b.r = []
        self.n_op += 1

    @staticmethod
    def _compact(toks):
        best = {}
        for k, v, e in toks:
            if k not in best or best[k][1] < v:
                best[k] = (k, v, e)
        return list(best.values())

    def dma(self, fn, reads=(), writes=(), queue="sp"):
        waits = self._need(queue, reads, writes)
        self._emit_waits(queue, waits)
        if queue == "pool":
            k = self.sw_dma[self.sw_rr % len(self.sw_dma)]
            self.sw_rr += 1
        else:
            k = self.free_dma[self.dma_rr % len(self.free_dma)]
            self.dma_rr += 1
        if self.val[k] > 0 and self.waited[queue].get(k, 0) < self.val[k]:
            self.waited[queue][k] = self.val[k]
            self._emit_waits(queue, [(k, self.val[k])])
        self.val[k] += 16
        v = self.val[k]
        sem = self.sems[k]
        self.q[queue].append(lambda E, fn=fn, sem=sem: fn(E).then_inc(sem, 16))
        tok = (k, v, "dma")
        for b in reads:
            b.r.append(tok)
        for b in writes:
            b.w = tok
            b.r = []
        self.n_op += 1

    def barrier(self):
        for eng in self.ENGS:
            waits = []
            for k, v in self.val.items():
                if v > 0 and self.waited[eng].get(k, 0) < v and not (k == eng):
                    self.waited[eng][k] = v
                    waits.append((k, v))
            self._emit_waits(eng, waits)

    def final_wait(self, bufs, eng="sp"):
        waits = self._need(eng, bufs, bufs)
        self._emit_waits(eng, waits)

    def emit(self):
        nc = self.nc
        with nc.Block() as block:
            @block.sync
            def _(E):
                for f in self.q["sp"]:
                    f(E)

            @block.tensor
            def _(E):
                for f in self.q["pe"]:
                    f(E)

            @block.scalar
            def _(E):
                for f in self.q["act"]:
                    f(E)

            @block.vector
            def _(E):
                for f in self.q["dve"]:
                    f(E)

            @block.gpsimd
            def _(E):
                for f in self.q["pool"]:
                    f(E)
D = 1024; NH = 8; DH = 64; DFF = 2816; NRW = 15
TT = 512
P3DT = BF16
P3D2 = BF16


def build(NB, S, debug=False):
    nc = bass.Bass("TRN2", target_bir_lowering=False)
    TOK = NB * S
    NTL = S // TT
    dk = "ExternalOutput" if debug else "Internal"
    def din(name, shape, dt=F32):
        return nc.dram_tensor(name, list(shape), dt, kind="ExternalInput").ap()
    def dsc(name, shape, dt=BF16):
        return nc.dram_tensor(name, list(shape), dt, kind=dk).ap()
    x_d = din("x", [TOK, D]); cT_d = din("cT", [128, 8, NB]); wmod_d = din("w_mod", [D, 6 * D])
    bmodc_d = din("bmod_col", [128, 48]); bmodg_d = din("bmod_g", [NB, 2, D])
    ln1_d = din("ln1_col", [128, 8]); ln2_d = din("ln2_col", [128, 8]); fing_d = din("fin_g", [D])
    wqkv_d = din("w_qkv", [D, 1536]); wf_d = din("w_f", [D, 8]); wrw_d = din("w_rw", [D, NRW * 128]); wgate_d = din("w_gate", [D, 2048])
    bf_d = din("fox_bf", [8, 1]); mu_d = din("mu_col", [128, NRW])
    rwp_d = din("rw_par", [128, 7, 4])
    w2_d = din("w2", [64, 512]); a2_d = din("a2", [64, 512]); g2_d = din("g2", [128, 512])
    foxo_d = din("fox_w_out", [512, D]); rwo_d = din("rwkv_w_out", [512, D]); wo_d = din("w_o", [D, D])
    wup_d = din("w_up", [D, 2 * DFF]); wdn_d = din("w_down", [DFF, D])
    convw_d = din("conv_col", [128, 22, 3]); convb_d = din("convb_col", [128, 22])
    out_d = nc.dram_tensor("out", [TOK, D], F32, kind="ExternalOutput").ap()
    q_s = dsc("q_s", [NB, NH, 68, S]); k_s = dsc("k_s", [NB, NH, 68, S]); v_s = dsc("v_s", [NB, S, NH * 65])
    gate_s = dsc("gate_s", [NB, 16, 128, S]); rw_s = dsc("rw_s", [NB, 7, NH, 64, S]); e_s = dsc("e_s", [NB, 2, NH, 64, S])
    pc_s = dsc("pc_s", [NB, NH, 64, S // 64], F32); gates_s = dsc("gates_s", [NB, 2, D], F32)
    ofox_s = dsc("ofox_s", [NB, 4, 128, S]); orw_s = dsc("orw_s", [NB, 4, 128, S]); h_s = dsc("h_s", [TOK, D], F32)

    with ExitStack() as top:
        SC = Sched(nc, top)
        B = {}
        def buf(n):
            if n not in B:
                B[n] = Buf(n)
            return B[n]
        def bl(names):
            return [buf(n) for n in names.split()] if isinstance(names, str) else [buf(n) for n in names]
        from collections import deque
        jobs = deque(); jstate = {"in": False, "n": 0, "K": 14}
        _op, _dma = SC.op, SC.dma
        def tick():
            if jstate["in"] or not jobs:
                return
            jstate["n"] += 1
            if jstate["n"] % jstate["K"] == 0:
                jstate["in"] = True
                jobs.popleft()()
                jstate["in"] = False
        def flush_jobs():
            jstate["in"] = True
            while jobs:
                jobs.popleft()()
            jstate["in"] = False
        def op_t(*a, **k):
            _op(*a, **k); tick()
        def dma_t(*a, **k):
            _dma(*a, **k); tick()
        SC.op = op_t; SC.dma = dma_t
        def mm(out, lhsT, rhs, start, stop, R, W, skip=False):
            _op("pe", lambda E: E.matmul(out, lhsT=lhsT, rhs=rhs, start=start, stop=stop, skip_group_check=skip), bl(R), bl(W))
        def act(out, in_, func, R, W, bias=None, scale=None, accum=None):
            kw = {}
            if bias is not None: kw["bias"] = bias
            if scale is not None: kw["scale"] = scale
            if accum is not None: kw["accum_out"] = accum
            SC.op("act", lambda E: E.activation(out=out, in_=in_, func=func, **kw), bl(R), bl(W))
        def tt(eng, out, in0, in1, op, R, W):
            SC.op(eng, lambda E: E.tensor_tensor(out=out, in0=in0, in1=in1, op=op), bl(R), bl(W))
        def ts(eng, out, in0, s1, s2, op0, op1, R, W):
            if op1 is None:
                SC.op(eng, lambda E: E.tensor_scalar(out=out, in0=in0, scalar1=s1, scalar2=None, op0=op0), bl(R), bl(W))
            else:
                SC.op(eng, lambda E: E.tensor_scalar(out=out, in0=in0, scalar1=s1, scalar2=s2, op0=op0, op1=op1), bl(R), bl(W))
        def stt(out, in0, sc, in1, op0, op1, R, W):
            SC.op("dve", lambda E: E.scalar_tensor_tensor(out=out, in0=in0, scalar=sc, in1=in1, op0=op0, op1=op1), bl(R), bl(W))
        def cp(eng, out, in_, R, W):
            if eng == "act":
                SC.op("act", lambda E: E.activation(out=out, in_=in_, func=AF.Identity), bl(R), bl(W))
            else:
                SC.op(eng, lambda E: E.tensor_copy(out=out, in_=in_), bl(R), bl(W))
        def mset(eng, ap, val, W):
            SC.op(eng, lambda E: E.memset(ap, val), [], bl(W))
        def dma(out, in_, R, W, queue="sp"):
            SC.dma(lambda E: E.dma_start(out=out, in_=in_), bl(R), bl(W), queue=queue)

        gsb = lambda n, shp, dt=F32: top.enter_context(nc.sbuf_tensor(n, shp, dt))
        ident = gsb("ident", [128, 128], BF16)
        cst = gsb("cst", [128, 8])
        modc = gsb("modc", [128, 6, 8, NB])
        G1 = gsb("G1", [128, 8, NB]); G2 = gsb("G2", [128, 8, NB])
        mset("pool", ident[:], 1.0, "ident")
        SC.op("pool", lambda E: E.affine_select(out=ident[:], in_=ident[:], pattern=[[-1, 128]], compare_op=ALU.is_equal, fill=0.0, base=0, channel_multiplier=1), bl("ident"), bl("ident"))
        for i, v in enumerate([1e-6, 1.0, -0.5, 0.0, 64e-5, 1e-24]):
            mset("pool", cst[:, i:i + 1], v, "cst")
        EPS, ONE, MHALF, ZERO, GNEPS = (cst[:, i:i + 1] for i in range(5))

        pw = ExitStack()
        pwsb = lambda n, shp, dt=F32: pw.enter_context(nc.sbuf_tensor(n, shp, dt))
        wqkv = pwsb("wqkv", [128, 8, 1536], BF16); wfb = pwsb("wfb", [128, 8, 8], BF16)
        wrw = pwsb("wrw", [128, 8, NRW * 128], BF16); wgate = pwsb("wgate", [128, 8, 2048], BF16)
        w2b = pwsb("w2b", [64, 512], BF16); a2b = pwsb("a2b", [64, 512], BF16); g2b = pwsb("g2b", [128, 512], BF16)
        vw = lambda d: d.rearrange("(kc p) n -> p kc n", p=128)
        dma(wqkv[:], vw(wqkv_d), "", "wqkv", queue="pool"); dma(wfb[:], vw(wf_d), "", "wfb", queue="pool")
        dma(wrw[:], vw(wrw_d), "", "wrw", queue="pool"); dma(wgate[:], vw(wgate_d), "", "wgate", queue="pool")
        dma(w2b[:], w2_d, "", "w2b", queue="pool"); dma(a2b[:], a2_d, "", "a2b", queue="pool"); dma(g2b[:], g2_d, "", "g2b", queue="pool")
        with ExitStack() as ph:
            sb = lambda n, shp, dt=F32: ph.enter_context(nc.sbuf_tensor(n, shp, dt))
            ps = lambda n, shp, dt=F32: ph.enter_context(nc.psum_tensor(n, shp, dt))
            cTt = sb("cTt", [128, 8, NB]); scT = sb("scT", [128, 8, NB]); bmodc = sb("bmodc", [128, 48]); bmodg = sb("bmodg", [NB, 2, D])
            ln1c = sb("ln1c", [128, 8]); ln2c = sb("ln2c", [128, 8]); grow = sb("grow", [NB, 2, D])
            wm = [sb("wm%d" % i, [128, 8, D]) for i in range(2)]
            pcol = ps("pcol", [128, 8, NB]); prow = [ps("prow%d" % i, [NB, 512]) for i in range(2)]
            dma(cTt[:], cT_d, "", "cTt"); dma(bmodc[:], bmodc_d, "", "bmodc"); dma(bmodg[:], bmodg_d, "", "bmodg")
            dma(ln1c[:], ln1_d, "", "ln1c"); dma(ln2c[:], ln2_d, "", "ln2c")
            act(scT[:], cTt[:], AF.Silu, "cTt", "scT")
            wmv = wmod_d.rearrange("(kc p) n -> p kc n", p=128)
            for m in range(6):
                w_ = wm[m % 2]; wn = "wm%d" % (m % 2)
                dma(w_[:], wmv[:, :, m * D:(m + 1) * D], "", wn)
                if m in (2, 5):
                    g = 0 if m == 2 else 1
                    for hf in range(2):
                        pr = prow[hf]; pn = "prow%d" % hf
                        for kc in range(8):
                            mm(pr[:, :], scT[:, kc, :], w_[:, kc, hf * 512:(hf + 1) * 512], kc == 0, kc == 7, [wn, "scT"], [pn])
                        tt("dve", grow[:, g, hf * 512:(hf + 1) * 512], pr[:, :], bmodg[:, g, hf * 512:(hf + 1) * 512], ALU.add, [pn, "bmodg"], ["grow"])
                else:
                    for j in range(8):
                        for kc in range(8):
                            mm(pcol[:, j, :], w_[:, kc, j * 128:(j + 1) * 128], scT[:, kc, :], kc == 0, kc == 7, [wn, "scT"], ["pcol"])
                    for b in range(NB):
                        tt("dve", modc[:, m, :, b], pcol[:, :, b], bmodc[:, m * 8:(m + 1) * 8], ALU.add, ["pcol", "bmodc"], ["modc"])
            dma(gates_s, grow[:], "grow", "gates_s")
            for b in range(NB):
                stt(G1[:, :, b], modc[:, 1, :, b], 1.0, ln1c[:], ALU.add, ALU.mult, ["modc", "ln1c"], ["G1"])
                stt(G2[:, :, b], modc[:, 4, :, b], 1.0, ln2c[:], ALU.add, ALU.mult, ["modc", "ln2c"], ["G2"])

        def norm_T(xt, xn, nblk, Gc, shc, b, uT, un, tmp, tmpn, ssq, rstd, xs, pT, pfx, lnexp=False):
            for blk in range(nblk):
                act(tmp, xt[:, blk, :], AF.Square, [xn], [tmpn, pfx + "ssq"], accum=ssq[:, blk:blk + 1])
            if lnexp:
                act(rstd[:, 0:nblk], ssq[:, 0:nblk], AF.Ln, [pfx + "ssq"], [pfx + "rstd"], bias=EPS, scale=1.0 / D)
                act(rstd[:, 0:nblk], rstd[:, 0:nblk], AF.Exp, [pfx + "rstd"], [pfx + "rstd"], scale=-0.5)
            else:
                act(rstd[:, 0:nblk], ssq[:, 0:nblk], AF.Sqrt, [pfx + "ssq"], [pfx + "rstd"], bias=EPS, scale=1.0 / D)
                SC.op("dve", lambda E: E.reciprocal(out=rstd[:, 0:nblk], in_=rstd[:, 0:nblk]), bl([pfx + "rstd"]), bl([pfx + "rstd"]))
            for blk in range(nblk):
                ts("dve", xs[:, blk, :], xt[:, blk, :], rstd[:, blk:blk + 1], None, ALU.mult, None, [xn, pfx + "rstd"], [pfx + "xs"])
                p_ = pT[blk % 2]; pn = pfx + "pT%d" % (blk % 2)
                for j in range(8):
                    SC.op("pe", lambda E, p_=p_, j=j, blk=blk: E.transpose(out=p_[:, j * 128:(j + 1) * 128], in_=xs[:, blk, j * 128:(j + 1) * 128], identity=ident[:]), bl([pfx + "xs", "ident"]), bl([pn]))
                for j in range(8):
                    act(uT[:, j, blk * 128:(blk + 1) * 128], p_[:, j * 128:(j + 1) * 128], AF.Identity, [pn, "G1", "G2", "modc"], [un], bias=shc[:, j, b:b + 1], scale=Gc[:, j, b:b + 1])

        SC.barrier()
        with ExitStack() as ph:
            sb = lambda n, shp, dt=F32: ph.enter_context(nc.sbuf_tensor(n, shp, dt))
            ps = lambda n, shp, dt=F32: ph.enter_context(nc.psum_tensor(n, shp, dt))
            muc = sb("muc", [128, NRW]); ommu = sb("ommu", [128, NRW]); rwp = sb("rwp", [128, 7, 4]); omka = sb("omka", [128, 4])
            nw0 = sb("nw0", [128, 4]); na0 = sb("na0", [128, 4]); bfc = sb("bfc", [8, 1]); nbf = sb("nbf", [8, 1]); blk1 = sb("blk1", [128, 128])
            dma(muc[:], mu_d, "", "muc"); dma(rwp[:], rwp_d, "", "rwp"); dma(bfc[:], bf_d, "", "bfc")
            ts("dve", ommu[:], muc[:], -1.0, 1.0, ALU.mult, ALU.add, "muc", "ommu")
            ts("dve", omka[:], rwp[:, 3, :], -1.0, 1.0, ALU.mult, ALU.add, "rwp", "omka")
            ts("dve", nw0[:], rwp[:, 0, :], -1.0, None, ALU.mult, None, "rwp", "nw0")
            ts("dve", na0[:], rwp[:, 1, :], -1.0, None, ALU.mult, None, "rwp", "na0")
            ts("dve", nbf[:], bfc[:], -1.0, None, ALU.mult, None, "bfc", "nbf")
            mset("pool", blk1[:], 0.0, "blk1"); mset("pool", blk1[0:64, 0:64], 1.0, "blk1"); mset("pool", blk1[64:128, 64:128], 1.0, "blk1")
            W0, A0, KK_, KA, RK, LNW, LNB = (lambda c4, i=i: rwp[:, i, c4:c4 + 1] for i in range(7))
            fcar = sb("fcar", [8, 1]); pcar = sb("pcar", [128, NRW])
            xt = sb("xt", [128, 4, D]); xs = sb("xs", [128, 4, D], BF16); uTl = [sb("uT%d" % i, [128, 8, TT], BF16) for i in range(2)]; uT = uTl[0]; un = "uT0"
            ssq = sb("ssq", [128, 4]); rstd = sb("rstd", [128, 4])
            qst2 = [sb("qst0", [128, TT], BF16)] * 2; vt = sb("vt", [128, 2, NH, 65], BF16)
            fe = sb("fe", [8, TT]); cum = sb("cum", [8, TT])
            hi = sb("hi", [8, TT], BF16); lo = sb("lo", [8, TT], BF16); nhi = sb("nhi", [8, TT], BF16); nlo = sb("nlo", [8, TT], BF16)
            one8 = sb("one8", [8, 2, TT], BF16)
            Tsets = [[sb("t%d_%d" % (i, s_), [128, TT]) for i in range(9)] for s_ in range(2)]; T_ = Tsets[0]
            praws = [sb("praw_%d" % s_, [128, TT + 1]) for s_ in range(2)]; praw0 = praws[0]; onesf = sb("onesf", [128, 64])
            twb = sb("twb", [64, TT], BF16); xab = sb("xab", [64, TT], BF16); sgb = sb("sgb", [128, TT], BF16)
            rwts = [sb("rwt_%d" % s_, [128, 7, TT], BF16) for s_ in range(2)]; ets = [sb("et_%d" % s_, [128, 2, TT], BF16) for s_ in range(2)]; pcts = [sb("pct_%d" % s_, [128, TT // 64]) for s_ in range(2)]
            gtl = sb("gt", [128, 2, TT], BF16)
            pT = [ps("pT%d" % i, [128, D], BF16) for i in range(2)]
            pm = [ps("pm%d" % i, [128, TT]) for i in range(3)]
            psss = [ps("pss_%d" % s_, [128, TT]) for s_ in range(2)]; pf = ps("pf", [8, TT])
            mset("pool", one8[:], 1.0, "one8"); mset("pool", onesf[:], 1.0, "onesf")
            mset("pool", vt[:], 1.0, "vt")
            pmi = [0]
            def nextpm():
                i = pmi[0] % 3; pmi[0] += 1
                return pm[i], "pm%d" % i

            def proj_fm(wt, wn, c0, M):
                p_, pn = nextpm()
                for kc in range(8):
                    mm(p_[0:M, :], wt[:, kc, c0:c0 + M], uT[:, kc, :], kc == 0, kc == 7, [wn, un], [pn])
                return p_, pn

            def lerp(cc, dst, dn, praw=None, pn_="praw_0"):
                praw = praw0 if praw is None else praw
                p_, pn = proj_fm(wrw, "wrw", cc * 128, 128)
                cp("act", praw[:, 1:TT + 1], p_[:, :], [pn], [pn_])
                cp("pool", praw[:, 0:1], pcar[:, cc:cc + 1], ["pcar"], [pn_])
                ts("dve", dst[:], praw[:, 1:TT + 1], ommu[:, cc:cc + 1], None, ALU.mult, None, [pn_, "ommu"], [dn])
                stt(dst[:], praw[:, 0:TT], muc[:, cc:cc + 1], dst[:], ALU.mult, ALU.add, [pn_, "muc", dn], [dn])
                cp("pool", pcar[:, cc:cc + 1], praw[:, TT:TT + 1], [pn_], ["pcar"])

            def prologue(gt):
                norm_T(xt, "xt", 4, G1, modc[:, 0], gt // NTL, uTl[gt % 2], "uT%d" % (gt % 2), xs[:, 0, :], "xs", ssq, rstd, xs, pT, "", lnexp=True)
                if gt + 1 < NB * NTL:
                    t1 = (gt + 1) * TT
                    dma(xt[:], x_d[t1:t1 + TT, :].rearrange("(k p) d -> p k d", p=128), "", "xt")
            dma(xt[:], x_d[0:TT, :].rearrange("(k p) d -> p k d", p=128), "", "xt")
            prologue(0)
            for b in range(NB):
                mset("pool", fcar[:], 0.0, "fcar"); mset("pool", pcar[:], 0.0, "pcar")
                for tl in range(NTL):
                    t0 = b * S + tl * TT
                    tsl = slice(tl * TT, (tl + 1) * TT)
                    gt = b * NTL + tl
                    uT = uTl[gt % 2]; un = "uT%d" % (gt % 2)
                    def qk_job(qk, hp, b=b, tsl=tsl):
                        p_, pn = proj_fm(wqkv, "wqkv", qk * 512 + hp * 128, 128)
                        st_ = qst2[(qk * 4 + hp) % 2]; sn = "qst0"
                        if qk == 0:
                            act(st_[:], p_[:, :], AF.Identity, [pn], [sn], scale=0.125)
                        else:
                            cp("dve", st_[:], p_[:, :], [pn], [sn])
                        for hh in range(2):
                            dma((q_s if qk == 0 else k_s)[b, 2 * hp + hh, 0:64, tsl], st_[hh * 64:(hh + 1) * 64, :], [sn], "qk_s")
                    def v_job(blk, b=b, tsl=tsl, tl=tl):
                        p_, pn = nextpm()
                        for kc in range(8):
                            mm(p_[:, :], uT[:, kc, blk * 128:(blk + 1) * 128], wqkv[:, kc, 1024:1536], kc == 0, kc == 7, ["wqkv", un], [pn])
                        cp("act", vt[:, blk % 2, :, 0:64], p_[:, :].rearrange("p (h d) -> p h d", h=NH), [pn], ["vt"])
                        if blk % 2 == 1:
                            hs = slice(tl * TT + (blk - 1) * 128, tl * TT + (blk + 1) * 128)
                            dma(v_s[b, hs, :].rearrange("(k p) c -> p k c", p=128), vt[:].rearrange("p k h c -> p k (h c)"), "vt", "v_s")
                    def g_job(gc, b=b, tsl=tsl):
                        gq, gg = gc // 2, gc % 2
                        p_, pn = proj_fm(wgate, "wgate", gc * 128, 128)
                        act(gtl[:, gg, :], p_[:, :], AF.Sigmoid, [pn], ["gt"])
                        if gg == 1:
                            dma(gate_s[b, gq * 2:(gq + 1) * 2, :, tsl].rearrange("g p t -> p g t"), gtl[:], "gt", "gate_s")
                    jstate["in"] = True
                    lerp(12, T_[0], "t0_0"); act(twb[:], T_[0][0:64, :], AF.Tanh, ["t0_0"], ["twb"])
                    lerp(14, T_[0], "t0_0"); act(sgb[:], T_[0][:, :], AF.Sigmoid, ["t0_0"], ["sgb"])
                    for gc in range(16):
                        g_job(gc)
                    lerp(13, T_[0], "t0_0"); cp("dve", xab[:], T_[0][0:64, :], ["t0_0"], ["xab"])
                    jstate["in"] = False
                    for qk in range(2):
                        for hp in range(4):
                            jobs.append(lambda qk=qk, hp=hp: qk_job(qk, hp))
                    for blk in range(4):
                        jobs.append(lambda blk=blk: v_job(blk))
                    if gt + 1 < NB * NTL:
                        jobs.append(lambda gt=gt: prologue(gt + 1))
                    for kc in range(8):
                        mm(pf[:, :], wfb[:, kc, :], uT[:, kc, :], kc == 0, kc == 7, ["wfb", un], ["pf"])
                    act(fe[:], pf[:, :], AF.Exp, ["pf", "nbf"], ["fe"], bias=nbf[:], scale=-1.0)
                    act(fe[:], fe[:], AF.Ln, ["fe", "cst"], ["fe"], bias=ONE[0:8, :])
                    SC.op("dve", lambda E: E.tensor_tensor_scan(out=cum[:], data0=one8[:, 0, :], data1=fe[:], initial=fcar[:], op0=ALU.mult, op1=ALU.subtract), bl("one8 fe fcar"), bl("cum"))
                    cp("pool", fcar[:], cum[:, TT - 1:TT], ["cum"], ["fcar"])
                    cp("dve", hi[:], cum[:], ["cum"], ["hi"])
                    tt("dve", lo[:], cum[:], hi[:], ALU.subtract, ["cum", "hi"], ["lo"])
                    ts("pool", nhi[:], hi[:], -1.0, None, ALU.mult, None, ["hi"], ["nhi"])
                    ts("pool", nlo[:], lo[:], -1.0, None, ALU.mult, None, ["lo"], ["nlo"])
                    dma(q_s[b, :, 64, tsl], hi[:], "hi", "qk_s"); dma(q_s[b, :, 65, tsl], lo[:], "lo", "qk_s")
                    dma(q_s[b, :, 66:68, tsl], one8[:], "one8", "qk_s"); dma(k_s[b, :, 64:66, tsl], one8[:], "one8", "qk_s")
                    dma(k_s[b, :, 66, tsl], nhi[:], "nhi", "qk_s"); dma(k_s[b, :, 67, tsl], nlo[:], "nlo", "qk_s")
                    def chain(c4, si):
                        T_s = Tsets[si]; praw = praws[si]; rwt = rwts[si]; et = ets[si]; pct = pcts[si]; pss = psss[si]
                        N = lambda s: s + '_%d' % si
                        r_, k_, v_, e2, cl, a_, g_, kk, t8 = T_s
                        lerp(c4, r_, N("t0"), praw, N("praw")); lerp(4 + c4, k_, N("t1"), praw, N("praw")); lerp(8 + c4, v_, N("t2"), praw, N("praw"))
                        yield
                        cs = slice(c4 * 128, (c4 + 1) * 128)
                        p_, pn = nextpm(); mm(p_[:, :], w2b[:, cs], twb[:], True, True, ["w2b", "twb"], [pn])
                        act(e2[:], p_[:, :], AF.Exp, [pn, "nw0"], [N("t3")], bias=nw0[:, c4:c4 + 1], scale=-1.0)
                        yield
                        act(e2[:], e2[:], AF.Ln, [N("t3"), "cst"], [N("t3")], bias=ONE)
                        yield
                        act(e2[:], e2[:], AF.Exp, [N("t3"), "cst"], [N("t3")], bias=MHALF, scale=-1.0)
                        yield
                        for c in range(TT // 64):
                            sl = slice(c * 64, (c + 1) * 64)
                            SC.op("dve", lambda E, sl=sl: E.tensor_tensor_scan(out=cl[:, sl], data0=onesf[:], data1=e2[:, sl], initial=0.0, op0=ALU.mult, op1=ALU.subtract), bl(["onesf", N("t3")]), bl([N("t4")]))
                        p_, pn = nextpm(); mm(p_[:, :], a2b[:, cs], xab[:], True, True, ["a2b", "xab"], [pn])
                        act(a_[:], p_[:, :], AF.Exp, [pn, "na0"], [N("t5")], bias=na0[:, c4:c4 + 1], scale=-1.0); act(a_[:], a_[:], AF.Ln, [N("t5"), "cst"], [N("t5")], bias=ONE); act(a_[:], a_[:], AF.Exp, [N("t5")], [N("t5")], scale=-1.0)
                        yield
                        p_, pn = nextpm(); mm(p_[:, :], g2b[:, cs], sgb[:], True, True, ["g2b", "sgb"], [pn])
                        cp("act", g_[:], p_[:, :], [pn], [N("t6")])
                        yield
                        ts("dve", kk[:], k_[:], KK_(c4), None, ALU.mult, None, [N("t1"), "rwp"], [N("t7")])
                        yield
                        tt("dve", t8[:], kk[:], kk[:], ALU.mult, [N("t7")], [N("t8")])
                        yield
                        mm(pss[:, :], blk1[:], t8[:], True, True, ["blk1", N("t8")], [N("pss")])
                        ts("dve", t8[:], pss[:, :], 1e-24, None, ALU.max, None, [N("pss")], [N("t8")])
                        yield
                        act(t8[:], t8[:], AF.Ln, [N("t8")], [N("t8")])
                        yield
                        act(t8[:], t8[:], AF.Exp, [N("t8")], [N("t8")], scale=-0.5)
                        yield
                        tt("dve", kk[:], kk[:], t8[:], ALU.mult, [N("t7"), N("t8")], [N("t7")])
                        yield
                        ts("dve", t8[:], a_[:], KA(c4), omka[:, c4:c4 + 1], ALU.mult, ALU.add, [N("t5"), "rwp", "omka"], [N("t8")])
                        yield
                        tt("dve", k_[:], k_[:], t8[:], ALU.mult, [N("t1"), N("t8")], [N("t1")])
                        yield
                        stt(t8[:], r_[:], RK(c4), k_[:], ALU.mult, ALU.mult, [N("t0"), "rwp", N("t1")], [N("t8")])
                        yield
                        mm(pss[:, :], blk1[:], t8[:], True, True, ["blk1", N("t8")], [N("pss")])
                        tt("dve", t8[:], pss[:, :], v_[:], ALU.mult, [N("pss"), N("t2")], [N("t8")])
                        yield
                        stt(et[:, 0, :], t8[:], LNB(c4), g_[:], ALU.add, ALU.mult, [N("t8"), "rwp", N("t6")], [N("et")])
                        yield
                        ts("dve", et[:, 1, :], g_[:], LNW(c4), None, ALU.mult, None, [N("t6"), "rwp"], [N("et")])
                        yield
                        act(t8[:], cl[:], AF.Exp, [N("t4")], [N("t8")])
                        yield
                        tt("dve", rwt[:, 3, :], r_[:], t8[:], ALU.mult, [N("t0"), N("t8")], [N("rwt")])
                        yield
                        tt("dve", t8[:], cl[:], e2[:], ALU.add, [N("t4"), N("t3")], [N("t8")])
                        yield
                        act(t8[:], t8[:], AF.Exp, [N("t8")], [N("t8")])
                        yield
                        stt(rwt[:, 0, :], kk[:], -1.0, t8[:], ALU.mult, ALU.mult, [N("t7"), N("t8")], [N("rwt")])
                        yield
                        tt("dve", a_[:], kk[:], a_[:], ALU.mult, [N("t7"), N("t5")], [N("t5")])
                        yield
                        act(t8[:], cl[:], AF.Exp, [N("t4")], [N("t8")], scale=-1.0)
                        yield
                        tt("dve", rwt[:, 1, :], a_[:], t8[:], ALU.mult, [N("t5"), N("t8")], [N("rwt")])
                        yield
                        tt("dve", rwt[:, 2, :], k_[:], t8[:], ALU.mult, [N("t1"), N("t8")], [N("rwt")])
                        yield
                        for c in range(TT // 64):
                            sl = slice(c * 64, (c + 1) * 64)
                            act(t8[:, sl], cl[:, sl], AF.Exp, [N("t4")], [N("t8")], bias=cl[:, c * 64 + 63:c * 64 + 64], scale=-1.0)
                        tt("dve", rwt[:, 4, :], a_[:], t8[:], ALU.mult, [N("t5"), N("t8")], [N("rwt")])
                        yield
                        tt("dve", rwt[:, 5, :], k_[:], t8[:], ALU.mult, [N("t1"), N("t8")], [N("rwt")])
                        yield
                        cp("pool", rwt[:, 6, :], v_[:], [N("t2")], [N("rwt")])
                        yield
                        act(pct[:], cl[:, 63:TT:64], AF.Exp, [N("t4")], [N("pct")])
                        yield
                        dma(rw_s[b, :, 2 * c4:2 * c4 + 2, :, tsl].rearrange("q h k t -> (h k) q t"), rwt[:], N("rwt"), "rw_s")
                        yield
                        dma(e_s[b, :, 2 * c4:2 * c4 + 2, :, tsl].rearrange("q h k t -> (h k) q t"), et[:], N("et"), "e_s")
                        yield
                        dma(pc_s[b, 2 * c4:2 * c4 + 2, :, tl * 8:(tl + 1) * 8].rearrange("h k c -> (h k) c"), pct[:], N("pct"), "pc_s")
                        yield
                    for pair in ((0, 1), (2, 3)):
                        gens = [chain(pair[0], 0), chain(pair[1], 1)]
                        live = [True, True]
                        while any(live):
                            for gi in range(2):
                                if live[gi]:
                                    try:
                                        next(gens[gi])
                                    except StopIteration:
                                        live[gi] = False
                    flush_jobs()

        SC.barrier()
        pw.close()
        with ExitStack() as ph:
            sb = lambda n, shp, dt=F32: ph.enter_context(nc.sbuf_tensor(n, shp, dt))
            ps = lambda n, shp, dt=F32: ph.enter_context(nc.psum_tensor(n, shp, dt))
            NKB = S // 128
            kT = sb("kT", [68, NH, S], BF16); vS = sb("vS", [128, NKB, NH * 65], BF16); qTl = [sb("qq%d" % i, [68, NH, TT], BF16) for i in range(2)]
            PT = [sb("PT%d" % i, [128, TT], BF16) for i in range(3)]
            msk = sb("msk", [128, 4, TT], BF16); otok = sb("otok", [128, 4, NH * 64], BF16); rec = sb("rec", [128, 4])
            ofT = sb("ofT", [128, 4, TT], BF16); clampS = sb("clampS", [128, TT])
            pS = [ps("pS%d" % i, [128, TT]) for i in range(3)]
            pO = [ps("pO%d" % i, [128, 4, 128]) for i in range(2)]; hcnt = [0]
            pTr = ps("pTr", [128, D], BF16)
            mset("pool", msk[:], 1.0, "msk")
            for jj in range(4):
                SC.op("pool", lambda E, jj=jj: E.affine_select(out=msk[:, jj, :], in_=msk[:, jj, :], pattern=[[1, TT]], compare_op=ALU.is_ge, fill=0.0, base=-jj * 128, channel_multiplier=-1), bl("msk"), bl("msk"))
            for b in range(NB):
                dma(kT[:], k_s[b].rearrange("h k t -> k h t"), "qk_s", "kT")
                dma(vS[:], v_s[b].rearrange("(j p) c -> p j c", p=128), "v_s", "vS")
                def qload(tl_, b=b):
                    dma(qTl[tl_ % 2][:], q_s[b, :, :, tl_ * TT:(tl_ + 1) * TT].rearrange("h k t -> k h t"), "qk_s", "qq%d" % (tl_ % 2))
                qload(0)
                for tl in range(NTL):
                    tsl = slice(tl * TT, (tl + 1) * TT)
                    qT = qTl[tl % 2]; qn = "qq%d" % (tl % 2)
                    if tl + 1 < NTL:
                        qload(tl + 1)
                    for h in range(NH):
                        nj = 4 * tl + 4
                        ob = hcnt[0] % 2; hcnt[0] += 1
                        pOh = pO[ob]; pOn = "pO%d" % ob
                        def QK(j):
                            mm(pS[j % 3][:, :], kT[:, h, j * 128:(j + 1) * 128], qT[:, h, :], True, True, ["kT", qn], ["pS%d" % (j % 3)])
                        QK(0)
                        if nj > 1:
                            QK(1)
                        for j in range(nj):
                            p_ = pS[j % 3]; pn = "pS%d" % (j % 3); P_ = PT[j % 3]; Pn = "PT%d" % (j % 3)
                            jj = j - 4 * tl
                            if jj >= 0:
                                c0 = jj * 128
                                ts("dve", clampS[:, c0:], p_[:, c0:], 60.0, None, ALU.min, None, [pn], ["clampS"])
                                act(P_[:, c0:], clampS[:, c0:], AF.Exp, ["clampS"], [Pn])
                                tt("dve", P_[:, c0:], P_[:, c0:], msk[:, jj, c0:], ALU.mult, [Pn, "msk"], [Pn])
                            else:
                                act(P_[:], p_[:, :], AF.Exp, [pn], [Pn])
                            if j + 2 < nj:
                                QK(j + 2)
                            for qb in range(4):
                                if j <= 4 * tl + qb:
                                    mm(pOh[:, qb, 0:65], P_[:, qb * 128:(qb + 1) * 128], vS[:, j, h * 65:(h + 1) * 65], j == 0 and qb == 0, j == 4 * tl + qb, [Pn, "vS"], [pOn], skip=True)
                        for qb in range(4):
                            SC.op("dve", lambda E, qb=qb, pOh=pOh: E.reciprocal(out=rec[:, qb:qb + 1], in_=pOh[:, qb, 64:65]), bl([pOn]), bl("rec"))
                            ts("dve", otok[:, qb, h * 64:(h + 1) * 64], pOh[:, qb, 0:64], rec[:, qb:qb + 1], None, ALU.mult, None, [pOn, "rec"], ["otok"])
                    for qb in range(4):
                        for c in range(4):
                            SC.op("pe", lambda E, qb=qb, c=c: E.transpose(out=pTr[:, c * 128:(c + 1) * 128], in_=otok[:, qb, c * 128:(c + 1) * 128], identity=ident[:]), bl("otok ident"), bl("pTr"))
                        cp("act", ofT[:, :, qb * 128:(qb + 1) * 128], pTr[:, 0:512].rearrange("p (c t) -> p c t", c=4), ["pTr"], ["ofT"])
                    dma(ofox_s[b, :, :, tsl].rearrange("c p t -> p c t"), ofT[:], "ofT", "ofox_s")

        SC.barrier()
        with ExitStack() as ph:
            sb = lambda n, shp, dt=F32: ph.enter_context(nc.sbuf_tensor(n, shp, dt))
            ps = lambda n, shp, dt=F32: ph.enter_context(nc.psum_tensor(n, shp, dt))
            PD = P3DT
            D2 = P3D2
            UC = 4 if PD == BF16 else 2
            UT = UC * 64
            NU = S // UT
            RW = [sb("RW%d" % i, [64, 7, NH, UT], PD) for i in range(2)]
            ET = [sb("ET%d" % i, [128, 2, 4, UT], BF16) for i in range(2)]
            PC = [sb("PC%d" % i, [64, NH, UC]) for i in range(2)]
            Hf = sb("Hf", [64, NH, 64]); Hhi = sb("Hhi", [64, NH, 64], PD); Hlo = sb("Hlo", [64, NH, 64], PD)
            mLT = sb("mLT", [64, NH, 64]); mL = sb("mL", [64, NH, 64]); mGT = sb("mGT", [64, NH, 64]); idF = sb("idF", [64, NH, 64]); idP = sb("idP", [64, 64], PD)
            tok = [[sb("tok%d_%d" % (p, c), [64, 4, NH, 64], PD) for c in range(UC)] for p in range(2)]
            TTt = [[sb("TT%d_%d" % (p, c), [64, NH, 64], D2) for c in range(UC)] for p in range(2)]
            LakT = [[sb("LakT%d_%d" % (p, c), [64, NH, 64], PD) for c in range(UC)] for p in range(2)]
            GrbT = [[sb("GrbT%d_%d" % (p, c), [64, NH, 64], PD) for c in range(UC)] for p in range(2)]
            GrkT = [[sb("GrkT%d_%d" % (p, c), [64, NH, 64], PD) for c in range(UC)] for p in range(2)]
            A_ = [[sb("A%d_%d" % (i, c), [64, NH, 64], BF16) for c in range(UC)] for i in range(2)]
            Tb = [sb("Tb_%d" % c, [64, NH, 64], BF16) for c in range(UC)]; TTb = [sb("TTb_%d" % c, [64, NH, 64], BF16) for c in range(UC)]
            L0b = [sb("L0b_%d" % c, [64, NH, 64], BF16) for c in range(UC)]; R2T = [sb("R2T_%d" % c, [64, NH, 64], BF16) for c in range(UC)]
            AT_ = [[sb("AT%d_%d" % (i, c), [64, NH, 64], BF16) for c in range(UC)] for i in range(2)]
            R2f = sb("R2f", [64, NH, 64]); RHS = sb("RHS", [64, NH, 64], D2); U = sb("U", [64, NH, 64], PD)
            ytl = [sb("yt%d" % i, [64, NH, 64]) for i in range(2)]; ysq = sb("ysq", [64, NH, 64]); yn = sb("yn", [64, NH, 64], BF16)
            st8 = sb("st8", [64, NH]); rs8 = sb("rs8", [64, NH]); orT = [sb("orT%d" % i, [128, 4, UT], BF16) for i in range(2)]; o1 = sb("o1", [128, 4, 64])
            pp = [ps("pp%d" % i, [64, NH, 64]) for i in range(4)]
            pq = [ps("pq%d" % i, [64, NH, 64]) for i in range(2)]
            ptrA = ps("ptrA", [64, 2, NH * 64], BF16); ptr = ps("ptr", [128, 256], BF16)
            for t_, op_, cm, st in ((mLT, ALU.is_gt, -1, 1), (mL, ALU.is_gt, 1, -1), (mGT, ALU.is_ge, -1, 1), (idF, ALU.is_equal, 1, -1)):
                SC.op("pool", lambda E, t_=t_: E.memset(t_[:], 1.0), [], bl("masks"))
                SC.op("pool", lambda E, t_=t_, op_=op_, cm=cm, st=st: E.affine_select(out=t_[:], in_=t_[:], pattern=[[0, NH], [st, 64]], compare_op=op_, fill=0.0, base=0, channel_multiplier=cm), bl("masks"), bl("masks"))
            cp("dve", idP[:], idF[:, 0, :], ["masks"], ["idP"])
            ppi = [0]; pqi = [0]; pti = [0]
            def grp(pool, idx, lhs_fn, rhs_fn, R, start=True, stop=True, same=None):
                if same is None:
                    i = idx[0] % len(pool); idx[0] += 1
                else:
                    i = same
                nm = ("pp%d" if pool is pp else "pq%d") % i
                multi = not (start and stop)
                for h in range(NH):
                    mm(pool[i][:, h, :], lhs_fn(h), rhs_fn(h), start and (h == 0 or not multi), stop, R, [nm], skip=multi)
                return pool[i], nm, i

            def A_groups(b, u):
                p = u % 2
                rw = RW[p]; rwn = "RW%d" % p
                usl = slice(u * UT, (u + 1) * UT)
                G = []
                def load():
                    q_ = "pool" if PD == F32 else "sp"
                    dma(rw[:], rw_s[b, :, :, :, usl].rearrange("q h k t -> k q h t"), "rw_s", rwn, queue=q_)
                    dma(ET[p][:], e_s[b, :, :, :, usl].rearrange("q (c hh) k t -> (hh k) q c t", hh=2), "e_s", "ET%d" % p)
                    dma(PC[p][:], pc_s[b, :, :, u * UC:(u + 1) * UC].rearrange("h k c -> k h c"), "pc_s", "PC%d" % p)
                G.append(load)
                X = lambda q, c: (lambda h: rw[:, q, h, c * 64:(c + 1) * 64])
                def tr_stage(i, q):
                    def f(c):
                        tkn = "tok%d_%d" % (p, c)
                        if PD == BF16:
                            k = pti[0] % 2; pti[0] += 1
                            for h in range(NH):
                                SC.op("pe", lambda E, h=h, k=k, c=c: E.transpose(out=ptrA[:, k, h * 64:(h + 1) * 64], in_=rw[:, q, h, c * 64:(c + 1) * 64], identity=idP[:]), bl([rwn, "idP"]), bl(["ptrA"]))
                            cp("act", tok[p][c][:, i, :, :], ptrA[:, k, :].rearrange("p (h d) -> p h d", h=NH), ["ptrA"], [tkn])
                        else:
                            j = ppi[0] % 4; ppi[0] += 1
                            for h in range(NH):
                                SC.op("pe", lambda E, h=h, j=j, c=c: E.transpose(out=pp[j][:, h, :], in_=rw[:, q, h, c * 64:(c + 1) * 64], identity=idP[:]), bl([rwn, "idP"]), bl(["pp%d" % j]))
                            cp("act" if i % 2 else "dve", tok[p][c][:, i, :, :], pp[j][:], ["pp%d" % j], [tkn])
                    return f
                def prod_stage(ql, qr, dst_fn, dn_fn, mask, eng):
                    def f(c):
                        t_, nm, _ = grp(pp, ppi, X(ql, c), X(qr, c), [rwn])
                        tt(eng, dst_fn(c)[:], t_[:], mask[:], ALU.mult, [nm, "masks"], [dn_fn(c)])
                    return f
                stages = [tr_stage(i, q) for i, q in enumerate((0, 4, 5, 6))]
                stages.append(prod_stage(1, 0, lambda c: AT_[0][c], lambda c: "AT0_%d" % c, mLT, "dve"))
                def a0_stage(c):
                    t_, nm, _ = grp(pp, ppi, X(0, c), X(1, c), [rwn])
                    tt("dve", A_[0][c][:], t_[:], mL[:], ALU.mult, [nm, "masks"], ["A0_%d" % c])
                    cp("act", L0b[c][:], A_[0][c][:], ["A0_%d" % c], ["L0b_%d" % c])
                stages.append(a0_stage)
                stages.append(prod_stage(2, 0, lambda c: LakT[p][c], lambda c: "LakT%d_%d" % (p, c), mLT, "dve"))
                stages.append(prod_stage(1, 3, lambda c: GrbT[p][c], lambda c: "GrbT%d_%d" % (p, c), mGT, "dve"))
                stages.append(prod_stage(2, 3, lambda c: GrkT[p][c], lambda c: "GrkT%d_%d" % (p, c), mGT, "dve"))
                def st_init(c):
                    tt("dve", TTb[c][:], AT_[0][c][:], idF[:], ALU.add, ["AT0_%d" % c, "masks"], ["TTb_%d" % c])
                stages.append(st_init)
                for r in range(6):
                    cur = r % 2; nxt = 1 - cur
                    def sq1(c, cur=cur, nxt=nxt):
                        t_, nm, _ = grp(pp, ppi, lambda h: A_[cur][c][:, h, :], lambda h: AT_[cur][c][:, h, :], ["A%d_%d" % (cur, c), "AT%d_%d" % (cur, c)])
                        cp("act", AT_[nxt][c][:], t_[:], [nm], ["AT%d_%d" % (nxt, c)])
                    def sq2(c, cur=cur, nxt=nxt):
                        t_, nm, _ = grp(pp, ppi, lambda h: AT_[cur][c][:, h, :], lambda h: A_[cur][c][:, h, :], ["A%d_%d" % (cur, c), "AT%d_%d" % (cur, c)])
                        cp("act", A_[nxt][c][:], t_[:], [nm], ["A%d_%d" % (nxt, c)])
                    def supd(c, cur=cur):
                        t_, nm, _ = grp(pp, ppi, lambda h: A_[cur][c][:, h, :], lambda h: TTb[c][:, h, :], ["A%d_%d" % (cur, c), "TTb_%d" % c])
                        tt("dve", TTb[c][:], TTb[c][:], t_[:], ALU.add, ["TTb_%d" % c, nm], ["TTb_%d" % c])
                    def supd2(c, cur=cur):
                        t_, nm, _ = grp(pp, ppi, lambda h: AT_[cur][c][:, h, :], lambda h: Tb[c][:, h, :], ["AT%d_%d" % (cur, c), "Tb_%d" % c])
                        tt("dve", Tb[c][:], Tb[c][:], t_[:], ALU.add, ["Tb_%d" % c, nm], ["Tb_%d" % c])
                    if r >= 1:
                        stages.append(supd)
                    if r <= 4:
                        stages.append(sq1); stages.append(sq2)
                def nwT(c):
                    k = pti[0] % 2; pti[0] += 1
                    for h in range(NH):
                        SC.op("pe", lambda E, h=h, k=k, c=c: E.transpose(out=ptrA[:, k, h * 64:(h + 1) * 64], in_=TTb[c][:, h, :], identity=idP[:]), bl(["TTb_%d" % c, "idP"]), bl(["ptrA"]))
                    cp("act", Tb[c][:], ptrA[:, k, :].rearrange("p (h d) -> p h d", h=NH), ["ptrA"], ["Tb_%d" % c])
                def nw1(c):
                    t_, nm, _ = grp(pp, ppi, lambda h: L0b[c][:, h, :], lambda h: TTb[c][:, h, :], ["L0b_%d" % c, "TTb_%d" % c])
                    stt(R2f[:], TTb[c][:], -1.0, t_[:], ALU.mult, ALU.add, ["TTb_%d" % c, nm], ["R2f"])
                    tt("dve", R2T[c][:], R2f[:], idF[:], ALU.add, ["R2f", "masks"], ["R2T_%d" % c])
                def nw2(c):
                    t_, nm, _ = grp(pp, ppi, lambda h: Tb[c][:, h, :], lambda h: R2T[c][:, h, :], ["Tb_%d" % c, "R2T_%d" % c])
                    tt("dve", TTt[p][c][:], TTb[c][:], t_[:], ALU.add, ["TTb_%d" % c, nm], ["TT%d_%d" % (p, c)])
                stages += [nwT, nw1, nw2]
                for stg in stages:
                    for c in range(UC):
                        G.append(lambda stg=stg, c=c: stg(c))
                return G

            def B_steps(b, u):
                p = u % 2
                rw = RW[p]; rwn = "RW%d" % p
                usl = slice(u * UT, (u + 1) * UT)
                Bs = []
                pend = []
                for c in range(UC):
                    sl = slice(c * 64, (c + 1) * 64)
                    X = lambda q, sl=sl: (lambda h: rw[:, q, h, sl])
                    tk = tok[p][c]; tkn = "tok%d_%d" % (p, c)
                    TK = lambda i, tk=tk: (lambda h: tk[:, i, h, :])
                    Hn = ["Hhi", "Hlo"] if PD == BF16 else ["Hf"]
                    def s1(c=c, X=X, TK=TK, tkn=tkn):
                        if PD == BF16:
                            t_, nm, i = grp(pq, pqi, X(0), lambda h: Hhi[:, h, :], [rwn, "Hhi"], True, False)
                            grp(pq, pqi, X(0), lambda h: Hlo[:, h, :], [rwn, "Hlo"], False, False, same=i)
                        else:
                            t_, nm, i = grp(pq, pqi, X(0), lambda h: Hf[:, h, :], [rwn, "Hf"], True, False)
                        grp(pq, pqi, lambda h: LakT[p][c][:, h, :], TK(3), ["LakT%d_%d" % (p, c), tkn], False, True, same=i)
                        cp("act", RHS[:], t_[:], [nm], ["RHS"])
                    def s2(c=c):
                        t_, nm, i = grp(pq, pqi, lambda h: TTt[p][c][:, h, :], lambda h: RHS[:, h, :], ["TT%d_%d" % (p, c), "RHS"])
                        cp("act", U[:], t_[:], [nm], ["U"])
                    def s3(c=c, X=X, TK=TK, tkn=tkn):
                        t_, nm, i = grp(pq, pqi, TK(1), lambda h: U[:, h, :], [tkn, "U"], True, False)
                        grp(pq, pqi, TK(2), TK(3), [tkn], False, True, same=i)
                        tt("dve", Hf[:], Hf[:], PC[p][:, :, c:c + 1].broadcast_to([64, NH, 64]), ALU.mult, ["Hf", "PC%d" % p], ["Hf"])
                        tt("dve", Hf[:], Hf[:], t_[:], ALU.add, ["Hf", nm], ["Hf"])
                        if PD == BF16:
                            cp("dve", Hhi[:], Hf[:], ["Hf"], ["Hhi"])
                            tt("dve", Hlo[:], Hf[:], Hhi[:], ALU.subtract, ["Hf", "Hhi"], ["Hlo"])
                    def s3y(c=c, X=X, TK=TK, tkn=tkn):
                        if PD == BF16:
                            t_, nm, i = grp(pq, pqi, X(3), lambda h: Hhi[:, h, :], [rwn, "Hhi"], True, False)
                            grp(pq, pqi, X(3), lambda h: Hlo[:, h, :], [rwn, "Hlo"], False, False, same=i)
                        else:
                            t_, nm, i = grp(pq, pqi, X(3), lambda h: Hf[:, h, :], [rwn, "Hf"], True, False)
                        grp(pq, pqi, lambda h: GrbT[p][c][:, h, :], lambda h: U[:, h, :], ["GrbT%d_%d" % (p, c), "U"], False, False, same=i)
                        grp(pq, pqi, lambda h: GrkT[p][c][:, h, :], TK(3), ["GrkT%d_%d" % (p, c), tkn], False, True, same=i)
                        cp("act", ytl[c % 2][:], t_[:], [nm], ["yt%d" % (c % 2)])
                    def s4(c=c, sl=sl):
                        yt_ = ytl[c % 2]; ytn = "yt%d" % (c % 2)
                        bc = lambda t: t[:, :].unsqueeze(2).broadcast_to([64, NH, 64])
                        SC.op("dve", lambda E: E.tensor_reduce(out=st8[:], in_=yt_[:], axis=AX.X, op=ALU.add), bl([ytn]), bl("st8"))
                        ts("dve", st8[:], st8[:], 1.0 / 64, None, ALU.mult, None, ["st8"], ["st8"])
                        tt("dve", yt_[:], yt_[:], bc(st8), ALU.subtract, [ytn, "st8"], [ytn])
                        act(ysq[:], yt_[:], AF.Square, [ytn], ["ysq"])
                        SC.op("dve", lambda E: E.tensor_reduce(out=rs8[:], in_=ysq[:], axis=AX.X, op=ALU.add), bl("ysq"), bl("rs8"))
                        act(rs8[:], rs8[:], AF.Sqrt, ["rs8", "cst"], ["rs8"], bias=GNEPS[0:64, :], scale=1.0 / 64)
                        SC.op("dve", lambda E: E.reciprocal(out=rs8[:], in_=rs8[:]), bl("rs8"), bl("rs8"))
                        tt("dve", yn[:], yt_[:], bc(rs8), ALU.mult, [ytn, "rs8"], ["yn"])
                        for c4 in range(4):
                            SC.op("pe", lambda E, c4=c4: E.transpose(out=ptr[:, c4 * 64:(c4 + 1) * 64], in_=yn[:, 2 * c4:2 * c4 + 2, :].rearrange("p h d -> p (h d)"), identity=ident[0:64, 0:64]), bl("yn ident"), bl("ptr"))
                        tt("dve", o1[:], ptr[:, :].rearrange("p (c t) -> p c t", c=4), ET[p][:, 1, :, sl], ALU.mult, ["ptr", "ET%d" % p], ["o1"])
                        tt("dve", orT[p][:, :, sl], o1[:], ET[p][:, 0, :, sl], ALU.add, ["o1", "ET%d" % p], ["orT%d" % p])
                    if pend:
                        Bs += [s1, s2, pend.pop(), s3y, s3]
                    else:
                        Bs += [s1, s2, s3y, s3]
                    pend.append(s4)
                Bs.append(pend.pop())
                def store():
                    dma(orw_s[b, :, :, usl].rearrange("c p t -> p c t"), orT[p][:], "orT%d" % p, "orw_s")
                Bs.append(store)
                return Bs

            for b in range(NB):
                mset("pool", Hf[:], 0.0, "Hf")
                if PD == BF16:
                    mset("pool", Hhi[:], 0.0, "Hhi"); mset("pool", Hlo[:], 0.0, "Hlo")
                for g in A_groups(b, 0):
                    g()
                for u in range(NU):
                    Bl = B_steps(b, u)
                    Al = A_groups(b, u + 1) if u + 1 < NU else []
                    na, nb_ = len(Al), len(Bl)
                    ia = 0
                    for ib, st in enumerate(Bl):
                        st()
                        tgt = (ib + 1) * na // nb_
                        while ia < tgt:
                            Al[ia](); ia += 1
                    while ia < na:
                        Al[ia](); ia += 1
        SC.barrier()
        with ExitStack() as ph:
            sb = lambda n, shp, dt=F32: ph.enter_context(nc.sbuf_tensor(n, shp, dt))
            ps = lambda n, shp, dt=F32: ph.enter_context(nc.psum_tensor(n, shp, dt))
            foxo = sb("foxo", [128, 4, D], BF16); rwo = sb("rwo", [128, 4, D], BF16); wo = sb("wo", [128, 8, D], BF16)
            vw = lambda d: d.rearrange("(kc p) n -> p kc n", p=128)
            dma(foxo[:], vw(foxo_d), "", "foxo", queue="pool"); dma(rwo[:], vw(rwo_d), "", "rwo", queue="pool"); dma(wo[:], vw(wo_d), "", "wo", queue="pool")
            ofTl = [sb("ofT4_%d" % i, [128, 4, TT], BF16) for i in range(2)]; orTl = [sb("orT4_%d" % i, [128, 4, TT], BF16) for i in range(2)]; gtsl = [sb("gts_%d" % i, [128, 16, TT], BF16) for i in range(2)]
            xtl = [sb("xt4_%d" % i, [128, 4, D]) for i in range(2)]; g1b = sb("g1b", [128, D]); mg = sb("mg", [128, 8, TT], BF16)
            m1l = [sb("m1_%d" % i, [128, TT]) for i in range(2)]; m2l = [sb("m2_%d" % i, [128, TT]) for i in range(2)]
            pF = [ps("pF%d" % i, [128, TT]) for i in range(2)]; pR = [ps("pR%d" % i, [128, TT]) for i in range(2)]; pM = [ps("pM%d" % i, [128, TT]) for i in range(2)]
            def loads(gt):
                i = gt % 2
                b_, tl_ = gt // NTL, gt % NTL
                t0_ = b_ * S + tl_ * TT
                tsl_ = slice(tl_ * TT, (tl_ + 1) * TT)
                dma(ofTl[i][:], ofox_s[b_, :, :, tsl_].rearrange("c p t -> p c t"), "ofox_s", "ofT4_%d" % i)
                dma(orTl[i][:], orw_s[b_, :, :, tsl_].rearrange("c p t -> p c t"), "orw_s", "orT4_%d" % i)
                dma(gtsl[i][:], gate_s[b_, :, :, tsl_].rearrange("g p t -> p g t"), "gate_s", "gts_%d" % i)
                dma(xtl[i][:], x_d[t0_:t0_ + TT, :].rearrange("(k p) d -> p k d", p=128), "", "xt4_%d" % i)
            loads(0)
            for b in range(NB):
                dma(g1b[:], gates_s[b, 0, :].partition_broadcast(128), "gates_s", "g1b")
                for tl in range(NTL):
                    t0 = b * S + tl * TT
                    gt = b * NTL + tl
                    i4 = gt % 2
                    ofT = ofTl[i4]; orT = orTl[i4]; gts = gtsl[i4]; xt = xtl[i4]
                    ofn, orn, gtn, xtn = "ofT4_%d" % i4, "orT4_%d" % i4, "gts_%d" % i4, "xt4_%d" % i4
                    if gt + 1 < NB * NTL:
                        loads(gt + 1)
                    for oc in range(8):
                        f_ = pF[oc % 2]; fn = "pF%d" % (oc % 2); r_ = pR[oc % 2]; rn = "pR%d" % (oc % 2)
                        for kc in range(4):
                            mm(f_[:, :], foxo[:, kc, oc * 128:(oc + 1) * 128], ofT[:, kc, :], kc == 0, kc == 3, ["foxo", ofn], [fn])
                        for kc in range(4):
                            mm(r_[:, :], rwo[:, kc, oc * 128:(oc + 1) * 128], orT[:, kc, :], kc == 0, kc == 3, ["rwo", orn], [rn])
                        m1 = m1l[oc % 2]; m2 = m2l[oc % 2]; m1n = "m1_%d" % (oc % 2); m2n = "m2_%d" % (oc % 2)
                        tt("dve", m1[:], f_[:, :], gts[:, oc, :], ALU.mult, [fn, gtn], [m1n])
                        tt("dve", m2[:], r_[:, :], gts[:, 8 + oc, :], ALU.mult, [rn, gtn], [m2n])
                        tt("pool", mg[:, oc, :], m1[:], m2[:], ALU.add, [m1n, m2n], ["mg"])
                    for blk in range(4):
                        for hf in range(2):
                            i = (blk * 2 + hf) % 2
                            for kc in range(8):
                                mm(pM[i][:, :], mg[:, kc, blk * 128:(blk + 1) * 128], wo[:, kc, hf * 512:(hf + 1) * 512], kc == 0, kc == 7, ["mg", "wo"], ["pM%d" % i])
                            m1 = m1l[i]; m1n = "m1_%d" % i
                            tt("dve", m1[:], pM[i][:, :], g1b[:, hf * 512:(hf + 1) * 512], ALU.mult, ["pM%d" % i, "g1b"], [m1n])
                            tt("pool", xt[:, blk, hf * 512:(hf + 1) * 512], xt[:, blk, hf * 512:(hf + 1) * 512], m1[:], ALU.add, [xtn, m1n], [xtn])
                    dma(h_s[t0:t0 + TT, :].rearrange("(k p) d -> p k d", p=128), xt[:], xtn, "h_s")

        SC.barrier()
        with ExitStack() as ph:
            sb = lambda n, shp, dt=F32: ph.enter_context(nc.sbuf_tensor(n, shp, dt))
            ps = lambda n, shp, dt=F32: ph.enter_context(nc.psum_tensor(n, shp, dt))
            T2 = 256
            wup = sb("wup", [128, 8, 2 * DFF], BF16); wdn = sb("wdn", [128, 22, D], BF16)
            vw = lambda d: d.rearrange("(kc p) n -> p kc n", p=128)
            dma(wup[:], vw(wup_d), "", "wup", queue="pool"); dma(wdn[:], vw(wdn_d), "", "wdn", queue="pool")
            cw = sb("cw", [128, 22, 3]); cb = sb("cb", [128, 22]); g2b_ = sb("g2bc", [128, D]); fgb = sb("fgb", [128, D])
            dma(cw[:], convw_d, "", "cw"); dma(cb[:], convb_d, "", "cb"); dma(fgb[:], fing_d.partition_broadcast(128), "", "fgb")
            htl = [sb("ht%d" % i, [128, 2, D]) for i in range(2)]; hx = sb("hx", [128, 2, D], BF16); u2l = [sb("u2T%d" % i, [128, 8, T2], BF16) for i in range(2)]; ssq3 = sb("ssq3", [128, 2]); rstd3 = sb("rstd3", [128, 2])
            ssq = sb("ssq2", [128, 2]); rstd = sb("rstd2", [128, 2]); acar = sb("acar", [128, 22, 2])
            araw_ = [sb("araw%d" % i, [128, T2 + 2]) for i in range(2)]; c1_ = [sb("c1%d" % i, [128, T2]) for i in range(2)]; gT = sb("gT", [128, 22, T2], BF16)
            m1bl = [sb("m1b_%d" % i, [128, 512]) for i in range(2)]; junk = sb("junk", [128, D], BF16); junk2 = sb("junk2", [128, D], BF16)
            pT = [ps("qT%d" % i, [128, D], BF16) for i in range(2)]
            pa = [ps("pa%d" % i, [128, T2]) for i in range(2)]; pb = [ps("pb%d" % i, [128, T2]) for i in range(2)]
            pD = [ps("pD%d" % i, [128, 512]) for i in range(2)]
            def hload(b, tl):
                i = tl % 2
                t0 = b * S + tl * T2
                dma(htl[i][:], h_s[t0:t0 + T2, :].rearrange("(k p) d -> p k d", p=128), "h_s", "ht%d" % i)
            def prologue(b, tl):
                i = tl % 2
                norm_T(htl[i], "ht%d" % i, 2, G2, modc[:, 3], b, u2l[i], "u2T%d" % i, junk[:], "junk", ssq, rstd, hx, pT, "b_")
            def main(b, tl):
                i = tl % 2
                u2T = u2l[i]; un = "u2T%d" % i
                for fc in range(22):
                    a_ = pa[fc % 2]; an = "pa%d" % (fc % 2); b_ = pb[fc % 2]; bn = "pb%d" % (fc % 2)
                    araw = araw_[fc % 2]; arn = "araw%d" % (fc % 2); c1 = c1_[fc % 2]; c1n = "c1%d" % (fc % 2)
                    for kc in range(8):
                        mm(a_[:, :], wup[:, kc, fc * 128:(fc + 1) * 128], u2T[:, kc, :], kc == 0, kc == 7, ["wup", un], [an])
                    for kc in range(8):
                        mm(b_[:, :], wup[:, kc, DFF + fc * 128:DFF + (fc + 1) * 128], u2T[:, kc, :], kc == 0, kc == 7, ["wup", un], [bn])
                    cp("act", araw[:, 2:T2 + 2], a_[:, :], [an], [arn])
                    cp("pool", araw[:, 0:2], acar[:, fc, :], ["acar"], [arn])
                    ts("dve", c1[:], araw[:, 2:T2 + 2], cw[:, fc, 2:3], cb[:, fc:fc + 1], ALU.mult, ALU.add, [arn, "cw", "cb"], [c1n])
                    stt(c1[:], araw[:, 1:T2 + 1], cw[:, fc, 1:2], c1[:], ALU.mult, ALU.add, [arn, "cw", c1n], [c1n])
                    stt(c1[:], araw[:, 0:T2], cw[:, fc, 0:1], c1[:], ALU.mult, ALU.add, [arn, "cw", c1n], [c1n])
                    cp("pool", acar[:, fc, :], araw[:, T2:T2 + 2], [arn], ["acar"])
                    act(c1[:], c1[:], AF.Silu, [c1n], [c1n])
                    tt("dve", gT[:, fc, :], c1[:], b_[:, :], ALU.mult, [c1n, bn], ["gT"])
            def epilogue(b, tl):
                i = tl % 2
                ht = htl[i]; hn = "ht%d" % i
                t0 = b * S + tl * T2
                for blk in range(2):
                    for hf in range(2):
                        for fc in range(22):
                            mm(pD[hf][:, :], gT[:, fc, blk * 128:(blk + 1) * 128], wdn[:, fc, hf * 512:(hf + 1) * 512], fc == 0, fc == 21, ["gT", "wdn"], ["pD%d" % hf])
                        m1 = m1bl[hf]; m1n = "m1b_%d" % hf
                        tt("dve", m1[:], pD[hf][:, :], g2b_[:, hf * 512:(hf + 1) * 512], ALU.mult, ["pD%d" % hf, "g2bc"], [m1n])
                        tt("pool", ht[:, blk, hf * 512:(hf + 1) * 512], ht[:, blk, hf * 512:(hf + 1) * 512], m1[:], ALU.add, [hn, m1n], [hn])
                for blk in range(2):
                    act(junk2[:], ht[:, blk, :], AF.Square, [hn], ["junk2", "ssq3"], accum=ssq3[:, blk:blk + 1])
                act(rstd3[:], ssq3[:], AF.Sqrt, ["ssq3", "cst"], ["rstd3"], bias=EPS, scale=1.0 / D)
                SC.op("dve", lambda E: E.reciprocal(out=rstd3[:], in_=rstd3[:]), bl("rstd3"), bl("rstd3"))
                for blk in range(2):
                    stt(ht[:, blk, :], ht[:, blk, :], rstd3[:, blk:blk + 1], fgb[:], ALU.mult, ALU.mult, [hn, "rstd3", "fgb"], [hn])
                dma(out_d[t0:t0 + T2, :].rearrange("(k p) d -> p k d", p=128), ht[:], hn, "out")
            NT2 = S // T2
            for b in range(NB):
                dma(g2b_[:], gates_s[b, 1, :].partition_broadcast(128), "gates_s", "g2bc")
                mset("pool", acar[:], 0.0, "acar")
                hload(b, 0)
                prologue(b, 0)
                for tl in range(NT2):
                    if tl + 1 < NT2:
                        hload(b, tl + 1)
                    main(b, tl)
                    if tl + 1 < NT2:
                        prologue(b, tl + 1)
                    epilogue(b, tl)
        SC.final_wait(bl("out"))
        if debug:
            SC.final_wait(list(B.values()))
        SC.emit()
    return nc


def host_inputs(inp, b0, NB):
    f = lambda a: np.ascontiguousarray(np.asarray(a, dtype=np.float32))
    S = inp["x"].shape[1]
    w_in = np.asarray(inp["w_in"])[0]
    rw = w_in[:, 1544:3336]
    z64 = np.zeros((D, 64), np.float32)
    w_rw = np.concatenate([rw[:, 0:1600], z64, rw[:, 1600:1664], z64, rw[:, 1664:1792]], axis=1)
    mu = np.asarray(inp["rwkv_mu"])[0]
    zz = np.zeros((64,), np.float32)
    mu_p = np.concatenate([mu[0:1600], zz, mu[1600:1664], zz, mu[1664:1792]])
    col = lambda v, n: f(np.asarray(v).reshape(n, 128).T)
    bm = np.asarray(inp["b_mod"])[0]
    par = [inp["rwkv_w0"], inp["rwkv_a0"], inp["rwkv_k_k"], inp["rwkv_k_a"], inp["rwkv_r_k"], inp["rwkv_ln_w"], inp["rwkv_ln_b"]]
    rw_par = np.stack([np.asarray(p)[0].reshape(4, 128).T for p in par], axis=1)
    return {
        "x": f(np.asarray(inp["x"])[b0:b0 + NB].reshape(NB * S, D)),
        "cT": f(np.asarray(inp["c"])[b0:b0 + NB].reshape(NB, 8, 128).transpose(2, 1, 0)),
        "w_mod": f(np.asarray(inp["w_mod"])[0]),
        "bmod_col": col(bm, 48),
        "bmod_g": f(np.broadcast_to(np.stack([bm[2048:3072], bm[5120:6144]])[None], (NB, 2, D))),
        "ln1_col": col(np.asarray(inp["ln1_g"])[0], 8), "ln2_col": col(np.asarray(inp["ln2_g"])[0], 8),
        "fin_g": f(inp["final_g"]),
        "w_qkv": f(w_in[:, 0:1536]), "w_f": f(w_in[:, 1536:1544]), "w_rw": f(w_rw), "w_gate": f(w_in[:, 3336:5384]),
        "fox_bf": f(np.asarray(inp["fox_b_f"])[0].reshape(8, 1)), "mu_col": col(mu_p, NRW),
        "rw_par": f(rw_par),
        "w2": f(np.asarray(inp["rwkv_w2"])[0]), "a2": f(np.asarray(inp["rwkv_a2"])[0]), "g2": f(np.asarray(inp["rwkv_g2"])[0]),
        "fox_w_out": f(np.asarray(inp["fox_w_out"])[0]), "rwkv_w_out": f(np.asarray(inp["rwkv_w_out"])[0]), "w_o": f(np.asarray(inp["w_o"])[0]),
        "w_up": f(np.asarray(inp["w_up"])[0]), "w_down": f(np.asarray(inp["w_down"])[0]),
        "conv_col": f(np.asarray(inp["conv_w"])[0].reshape(3, 22, 128).transpose(2, 1, 0)),
        "convb_col": col(np.asarray(inp["conv_b"])[0], 22),
    }


_NC_CACHE = {}


def kernel(**inputs):
    x = np.asarray(inputs["x"])
    Bt, S, _ = x.shape
    ncores = 8
    NB = Bt // ncores
    key = (NB, S)
    if key not in _NC_CACHE:
        _NC_CACHE[key] = build(NB, S)
    nc = _NC_CACHE[key]
    in_maps = [host_inputs(inputs, i * NB, NB) for i in range(ncores)]
    res = run_bass_kernel_spmd(nc, in_maps, core_ids=list(range(ncores)))
    out = np.concatenate([np.asarray(r["out"]).reshape(NB, S, D) for r in res.results], axis=0)
    return out.astype(np.float32)
```

```python
import numpy as np
import concourse.bass as bass
import concourse.mybir as mybir
from concourse.bass_utils import run_bass_kernel_spmd
from contextlib import ExitStack

F32 = mybir.dt.float32
BF16 = mybir.dt.bfloat16
AF = mybir.ActivationFunctionType
ALU = mybir.AluOpType
AX = mybir.AxisListType


class Buf:
    __slots__ = ("name", "w", "r")

    def __init__(self, name):
        self.name = name
        self.w = None
        self.r = []


class Sched:
    ENGS = ("pe", "act", "dve", "pool", "sp")

    def __init__(self, nc, stack, n_dma_sems=32):
        self.nc = nc
        self.q = {e: [] for e in self.ENGS}
        self.sems = {}
        self.val = {}
        for e in ("pe", "act", "dve", "pool"):
            self.sems[e] = stack.enter_context(nc.semaphore("s_" + e))
            self.val[e] = 0
        self.free_dma = []
        for i in range(n_dma_sems):
            k = "d%d" % i
            self.sems[k] = stack.enter_context(nc.semaphore("s_" + k))
            self.val[k] = 0
            self.free_dma.append(k)
        self.dma_rr = 0
        self.sw_dma = []
        for i in range(8):
            k = "e%d" % i
            self.sems[k] = stack.enter_context(nc.semaphore("s_" + k))
            self.val[k] = 0
            self.sw_dma.append(k)
        self.sw_rr = 0
        self.waited = {e: {} for e in self.ENGS}
        self.n_wait = 0
        self.n_op = 0

    def _need(self, eng, reads, writes):
        need = {}

        def add(tok):
            if tok is None:
                return
            k, v, e = tok
            if e == eng and eng == "pe":
                return
            if need.get(k, 0) < v:
                need[k] = v
        for b in reads:
            add(b.w)
        for b in writes:
            add(b.w)
            for t in b.r:
                add(t)
        out = []
        for k, v in need.items():
            if self.waited[eng].get(k, 0) < v:
                self.waited[eng][k] = v
                out.append((k, v))
        return out

    def _emit_waits(self, eng, waits):
        for k, v in waits:
            sem = self.sems[k]
            self.q[eng].append(lambda E, sem=sem, v=v: E.wait_ge(sem, v))
            self.n_wait += 1

    def op(self, eng, fn, reads=(), writes=()):
        waits = self._need(eng, reads, writes)
        self._emit_waits(eng, waits)
        self.val[eng] += 1
        v = self.val[eng]
        sem = self.sems[eng]
        self.q[eng].append(lambda E, fn=fn, sem=sem: fn(E).then_inc(sem, 1))
        tok = (eng, v, eng)
        for b in reads:
            b.r.append(tok)
            if len(b.r) > 24:
                b.r = b.r[-24:] if False else self._compact(b.r)
        for b in writes:
            b.w = tok
            b.r = []
        self.n_op += 1

    @staticmethod
    def _compact(toks):
        best = {}
        for k, v, e in toks:
            if k not in best or best[k][1] < v:
                best[k] = (k, v, e)
        return list(best.values())

    def dma(self, fn, reads=(), writes=(), queue="sp"):
        waits = self._need(queue, reads, writes)
        self._emit_waits(queue, waits)
        if queue == "pool":
            k = self.sw_dma[self.sw_rr % len(self.sw_dma)]
            self.sw_rr += 1
        else:
            k = self.free_dma[self.dma_rr % len(self.free_dma)]
            self.dma_rr += 1
        if self.val[k] > 0 and self.waited[queue].get(k, 0) < self.val[k]:
            self.waited[queue][k] = self.val[k]
            self._emit_waits(queue, [(k, self.val[k])])
        self.val[k] += 16
        v = self.val[k]
        sem = self.sems[k]
        self.q[queue].append(lambda E, fn=fn, sem=sem: fn(E).then_inc(sem, 16))
        tok = (k, v, "dma")
        for b in reads:
            b.r.append(tok)
        for b in writes:
            b.w = tok
            b.r = []
        self.n_op += 1

    def barrier(self):
        for eng in self.ENGS:
            waits = []
            for k, v in self.val.items():
                if v > 0 and self.waited[eng].get(k, 0) < v and not (k == eng):
                    self.waited[eng][k] = v
                    waits.append((k, v))
            self._emit_waits(eng, waits)

    def final_wait(self, bufs, eng="sp"):
        waits = self._need(eng, bufs, bufs)
        self._emit_waits(eng, waits)

    def emit(self):
        nc = self.nc
        with nc.Block() as block:
            @block.sync
            def _(E):
                for f in self.q["sp"]:
                    f(E)

            @block.tensor
            def _(E):
                for f in self.q["pe"]:
                    f(E)

            @block.scalar
            def _(E):
                for f in self.q["act"]:
                    f(E)

            @block.vector
            def _(E):
                for f in self.q["dve"]:
                    f(E)

            @block.gpsimd
            def _(E):
                for f in self.q["pool"]:
                    f(E)
D = 1024; NH = 8; DH = 64; DFF = 2816; NRW = 15
TT = 512
P3DT = BF16
P3D2 = BF16


def build(NB, S, debug=False):
    nc = bass.Bass("TRN2", target_bir_lowering=False)
    TOK = NB * S
    NTL = S // TT
    dk = "ExternalOutput" if debug else "Internal"
    def din(name, shape, dt=F32):
        return nc.dram_tensor(name, list(shape), dt, kind="ExternalInput").ap()
    def dsc(name, shape, dt=BF16):
        return nc.dram_tensor(name, list(shape), dt, kind=dk).ap()
    x_d = din("x", [TOK, D]); cT_d = din("cT", [128, 8, NB]); wmod_d = din("w_mod", [D, 6 * D])
    bmodc_d = din("bmod_col", [128, 48]); bmodg_d = din("bmod_g", [NB, 2, D])
    ln1_d = din("ln1_col", [128, 8]); ln2_d = din("ln2_col", [128, 8]); fing_d = din("fin_g", [D])
    wqkv_d = din("w_qkv", [D, 1536]); wf_d = din("w_f", [D, 8]); wrw_d = din("w_rw", [D, NRW * 128]); wgate_d = din("w_gate", [D, 2048])
    bf_d = din("fox_bf", [8, 1]); mu_d = din("mu_col", [128, NRW])
    rwp_d = din("rw_par", [128, 7, 4])
    w2_d = din("w2", [64, 512]); a2_d = din("a2", [64, 512]); g2_d = din("g2", [128, 512])
    foxo_d = din("fox_w_out", [512, D]); rwo_d = din("rwkv_w_out", [512, D]); wo_d = din("w_o", [D, D])
    wup_d = din("w_up", [D, 2 * DFF]); wdn_d = din("w_down", [DFF, D])
    convw_d = din("conv_col", [128, 22, 3]); convb_d = din("convb_col", [128, 22])
    out_d = nc.dram_tensor("out", [TOK, D], F32, kind="ExternalOutput").ap()
    q_s = dsc("q_s", [NB, NH, 68, S]); k_s = dsc("k_s", [NB, NH, 68, S]); v_s = dsc("v_s", [NB, S, NH * 65])
    gate_s = dsc("gate_s", [NB, 16, 128, S]); rw_s = dsc("rw_s", [NB, 7, NH, 64, S]); e_s = dsc("e_s", [NB, 2, NH, 64, S])
    pc_s = dsc("pc_s", [NB, NH, 64, S // 64], F32); gates_s = dsc("gates_s", [NB, 2, D], F32)
    ofox_s = dsc("ofox_s", [NB, 4, 128, S]); orw_s = dsc("orw_s", [NB, 4, 128, S]); h_s = dsc("h_s", [TOK, D], F32)

    with ExitStack() as top:
        SC = Sched(nc, top)
        B = {}
        def buf(n):
            if n not in B:
                B[n] = Buf(n)
            return B[n]
        def bl(names):
            return [buf(n) for n in names.split()] if isinstance(names, str) else [buf(n) for n in names]
        from collections import deque
        jobs = deque(); jstate = {"in": False, "n": 0, "K": 14}
        _op, _dma = SC.op, SC.dma
        def tick():
            if jstate["in"] or not jobs:
                return
            jstate["n"] += 1
            if jstate["n"] % jstate["K"] == 0:
                jstate["in"] = True
                jobs.popleft()()
                jstate["in"] = False
        def flush_jobs():
            jstate["in"] = True
            while jobs:
                jobs.popleft()()
            jstate["in"] = False
        def op_t(*a, **k):
            _op(*a, **k); tick()
        def dma_t(*a, **k):
            _dma(*a, **k); tick()
        SC.op = op_t; SC.dma = dma_t
        def mm(out, lhsT, rhs, start, stop, R, W, skip=False):
            _op("pe", lambda E: E.matmul(out, lhsT=lhsT, rhs=rhs, start=start, stop=stop, skip_group_check=skip), bl(R), bl(W))
        def act(out, in_, func, R, W, bias=None, scale=None, accum=None):
            kw = {}
            if bias is not None: kw["bias"] = bias
            if scale is not None: kw["scale"] = scale
            if accum is not None: kw["accum_out"] = accum
            SC.op("act", lambda E: E.activation(out=out, in_=in_, func=func, **kw), bl(R), bl(W))
        def tt(eng, out, in0, in1, op, R, W):
            SC.op(eng, lambda E: E.tensor_tensor(out=out, in0=in0, in1=in1, op=op), bl(R), bl(W))
        def ts(eng, out, in0, s1, s2, op0, op1, R, W):
            if op1 is None:
                SC.op(eng, lambda E: E.tensor_scalar(out=out, in0=in0, scalar1=s1, scalar2=None, op0=op0), bl(R), bl(W))
            else:
                SC.op(eng, lambda E: E.tensor_scalar(out=out, in0=in0, scalar1=s1, scalar2=s2, op0=op0, op1=op1), bl(R), bl(W))
        def stt(out, in0, sc, in1, op0, op1, R, W):
            SC.op("dve", lambda E: E.scalar_tensor_tensor(out=out, in0=in0, scalar=sc, in1=in1, op0=op0, op1=op1), bl(R), bl(W))
        def cp(eng, out, in_, R, W):
            if eng == "act":
                SC.op("act", lambda E: E.activation(out=out, in_=in_, func=AF.Identity), bl(R), bl(W))
            else:
                SC.op(eng, lambda E: E.tensor_copy(out=out, in_=in_), bl(R), bl(W))
        def mset(eng, ap, val, W):
            SC.op(eng, lambda E: E.memset(ap, val), [], bl(W))
        def dma(out, in_, R, W, queue="sp"):
            SC.dma(lambda E: E.dma_start(out=out, in_=in_), bl(R), bl(W), queue=queue)

        gsb = lambda n, shp, dt=F32: top.enter_context(nc.sbuf_tensor(n, shp, dt))
        ident = gsb("ident", [128, 128], BF16)
        cst = gsb("cst", [128, 8])
        modc = gsb("modc", [128, 6, 8, NB])
        G1 = gsb("G1", [128, 8, NB]); G2 = gsb("G2", [128, 8, NB])
        mset("pool", ident[:], 1.0, "ident")
        SC.op("pool", lambda E: E.affine_select(out=ident[:], in_=ident[:], pattern=[[-1, 128]], compare_op=ALU.is_equal, fill=0.0, base=0, channel_multiplier=1), bl("ident"), bl("ident"))
        for i, v in enumerate([1e-6, 1.0, -0.5, 0.0, 64e-5, 1e-24]):
            mset("pool", cst[:, i:i + 1], v, "cst")
        EPS, ONE, MHALF, ZERO, GNEPS = (cst[:, i:i + 1] for i in range(5))

        pw = ExitStack()
        pwsb = lambda n, shp, dt=F32: pw.enter_context(nc.sbuf_tensor(n, shp, dt))
        wqkv = pwsb("wqkv", [128, 8, 1536], BF16); wfb = pwsb("wfb", [128, 8, 8], BF16)
        wrw = pwsb("wrw", [128, 8, NRW * 128], BF16); wgate = pwsb("wgate", [128, 8, 2048], BF16)
        w2b = pwsb("w2b", [64, 512], BF16); a2b = pwsb("a2b", [64, 512], BF16); g2b = pwsb("g2b", [128, 512], BF16)
        vw = lambda d: d.rearrange("(kc p) n -> p kc n", p=128)
        dma(wqkv[:], vw(wqkv_d), "", "wqkv", queue="pool"); dma(wfb[:], vw(wf_d), "", "wfb", queue="pool")
        dma(wrw[:], vw(wrw_d), "", "wrw", queue="pool"); dma(wgate[:], vw(wgate_d), "", "wgate", queue="pool")
        dma(w2b[:], w2_d, "", "w2b", queue="pool"); dma(a2b[:], a2_d, "", "a2b", queue="pool"); dma(g2b[:], g2_d, "", "g2b", queue="pool")
        with ExitStack() as ph:
            sb = lambda n, shp, dt=F32: ph.enter_context(nc.sbuf_tensor(n, shp, dt))
            ps = lambda n, shp, dt=F32: ph.enter_context(nc.psum_tensor(n, shp, dt))
            cTt = sb("cTt", [128, 8, NB]); scT = sb("scT", [128, 8, NB]); bmodc = sb("bmodc", [128, 48]); bmodg = sb("bmodg", [NB, 2, D])
            ln1c = sb("ln1c", [128, 8]); ln2c = sb("ln2c", [128, 8]); grow = sb("grow", [NB, 2, D])
            wm = [sb("wm%d" % i, [128, 8, D]) for i in range(2)]
            pcol = ps("pcol", [128, 8, NB]); prow = [ps("prow%d" % i, [NB, 512]) for i in range(2)]
            dma(cTt[:], cT_d, "", "cTt"); dma(bmodc[:], bmodc_d, "", "bmodc"); dma(bmodg[:], bmodg_d, "", "bmodg")
            dma(ln1c[:], ln1_d, "", "ln1c"); dma(ln2c[:], ln2_d, "", "ln2c")
            act(scT[:], cTt[:], AF.Silu, "cTt", "scT")
            wmv = wmod_d.rearrange("(kc p) n -> p kc n", p=128)
            for m in range(6):
                w_ = wm[m % 2]; wn = "wm%d" % (m % 2)
                dma(w_[:], wmv[:, :, m * D:(m + 1) * D], "", wn)
                if m in (2, 5):
                    g = 0 if m == 2 else 1
                    for hf in range(2):
                        pr = prow[hf]; pn = "prow%d" % hf
                        for kc in range(8):
                            mm(pr[:, :], scT[:, kc, :], w_[:, kc, hf * 512:(hf + 1) * 512], kc == 0, kc == 7, [wn, "scT"], [pn])
                        tt("dve", grow[:, g, hf * 512:(hf + 1) * 512], pr[:, :], bmodg[:, g, hf * 512:(hf + 1) * 512], ALU.add, [pn, "bmodg"], ["grow"])
                else:
                    for j in range(8):
                        for kc in range(8):
                            mm(pcol[:, j, :], w_[:, kc, j * 128:(j + 1) * 128], scT[:, kc, :], kc == 0, kc == 7, [wn, "scT"], ["pcol"])
                    for b in range(NB):
                        tt("dve", modc[:, m, :, b], pcol[:, :, b], bmodc[:, m * 8:(m + 1) * 8], ALU.add, ["pcol", "bmodc"], ["modc"])
            dma(gates_s, grow[:], "grow", "gates_s")
            for b in range(NB):
                stt(G1[:, :, b], modc[:, 1, :, b], 1.0, ln1c[:], ALU.add, ALU.mult, ["modc", "ln1c"], ["G1"])
                stt(G2[:, :, b], modc[:, 4, :, b], 1.0, ln2c[:], ALU.add, ALU.mult, ["modc", "ln2c"], ["G2"])

        def norm_T(xt, xn, nblk, Gc, shc, b, uT, un, tmp, tmpn, ssq, rstd, xs, pT, pfx, lnexp=False):
            for blk in range(nblk):
                act(tmp, xt[:, blk, :], AF.Square, [xn], [tmpn, pfx + "ssq"], accum=ssq[:, blk:blk + 1])
            if lnexp:
                act(rstd[:, 0:nblk], ssq[:, 0:nblk], AF.Ln, [pfx + "ssq"], [pfx + "rstd"], bias=EPS, scale=1.0 / D)
                act(rstd[:, 0:nblk], rstd[:, 0:nblk], AF.Exp, [pfx + "rstd"], [pfx + "rstd"], scale=-0.5)
            else:
                act(rstd[:, 0:nblk], ssq[:, 0:nblk], AF.Sqrt, [pfx + "ssq"], [pfx + "rstd"], bias=EPS, scale=1.0 / D)
                SC.op("dve", lambda E: E.reciprocal(out=rstd[:, 0:nblk], in_=rstd[:, 0:nblk]), bl([pfx + "rstd"]), bl([pfx + "rstd"]))
            for blk in range(nblk):
                ts("dve", xs[:, blk, :], xt[:, blk, :], rstd[:, blk:blk + 1], None, ALU.mult, None, [xn, pfx + "rstd"], [pfx + "xs"])
                p_ = pT[blk % 2]; pn = pfx + "pT%d" % (blk % 2)
                for j in range(8):
                    SC.op("pe", lambda E, p_=p_, j=j, blk=blk: E.transpose(out=p_[:, j * 128:(j + 1) * 128], in_=xs[:, blk, j * 128:(j + 1) * 128], identity=ident[:]), bl([pfx + "xs", "ident"]), bl([pn]))
                for j in range(8):
                    act(uT[:, j, blk * 128:(blk + 1) * 128], p_[:, j * 128:(j + 1) * 128], AF.Identity, [pn, "G1", "G2", "modc"], [un], bias=shc[:, j, b:b + 1], scale=Gc[:, j, b:b + 1])

        SC.barrier()
        with ExitStack() as ph:
            sb = lambda n, shp, dt=F32: ph.enter_context(nc.sbuf_tensor(n, shp, dt))
            ps = lambda n, shp, dt=F32: ph.enter_context(nc.psum_tensor(n, shp, dt))
            muc = sb("muc", [128, NRW]); ommu = sb("ommu", [128, NRW]); rwp = sb("rwp", [128, 7, 4]); omka = sb("omka", [128, 4])
            nw0 = sb("nw0", [128, 4]); na0 = sb("na0", [128, 4]); bfc = sb("bfc", [8, 1]); nbf = sb("nbf", [8, 1]); blk1 = sb("blk1", [128, 128])
            dma(muc[:], mu_d, "", "muc"); dma(rwp[:], rwp_d, "", "rwp"); dma(bfc[:], bf_d, "", "bfc")
            ts("dve", ommu[:], muc[:], -1.0, 1.0, ALU.mult, ALU.add, "muc", "ommu")
            ts("dve", omka[:], rwp[:, 3, :], -1.0, 1.0, ALU.mult, ALU.add, "rwp", "omka")
            ts("dve", nw0[:], rwp[:, 0, :], -1.0, None, ALU.mult, None, "rwp", "nw0")
            ts("dve", na0[:], rwp[:, 1, :], -1.0, None, ALU.mult, None, "rwp", "na0")
            ts("dve", nbf[:], bfc[:], -1.0, None, ALU.mult, None, "bfc", "nbf")
            mset("pool", blk1[:], 0.0, "blk1"); mset("pool", blk1[0:64, 0:64], 1.0, "blk1"); mset("pool", blk1[64:128, 64:128], 1.0, "blk1")
            W0, A0, KK_, KA, RK, LNW, LNB = (lambda c4, i=i: rwp[:, i, c4:c4 + 1] for i in range(7))
            fcar = sb("fcar", [8, 1]); pcar = sb("pcar", [128, NRW])
            xt = sb("xt", [128, 4, D]); xs = sb("xs", [128, 4, D], BF16); uTl = [sb("uT%d" % i, [128, 8, TT], BF16) for i in range(2)]; uT = uTl[0]; un = "uT0"
            ssq = sb("ssq", [128, 4]); rstd = sb("rstd", [128, 4])
            qst2 = [sb("qst0", [128, TT], BF16)] * 2; vt = sb("vt", [128, 2, NH, 65], BF16)
            fe = sb("fe", [8, TT]); cum = sb("cum", [8, TT])
            hi = sb("hi", [8, TT], BF16); lo = sb("lo", [8, TT], BF16); nhi = sb("nhi", [8, TT], BF16); nlo = sb("nlo", [8, TT], BF16)
            one8 = sb("one8", [8, 2, TT], BF16)
            Tsets = [[sb("t%d_%d" % (i, s_), [128, TT]) for i in range(9)] for s_ in range(2)]; T_ = Tsets[0]
            praws = [sb("praw_%d" % s_, [128, TT + 1]) for s_ in range(2)]; praw0 = praws[0]; onesf = sb("onesf", [128, 64])
            twb = sb("twb", [64, TT], BF16); xab = sb("xab", [64, TT], BF16); sgb = sb("sgb", [128, TT], BF16)
            rwts = [sb("rwt_%d" % s_, [128, 7, TT], BF16) for s_ in range(2)]; ets = [sb("et_%d" % s_, [128, 2, TT], BF16) for s_ in range(2)]; pcts = [sb("pct_%d" % s_, [128, TT // 64]) for s_ in range(2)]
            gtl = sb("gt", [128, 2, TT], BF16)
            pT = [ps("pT%d" % i, [128, D], BF16) for i in range(2)]
            pm = [ps("pm%d" % i, [128, TT]) for i in range(3)]
            psss = [ps("pss_%d" % s_, [128, TT]) for s_ in range(2)]; pf = ps("pf", [8, TT])
            mset("pool", one8[:], 1.0, "one8"); mset("pool", onesf[:], 1.0, "onesf")
            mset("pool", vt[:], 1.0, "vt")
            pmi = [0]
            def nextpm():
                i = pmi[0] % 3; pmi[0] += 1
                return pm[i], "pm%d" % i

            def proj_fm(wt, wn, c0, M):
                p_, pn = nextpm()
                for kc in range(8):
                    mm(p_[0:M, :], wt[:, kc, c0:c0 + M], uT[:, kc, :], kc == 0, kc == 7, [wn, un], [pn])
                return p_, pn

            def lerp(cc, dst, dn, praw=None, pn_="praw_0"):
                praw = praw0 if praw is None else praw
                p_, pn = proj_fm(wrw, "wrw", cc * 128, 128)
                cp("act", praw[:, 1:TT + 1], p_[:, :], [pn], [pn_])
                cp("pool", praw[:, 0:1], pcar[:, cc:cc + 1], ["pcar"], [pn_])
                ts("dve", dst[:], praw[:, 1:TT + 1], ommu[:, cc:cc + 1], None, ALU.mult, None, [pn_, "ommu"], [dn])
                stt(dst[:], praw[:, 0:TT], muc[:, cc:cc + 1], dst[:], ALU.mult, ALU.add, [pn_, "muc", dn], [dn])
                cp("pool", pcar[:, cc:cc + 1], praw[:, TT:TT + 1], [pn_], ["pcar"])

            def prologue(gt):
                norm_T(xt, "xt", 4, G1, modc[:, 0], gt // NTL, uTl[gt % 2], "uT%d" % (gt % 2), xs[:, 0, :], "xs", ssq, rstd, xs, pT, "", lnexp=True)
                if gt + 1 < NB * NTL:
                    t1 = (gt + 1) * TT
                    dma(xt[:], x_d[t1:t1 + TT, :].rearrange("(k p) d -> p k d", p=128), "", "xt")
            dma(xt[:], x_d[0:TT, :].rearrange("(k p) d -> p k d", p=128), "", "xt")
            prologue(0)
            for b in range(NB):
                mset("pool", fcar[:], 0.0, "fcar"); mset("pool", pcar[:], 0.0, "pcar")
                for tl in range(NTL):
                    t0 = b * S + tl * TT
                    tsl = slice(tl * TT, (tl + 1) * TT)
                    gt = b * NTL + tl
                    uT = uTl[gt % 2]; un = "uT%d" % (gt % 2)
                    def qk_job(qk, hp, b=b, tsl=tsl):
                        p_, pn = proj_fm(wqkv, "wqkv", qk * 512 + hp * 128, 128)
                        st_ = qst2[(qk * 4 + hp) % 2]; sn = "qst0"
                        if qk == 0:
                            act(st_[:], p_[:, :], AF.Identity, [pn], [sn], scale=0.125)
                        else:
                            cp("dve", st_[:], p_[:, :], [pn], [sn])
                        for hh in range(2):
                            dma((q_s if qk == 0 else k_s)[b, 2 * hp + hh, 0:64, tsl], st_[hh * 64:(hh + 1) * 64, :], [sn], "qk_s")
                    def v_job(blk, b=b, tsl=tsl, tl=tl):
                        p_, pn = nextpm()
                        for kc in range(8):
                            mm(p_[:, :], uT[:, kc, blk * 128:(blk + 1) * 128], wqkv[:, kc, 1024:1536], kc == 0, kc == 7, ["wqkv", un], [pn])
                        cp("act", vt[:, blk % 2, :, 0:64], p_[:, :].rearrange("p (h d) -> p h d", h=NH), [pn], ["vt"])
                        if blk % 2 == 1:
                            hs = slice(tl * TT + (blk - 1) * 128, tl * TT + (blk + 1) * 128)
                            dma(v_s[b, hs, :].rearrange("(k p) c -> p k c", p=128), vt[:].rearrange("p k h c -> p k (h c)"), "vt", "v_s")
                    def g_job(gc, b=b, tsl=tsl):
                        gq, gg = gc // 2, gc % 2
                        p_, pn = proj_fm(wgate, "wgate", gc * 128, 128)
                        act(gtl[:, gg, :], p_[:, :], AF.Sigmoid, [pn], ["gt"])
                        if gg == 1:
                            dma(gate_s[b, gq * 2:(gq + 1) * 2, :, tsl].rearrange("g p t -> p g t"), gtl[:], "gt", "gate_s")
                    jstate["in"] = True
                    lerp(12, T_[0], "t0_0"); act(twb[:], T_[0][0:64, :], AF.Tanh, ["t0_0"], ["twb"])
                    lerp(14, T_[0], "t0_0"); act(sgb[:], T_[0][:, :], AF.Sigmoid, ["t0_0"], ["sgb"])
                    for gc in range(16):
                        g_job(gc)
                    lerp(13, T_[0], "t0_0"); cp("dve", xab[:], T_[0][0:64, :], ["t0_0"], ["xab"])
                    jstate["in"] = False
                    for qk in range(2):
                        for hp in range(4):
                            jobs.append(lambda qk=qk, hp=hp: qk_job(qk, hp))
                    for blk in range(4):
                        jobs.append(lambda blk=blk: v_job(blk))
                    if gt + 1 < NB * NTL:
                        jobs.append(lambda gt=gt: prologue(gt + 1))
                    for kc in range(8):
                        mm(pf[:, :], wfb[:, kc, :], uT[:, kc, :], kc == 0, kc == 7, ["wfb", un], ["pf"])
                    act(fe[:], pf[:, :], AF.Exp, ["pf", "nbf"], ["fe"], bias=nbf[:], scale=-1.0)
                    act(fe[:], fe[:], AF.Ln, ["fe", "cst"], ["fe"], bias=ONE[0:8, :])
                    SC.op("dve", lambda E: E.tensor_tensor_scan(out=cum[:], data0=one8[:, 0, :], data1=fe[:], initial=fcar[:], op0=ALU.mult, op1=ALU.subtract), bl("one8 fe fcar"), bl("cum"))
                    cp("pool", fcar[:], cum[:, TT - 1:TT], ["cum"], ["fcar"])
                    cp("dve", hi[:], cum[:], ["cum"], ["hi"])
                    tt("dve", lo[:], cum[:], hi[:], ALU.subtract, ["cum", "hi"], ["lo"])
                    ts("pool", nhi[:], hi[:], -1.0, None, ALU.mult, None, ["hi"], ["nhi"])
                    ts("pool", nlo[:], lo[:], -1.0, None, ALU.mult, None, ["lo"], ["nlo"])
                    dma(q_s[b, :, 64, tsl], hi[:], "hi", "qk_s"); dma(q_s[b, :, 65, tsl], lo[:], "lo", "qk_s")
                    dma(q_s[b, :, 66:68, tsl], one8[:], "one8", "qk_s"); dma(k_s[b, :, 64:66, tsl], one8[:], "one8", "qk_s")
                    dma(k_s[b, :, 66, tsl], nhi[:], "nhi", "qk_s"); dma(k_s[b, :, 67, tsl], nlo[:], "nlo", "qk_s")
                    def chain(c4, si):
                        T_s = Tsets[si]; praw = praws[si]; rwt = rwts[si]; et = ets[si]; pct = pcts[si]; pss = psss[si]
                        N = lambda s: s + '_%d' % si
                        r_, k_, v_, e2, cl, a_, g_, kk, t8 = T_s
                        lerp(c4, r_, N("t0"), praw, N("praw")); lerp(4 + c4, k_, N("t1"), praw, N("praw")); lerp(8 + c4, v_, N("t2"), praw, N("praw"))
                        yield
                        cs = slice(c4 * 128, (c4 + 1) * 128)
                        p_, pn = nextpm(); mm(p_[:, :], w2b[:, cs], twb[:], True, True, ["w2b", "twb"], [pn])
                        act(e2[:], p_[:, :], AF.Exp, [pn, "nw0"], [N("t3")], bias=nw0[:, c4:c4 + 1], scale=-1.0)
                        yield
                        act(e2[:], e2[:], AF.Ln, [N("t3"), "cst"], [N("t3")], bias=ONE)
                        yield
                        act(e2[:], e2[:], AF.Exp, [N("t3"), "cst"], [N("t3")], bias=MHALF, scale=-1.0)
                        yield
                        for c in range(TT // 64):
                            sl = slice(c * 64, (c + 1) * 64)
                            SC.op("dve", lambda E, sl=sl: E.tensor_tensor_scan(out=cl[:, sl], data0=onesf[:], data1=e2[:, sl], initial=0.0, op0=ALU.mult, op1=ALU.subtract), bl(["onesf", N("t3")]), bl([N("t4")]))
                        p_, pn = nextpm(); mm(p_[:, :], a2b[:, cs], xab[:], True, True, ["a2b", "xab"], [pn])
                        act(a_[:], p_[:, :], AF.Exp, [pn, "na0"], [N("t5")], bias=na0[:, c4:c4 + 1], scale=-1.0); act(a_[:], a_[:], AF.Ln, [N("t5"), "cst"], [N("t5")], bias=ONE); act(a_[:], a_[:], AF.Exp, [N("t5")], [N("t5")], scale=-1.0)
                        yield
                        p_, pn = nextpm(); mm(p_[:, :], g2b[:, cs], sgb[:], True, True, ["g2b", "sgb"], [pn])
                        cp("act", g_[:], p_[:, :], [pn], [N("t6")])
                        yield
                        ts("dve", kk[:], k_[:], KK_(c4), None, ALU.mult, None, [N("t1"), "rwp"], [N("t7")])
                        yield
                        tt("dve", t8[:], kk[:], kk[:], ALU.mult, [N("t7")], [N("t8")])
                        yield
                        mm(pss[:, :], blk1[:], t8[:], True, True, ["blk1", N("t8")], [N("pss")])
                        ts("dve", t8[:], pss[:, :], 1e-24, None, ALU.max, None, [N("pss")], [N("t8")])
                        yield
                        act(t8[:], t8[:], AF.Ln, [N("t8")], [N("t8")])
                        yield
                        act(t8[:], t8[:], AF.Exp, [N("t8")], [N("t8")], scale=-0.5)
                        yield
                        tt("dve", kk[:], kk[:], t8[:], ALU.mult, [N("t7"), N("t8")], [N("t7")])
                        yield
                        ts("dve", t8[:], a_[:], KA(c4), omka[:, c4:c4 + 1], ALU.mult, ALU.add, [N("t5"), "rwp", "omka"], [N("t8")])
                        yield
                        tt("dve", k_[:], k_[:], t8[:], ALU.mult, [N("t1"), N("t8")], [N("t1")])
                        yield
                        stt(t8[:], r_[:], RK(c4), k_[:], ALU.mult, ALU.mult, [N("t0"), "rwp", N("t1")], [N("t8")])
                        yield
                        mm(pss[:, :], blk1[:], t8[:], True, True, ["blk1", N("t8")], [N("pss")])
                        tt("dve", t8[:], pss[:, :], v_[:], ALU.mult, [N("pss"), N("t2")], [N("t8")])
                        yield
                        stt(et[:, 0, :], t8[:], LNB(c4), g_[:], ALU.add, ALU.mult, [N("t8"), "rwp", N("t6")], [N("et")])
                        yield
                        ts("dve", et[:, 1, :], g_[:], LNW(c4), None, ALU.mult, None, [N("t6"), "rwp"], [N("et")])
                        yield
                        act(t8[:], cl[:], AF.Exp, [N("t4")], [N("t8")])
                        yield
                        tt("dve", rwt[:, 3, :], r_[:], t8[:], ALU.mult, [N("t0"), N("t8")], [N("rwt")])
                        yield
                        tt("dve", t8[:], cl[:], e2[:], ALU.add, [N("t4"), N("t3")], [N("t8")])
                        yield
                        act(t8[:], t8[:], AF.Exp, [N("t8")], [N("t8")])
                        yield
                        stt(rwt[:, 0, :], kk[:], -1.0, t8[:], ALU.mult, ALU.mult, [N("t7"), N("t8")], [N("rwt")])
                        yield
                        tt("dve", a_[:], kk[:], a_[:], ALU.mult, [N("t7"), N("t5")], [N("t5")])
                        yield
                        act(t8[:], cl[:], AF.Exp, [N("t4")], [N("t8")], scale=-1.0)
                        yield
                        tt("dve", rwt[:, 1, :], a_[:], t8[:], ALU.mult, [N("t5"), N("t8")], [N("rwt")])
                        yield
                        tt("dve", rwt[:, 2, :], k_[:], t8[:], ALU.mult, [N("t1"), N("t8")], [N("rwt")])
                        yield
                        for c in range(TT // 64):
                            sl = slice(c * 64, (c + 1) * 64)
                            act(t8[:, sl], cl[:, sl], AF.Exp, [N("t4")], [N("t8")], bias=cl[:, c * 64 + 63:c * 64 + 64], scale=-1.0)
                        tt("dve", rwt[:, 4, :], a_[:], t8[:], ALU.mult, [N("t5"), N("t8")], [N("rwt")])
                        yield
                        tt("dve", rwt[:, 5, :], k_[:], t8[:], ALU.mult, [N("t1"), N("t8")], [N("rwt")])
                        yield
                        cp("pool", rwt[:, 6, :], v_[:], [N("t2")], [N("rwt")])
                        yield
                        act(pct[:], cl[:, 63:TT:64], AF.Exp, [N("t4")], [N("pct")])
                        yield
                        dma(rw_s[b, :, 2 * c4:2 * c4 + 2, :, tsl].rearrange("q h k t -> (h k) q t"), rwt[:], N("rwt"), "rw_s")
                        yield
                        dma(e_s[b, :, 2 * c4:2 * c4 + 2, :, tsl].rearrange("q h k t -> (h k) q t"), et[:], N("et"), "e_s")
                        yield
                        dma(pc_s[b, 2 * c4:2 * c4 + 2, :, tl * 8:(tl + 1) * 8].rearrange("h k c -> (h k) c"), pct[:], N("pct"), "pc_s")
                        yield
                    for pair in ((0, 1), (2, 3)):
                        gens = [chain(pair[0], 0), chain(pair[1], 1)]
                        live = [True, True]
                        while any(live):
                            for gi in range(2):
                                if live[gi]:
                                    try:
                                        next(gens[gi])
                                    except StopIteration:
                                        live[gi] = False
                    flush_jobs()

        SC.barrier()
        pw.close()
        with ExitStack() as ph:
            sb = lambda n, shp, dt=F32: ph.enter_context(nc.sbuf_tensor(n, shp, dt))
            ps = lambda n, shp, dt=F32: ph.enter_context(nc.psum_tensor(n, shp, dt))
            NKB = S // 128
            kT = sb("kT", [68, NH, S], BF16); vS = sb("vS", [128, NKB, NH * 65], BF16); qTl = [sb("qq%d" % i, [68, NH, TT], BF16) for i in range(2)]
            PT = [sb("PT%d" % i, [128, TT], BF16) for i in range(3)]
            msk = sb("msk", [128, 4, TT], BF16); otok = sb("otok", [128, 4, NH * 64], BF16); rec = sb("rec", [128, 4])
            ofT = sb("ofT", [128, 4, TT], BF16); clampS = sb("clampS", [128, TT])
            pS = [ps("pS%d" % i, [128, TT]) for i in range(3)]
            pO = [ps("pO%d" % i, [128, 4, 128]) for i in range(2)]; hcnt = [0]
            pTr = ps("pTr", [128, D], BF16)
            mset("pool", msk[:], 1.0, "msk")
            for jj in range(4):
                SC.op("pool", lambda E, jj=jj: E.affine_select(out=msk[:, jj, :], in_=msk[:, jj, :], pattern=[[1, TT]], compare_op=ALU.is_ge, fill=0.0, base=-jj * 128, channel_multiplier=-1), bl("msk"), bl("msk"))
            for b in range(NB):
                dma(kT[:], k_s[b].rearrange("h k t -> k h t"), "qk_s", "kT")
                dma(vS[:], v_s[b].rearrange("(j p) c -> p j c", p=128), "v_s", "vS")
                def qload(tl_, b=b):
                    dma(qTl[tl_ % 2][:], q_s[b, :, :, tl_ * TT:(tl_ + 1) * TT].rearrange("h k t -> k h t"), "qk_s", "qq%d" % (tl_ % 2))
                qload(0)
                for tl in range(NTL):
                    tsl = slice(tl * TT, (tl + 1) * TT)
                    qT = qTl[tl % 2]; qn = "qq%d" % (tl % 2)
                    if tl + 1 < NTL:
                        qload(tl + 1)
                    for h in range(NH):
                        nj = 4 * tl + 4
                        ob = hcnt[0] % 2; hcnt[0] += 1
                        pOh = pO[ob]; pOn = "pO%d" % ob
                        def QK(j):
                            mm(pS[j % 3][:, :], kT[:, h, j * 128:(j + 1) * 128], qT[:, h, :], True, True, ["kT", qn], ["pS%d" % (j % 3)])
                        QK(0)
                        if nj > 1:
                            QK(1)
                        for j in range(nj):
                            p_ = pS[j % 3]; pn = "pS%d" % (j % 3); P_ = PT[j % 3]; Pn = "PT%d" % (j % 3)
                            jj = j - 4 * tl
                            if jj >= 0:
                                c0 = jj * 128
                                ts("dve", clampS[:, c0:], p_[:, c0:], 60.0, None, ALU.min, None, [pn], ["clampS"])
                                act(P_[:, c0:], clampS[:, c0:], AF.Exp, ["clampS"], [Pn])
                                tt("dve", P_[:, c0:], P_[:, c0:], msk[:, jj, c0:], ALU.mult, [Pn, "msk"], [Pn])
                            else:
                                act(P_[:], p_[:, :], AF.Exp, [pn], [Pn])
                            if j + 2 < nj:
                                QK(j + 2)
                            for qb in range(4):
                                if j <= 4 * tl + qb:
                                    mm(pOh[:, qb, 0:65], P_[:, qb * 128:(qb + 1) * 128], vS[:, j, h * 65:(h + 1) * 65], j == 0 and qb == 0, j == 4 * tl + qb, [Pn, "vS"], [pOn], skip=True)
                        for qb in range(4):
                            SC.op("dve", lambda E, qb=qb, pOh=pOh: E.reciprocal(out=rec[:, qb:qb + 1], in_=pOh[:, qb, 64:65]), bl([pOn]), bl("rec"))
                            ts("dve", otok[:, qb, h * 64:(h + 1) * 64], pOh[:, qb, 0:64], rec[:, qb:qb + 1], None, ALU.mult, None, [pOn, "rec"], ["otok"])
                    for qb in range(4):
                        for c in range(4):
                            SC.op("pe", lambda E, qb=qb, c=c: E.transpose(out=pTr[:, c * 128:(c + 1) * 128], in_=otok[:, qb, c * 128:(c + 1) * 128], identity=ident[:]), bl("otok ident"), bl("pTr"))
                        cp("act", ofT[:, :, qb * 128:(qb + 1) * 128], pTr[:, 0:512].rearrange("p (c t) -> p c t", c=4), ["pTr"], ["ofT"])
                    dma(ofox_s[b, :, :, tsl].rearrange("c p t -> p c t"), ofT[:], "ofT", "ofox_s")

        SC.barrier()
        with ExitStack() as ph:
            sb = lambda n, shp, dt=F32: ph.enter_context(nc.sbuf_tensor(n, shp, dt))
            ps = lambda n, shp, dt=F32: ph.enter_context(nc.psum_tensor(n, shp, dt))
            PD = P3DT
            D2 = P3D2
            UC = 4 if PD == BF16 else 2
            UT = UC * 64
            NU = S // UT
            RW = [sb("RW%d" % i, [64, 7, NH, UT], PD) for i in range(2)]
            ET = [sb("ET%d" % i, [128, 2, 4, UT], BF16) for i in range(2)]
            PC = [sb("PC%d" % i, [64, NH, UC]) for i in range(2)]
            Hf = sb("Hf", [64, NH, 64]); Hhi = sb("Hhi", [64, NH, 64], PD); Hlo = sb("Hlo", [64, NH, 64], PD)
            mLT = sb("mLT", [64, NH, 64]); mL = sb("mL", [64, NH, 64]); mGT = sb("mGT", [64, NH, 64]); idF = sb("idF", [64, NH, 64]); idP = sb("idP", [64, 64], PD)
            tok = [[sb("tok%d_%d" % (p, c), [64, 4, NH, 64], PD) for c in range(UC)] for p in range(2)]
            TTt = [[sb("TT%d_%d" % (p, c), [64, NH, 64], D2) for c in range(UC)] for p in range(2)]
            LakT = [[sb("LakT%d_%d" % (p, c), [64, NH, 64], PD) for c in range(UC)] for p in range(2)]
            GrbT = [[sb("GrbT%d_%d" % (p, c), [64, NH, 64], PD) for c in range(UC)] for p in range(2)]
            GrkT = [[sb("GrkT%d_%d" % (p, c), [64, NH, 64], PD) for c in range(UC)] for p in range(2)]
            A_ = [[sb("A%d_%d" % (i, c), [64, NH, 64], BF16) for c in range(UC)] for i in range(2)]
            Tb = [sb("Tb_%d" % c, [64, NH, 64], BF16) for c in range(UC)]; TTb = [sb("TTb_%d" % c, [64, NH, 64], BF16) for c in range(UC)]
            L0b = [sb("L0b_%d" % c, [64, NH, 64], BF16) for c in range(UC)]; R2T = [sb("R2T_%d" % c, [64, NH, 64], BF16) for c in range(UC)]
            AT_ = [[sb("AT%d_%d" % (i, c), [64, NH, 64], BF16) for c in range(UC)] for i in range(2)]
            R2f = sb("R2f", [64, NH, 64]); RHS = sb("RHS", [64, NH, 64], D2); U = sb("U", [64, NH, 64], PD)
            ytl = [sb("yt%d" % i, [64, NH, 64]) for i in range(2)]; ysq = sb("ysq", [64, NH, 64]); yn = sb("yn", [64, NH, 64], BF16)
            st8 = sb("st8", [64, NH]); rs8 = sb("rs8", [64, NH]); orT = [sb("orT%d" % i, [128, 4, UT], BF16) for i in range(2)]; o1 = sb("o1", [128, 4, 64])
            pp = [ps("pp%d" % i, [64, NH, 64]) for i in range(4)]
            pq = [ps("pq%d" % i, [64, NH, 64]) for i in range(2)]
            ptrA = ps("ptrA", [64, 2, NH * 64], BF16); ptr = ps("ptr", [128, 256], BF16)
            for t_, op_, cm, st in ((mLT, ALU.is_gt, -1, 1), (mL, ALU.is_gt, 1, -1), (mGT, ALU.is_ge, -1, 1), (idF, ALU.is_equal, 1, -1)):
                SC.op("pool", lambda E, t_=t_: E.memset(t_[:], 1.0), [], bl("masks"))
                SC.op("pool", lambda E, t_=t_, op_=op_, cm=cm, st=st: E.affine_select(out=t_[:], in_=t_[:], pattern=[[0, NH], [st, 64]], compare_op=op_, fill=0.0, base=0, channel_multiplier=cm), bl("masks"), bl("masks"))
            cp("dve", idP[:], idF[:, 0, :], ["masks"], ["idP"])
            ppi = [0]; pqi = [0]; pti = [0]
            def grp(pool, idx, lhs_fn, rhs_fn, R, start=True, stop=True, same=None):
                if same is None:
                    i = idx[0] % len(pool); idx[0] += 1
                else:
                    i = same
                nm = ("pp%d" if pool is pp else "pq%d") % i
                multi = not (start and stop)
                for h in range(NH):
                    mm(pool[i][:, h, :], lhs_fn(h), rhs_fn(h), start and (h == 0 or not multi), stop, R, [nm], skip=multi)
                return pool[i], nm, i

            def A_groups(b, u):
                p = u % 2
                rw = RW[p]; rwn = "RW%d" % p
                usl = slice(u * UT, (u + 1) * UT)
                G = []
                def load():
                    q_ = "pool" if PD == F32 else "sp"
                    dma(rw[:], rw_s[b, :, :, :, usl].rearrange("q h k t -> k q h t"), "rw_s", rwn, queue=q_)
                    dma(ET[p][:], e_s[b, :, :, :, usl].rearrange("q (c hh) k t -> (hh k) q c t", hh=2), "e_s", "ET%d" % p)
                    dma(PC[p][:], pc_s[b, :, :, u * UC:(u + 1) * UC].rearrange("h k c -> k h c"), "pc_s", "PC%d" % p)
                G.append(load)
                X = lambda q, c: (lambda h: rw[:, q, h, c * 64:(c + 1) * 64])
                def tr_stage(i, q):
                    def f(c):
                        tkn = "tok%d_%d" % (p, c)
                        if PD == BF16:
                            k = pti[0] % 2; pti[0] += 1
                            for h in range(NH):
                                SC.op("pe", lambda E, h=h, k=k, c=c: E.transpose(out=ptrA[:, k, h * 64:(h + 1) * 64], in_=rw[:, q, h, c * 64:(c + 1) * 64], identity=idP[:]), bl([rwn, "idP"]), bl(["ptrA"]))
                            cp("act", tok[p][c][:, i, :, :], ptrA[:, k, :].rearrange("p (h d) -> p h d", h=NH), ["ptrA"], [tkn])
                        else:
                            j = ppi[0] % 4; ppi[0] += 1
                            for h in range(NH):
                                SC.op("pe", lambda E, h=h, j=j, c=c: E.transpose(out=pp[j][:, h, :], in_=rw[:, q, h, c * 64:(c + 1) * 64], identity=idP[:]), bl([rwn, "idP"]), bl(["pp%d" % j]))
                            cp("act" if i % 2 else "dve", tok[p][c][:, i, :, :], pp[j][:], ["pp%d" % j], [tkn])
                    return f
                def prod_stage(ql, qr, dst_fn, dn_fn, mask, eng):
                    def f(c):
                        t_, nm, _ = grp(pp, ppi, X(ql, c), X(qr, c), [rwn])
                        tt(eng, dst_fn(c)[:], t_[:], mask[:], ALU.mult, [nm, "masks"], [dn_fn(c)])
                    return f
                stages = [tr_stage(i, q) for i, q in enumerate((0, 4, 5, 6))]
                stages.append(prod_stage(1, 0, lambda c: AT_[0][c], lambda c: "AT0_%d" % c, mLT, "dve"))
                def a0_stage(c):
                    t_, nm, _ = grp(pp, ppi, X(0, c), X(1, c), [rwn])
                    tt("dve", A_[0][c][:], t_[:], mL[:], ALU.mult, [nm, "masks"], ["A0_%d" % c])
                    cp("act", L0b[c][:], A_[0][c][:], ["A0_%d" % c], ["L0b_%d" % c])
                stages.append(a0_stage)
                stages.append(prod_stage(2, 0, lambda c: LakT[p][c], lambda c: "LakT%d_%d" % (p, c), mLT, "dve"))
                stages.append(prod_stage(1, 3, lambda c: GrbT[p][c], lambda c: "GrbT%d_%d" % (p, c), mGT, "dve"))
                stages.append(prod_stage(2, 3, lambda c: GrkT[p][c], lambda c: "GrkT%d_%d" % (p, c), mGT, "dve"))
                def st_init(c):
                    tt("dve", TTb[c][:], AT_[0][c][:], idF[:], ALU.add, ["AT0_%d" % c, "masks"], ["TTb_%d" % c])
                stages.append(st_init)
                for r in range(6):
                    cur = r % 2; nxt = 1 - cur
                    def sq1(c, cur=cur, nxt=nxt):
                        t_, nm, _ = grp(pp, ppi, lambda h: A_[cur][c][:, h, :], lambda h: AT_[cur][c][:, h, :], ["A%d_%d" % (cur, c), "AT%d_%d" % (cur, c)])
                        cp("act", AT_[nxt][c][:], t_[:], [nm], ["AT%d_%d" % (nxt, c)])
                    def sq2(c, cur=cur, nxt=nxt):
                        t_, nm, _ = grp(pp, ppi, lambda h: AT_[cur][c][:, h, :], lambda h: A_[cur][c][:, h, :], ["A%d_%d" % (cur, c), "AT%d_%d" % (cur, c)])
                        cp("act", A_[nxt][c][:], t_[:], [nm], ["A%d_%d" % (nxt, c)])
                    def supd(c, cur=cur):
                        t_, nm, _ = grp(pp, ppi, lambda h: A_[cur][c][:, h, :], lambda h: TTb[c][:, h, :], ["A%d_%d" % (cur, c), "TTb_%d" % c])
                        tt("dve", TTb[c][:], TTb[c][:], t_[:], ALU.add, ["TTb_%d" % c, nm], ["TTb_%d" % c])
                    def supd2(c, cur=cur):
                        t_, nm, _ = grp(pp, ppi, lambda h: AT_[cur][c][:, h, :], lambda h: Tb[c][:, h, :], ["AT%d_%d" % (cur, c), "Tb_%d" % c])
                        tt("dve", Tb[c][:], Tb[c][:], t_[:], ALU.add, ["Tb_%d" % c, nm], ["Tb_%d" % c])
                    if r >= 1:
                        stages.append(supd)
                    if r <= 4:
                        stages.append(sq1); stages.append(sq2)
                def nwT(c):
                    k = pti[0] % 2; pti[0] += 1
                    for h in range(NH):
                        SC.op("pe", lambda E, h=h, k=k, c=c: E.transpose(out=ptrA[:, k, h * 64:(h + 1) * 64], in_=TTb[c][:, h, :], identity=idP[:]), bl(["TTb_%d" % c, "idP"]), bl(["ptrA"]))
                    cp("act", Tb[c][:], ptrA[:, k, :].rearrange("p (h d) -> p h d", h=NH), ["ptrA"], ["Tb_%d" % c])
                def nw1(c):
                    t_, nm, _ = grp(pp, ppi, lambda h: L0b[c][:, h, :], lambda h: TTb[c][:, h, :], ["L0b_%d" % c, "TTb_%d" % c])
                    stt(R2f[:], TTb[c][:], -1.0, t_[:], ALU.mult, ALU.add, ["TTb_%d" % c, nm], ["R2f"])
                    tt("dve", R2T[c][:], R2f[:], idF[:], ALU.add, ["R2f", "masks"], ["R2T_%d" % c])
                def nw2(c):
                    t_, nm, _ = grp(pp, ppi, lambda h: Tb[c][:, h, :], lambda h: R2T[c][:, h, :], ["Tb_%d" % c, "R2T_%d" % c])
                    tt("dve", TTt[p][c][:], TTb[c][:], t_[:], ALU.add, ["TTb_%d" % c, nm], ["TT%d_%d" % (p, c)])
                stages += [nwT, nw1, nw2]
                for stg in stages:
                    for c in range(UC):
                        G.append(lambda stg=stg, c=c: stg(c))
                return G

            def B_steps(b, u):
                p = u % 2
                rw = RW[p]; rwn = "RW%d" % p
                usl = slice(u * UT, (u + 1) * UT)
                Bs = []
                pend = []
                for c in range(UC):
                    sl = slice(c * 64, (c + 1) * 64)
                    X = lambda q, sl=sl: (lambda h: rw[:, q, h, sl])
                    tk = tok[p][c]; tkn = "tok%d_%d" % (p, c)
                    TK = lambda i, tk=tk: (lambda h: tk[:, i, h, :])
                    Hn = ["Hhi", "Hlo"] if PD == BF16 else ["Hf"]
                    def s1(c=c, X=X, TK=TK, tkn=tkn):
                        if PD == BF16:
                            t_, nm, i = grp(pq, pqi, X(0), lambda h: Hhi[:, h, :], [rwn, "Hhi"], True, False)
                        else:
                            t_, nm, i = grp(pq, pqi, X(0), lambda h: Hf[:, h, :], [rwn, "Hf"], True, False)
                        grp(pq, pqi, lambda h: LakT[p][c][:, h, :], TK(3), ["LakT%d_%d" % (p, c), tkn], False, True, same=i)
                        cp("act", RHS[:], t_[:], [nm], ["RHS"])
                    def s2(c=c):
                        tt("dve", Hf[:], Hf[:], PC[p][:, :, c:c + 1].broadcast_to([64, NH, 64]), ALU.mult, ["Hf", "PC%d" % p], ["Hf"])
                        t_, nm, i = grp(pq, pqi, lambda h: TTt[p][c][:, h, :], lambda h: RHS[:, h, :], ["TT%d_%d" % (p, c), "RHS"])
                        cp("act", U[:], t_[:], [nm], ["U"])
                    def s3(c=c, X=X, TK=TK, tkn=tkn):
                        t_, nm, i = grp(pq, pqi, TK(1), lambda h: U[:, h, :], [tkn, "U"], True, False)
                        grp(pq, pqi, TK(2), TK(3), [tkn], False, True, same=i)
                        tt("dve", Hf[:], Hf[:], t_[:], ALU.add, ["Hf", nm], ["Hf"])
                        if PD == BF16:
                            cp("dve", Hhi[:], Hf[:], ["Hf"], ["Hhi"])
                    def s3y(c=c, X=X, TK=TK, tkn=tkn):
                        if PD == BF16:
                            t_, nm, i = grp(pq, pqi, X(3), lambda h: Hhi[:, h, :], [rwn, "Hhi"], True, False)
                        else:
                            t_, nm, i = grp(pq, pqi, X(3), lambda h: Hf[:, h, :], [rwn, "Hf"], True, False)
                        grp(pq, pqi, lambda h: GrbT[p][c][:, h, :], lambda h: U[:, h, :], ["GrbT%d_%d" % (p, c), "U"], False, False, same=i)
                        grp(pq, pqi, lambda h: GrkT[p][c][:, h, :], TK(3), ["GrkT%d_%d" % (p, c), tkn], False, True, same=i)
                        cp("act", ytl[c % 2][:], t_[:], [nm], ["yt%d" % (c % 2)])
                    def s4(c=c, sl=sl):
                        yt_ = ytl[c % 2]; ytn = "yt%d" % (c % 2)
                        bc = lambda t: t[:, :].unsqueeze(2).broadcast_to([64, NH, 64])
                        SC.op("dve", lambda E: E.tensor_reduce(out=st8[:], in_=yt_[:], axis=AX.X, op=ALU.add), bl([ytn]), bl("st8"))
                        ts("dve", st8[:], st8[:], 1.0 / 64, None, ALU.mult, None, ["st8"], ["st8"])
                        tt("dve", yt_[:], yt_[:], bc(st8), ALU.subtract, [ytn, "st8"], [ytn])
                        act(ysq[:], yt_[:], AF.Square, [ytn], ["ysq"])
                        SC.op("dve", lambda E: E.tensor_reduce(out=rs8[:], in_=ysq[:], axis=AX.X, op=ALU.add), bl("ysq"), bl("rs8"))
                        act(rs8[:], rs8[:], AF.Sqrt, ["rs8", "cst"], ["rs8"], bias=GNEPS[0:64, :], scale=1.0 / 64)
                        SC.op("dve", lambda E: E.reciprocal(out=rs8[:], in_=rs8[:]), bl("rs8"), bl("rs8"))
                        tt("dve", yn[:], yt_[:], bc(rs8), ALU.mult, [ytn, "rs8"], ["yn"])
                        for c4 in range(4):
                            SC.op("pe", lambda E, c4=c4: E.transpose(out=ptr[:, c4 * 64:(c4 + 1) * 64], in_=yn[:, 2 * c4:2 * c4 + 2, :].rearrange("p h d -> p (h d)"), identity=ident[0:64, 0:64]), bl("yn ident"), bl("ptr"))
                        tt("dve", o1[:], ptr[:, :].rearrange("p (c t) -> p c t", c=4), ET[p][:, 1, :, sl], ALU.mult, ["ptr", "ET%d" % p], ["o1"])
                        tt("dve", orT[p][:, :, sl], o1[:], ET[p][:, 0, :, sl], ALU.add, ["o1", "ET%d" % p], ["orT%d" % p])
                    if pend:
                        Bs += [s1, s2, pend.pop(), s3y, s3]
                    else:
                        Bs += [s1, s2, s3y, s3]
                    pend.append(s4)
                Bs.append(pend.pop())
                def store():
                    dma(orw_s[b, :, :, usl].rearrange("c p t -> p c t"), orT[p][:], "orT%d" % p, "orw_s")
                Bs.append(store)
                return Bs

            for b in range(NB):
                mset("pool", Hf[:], 0.0, "Hf")
                if PD == BF16:
                    mset("pool", Hhi[:], 0.0, "Hhi"); mset("pool", Hlo[:], 0.0, "Hlo")
                for g in A_groups(b, 0):
                    g()
                for u in range(NU):
                    Bl = B_steps(b, u)
                    Al = A_groups(b, u + 1) if u + 1 < NU else []
                    na, nb_ = len(Al), len(Bl)
                    ia = 0
                    for ib, st in enumerate(Bl):
                        st()
                        tgt = (ib + 1) * na // nb_
                        while ia < tgt:
                            Al[ia](); ia += 1
                    while ia < na:
                        Al[ia](); ia += 1
        SC.barrier()
        with ExitStack() as ph:
            sb = lambda n, shp, dt=F32: ph.enter_context(nc.sbuf_tensor(n, shp, dt))
            ps = lambda n, shp, dt=F32: ph.enter_context(nc.psum_tensor(n, shp, dt))
            foxo = sb("foxo", [128, 4, D], BF16); rwo = sb("rwo", [128, 4, D], BF16); wo = sb("wo", [128, 8, D], BF16)
            vw = lambda d: d.rearrange("(kc p) n -> p kc n", p=128)
            dma(foxo[:], vw(foxo_d), "", "foxo", queue="pool"); dma(rwo[:], vw(rwo_d), "", "rwo", queue="pool"); dma(wo[:], vw(wo_d), "", "wo", queue="pool")
            ofTl = [sb("ofT4_%d" % i, [128, 4, TT], BF16) for i in range(2)]; orTl = [sb("orT4_%d" % i, [128, 4, TT], BF16) for i in range(2)]; gtsl = [sb("gts_%d" % i, [128, 16, TT], BF16) for i in range(2)]
            xtl = [sb("xt4_%d" % i, [128, 4, D]) for i in range(2)]; g1b = sb("g1b", [128, D]); mg = sb("mg", [128, 8, TT], BF16)
            m1l = [sb("m1_%d" % i, [128, TT]) for i in range(2)]; m2l = [sb("m2_%d" % i, [128, TT]) for i in range(2)]
            pF = [ps("pF%d" % i, [128, TT]) for i in range(2)]; pR = [ps("pR%d" % i, [128, TT]) for i in range(2)]; pM = [ps("pM%d" % i, [128, TT]) for i in range(2)]
            def loads(gt):
                i = gt % 2
                b_, tl_ = gt // NTL, gt % NTL
                t0_ = b_ * S + tl_ * TT
                tsl_ = slice(tl_ * TT, (tl_ + 1) * TT)
                dma(ofTl[i][:], ofox_s[b_, :, :, tsl_].rearrange("c p t -> p c t"), "ofox_s", "ofT4_%d" % i)
                dma(orTl[i][:], orw_s[b_, :, :, tsl_].rearrange("c p t -> p c t"), "orw_s", "orT4_%d" % i)
                dma(gtsl[i][:], gate_s[b_, :, :, tsl_].rearrange("g p t -> p g t"), "gate_s", "gts_%d" % i)
                dma(xtl[i][:], x_d[t0_:t0_ + TT, :].rearrange("(k p) d -> p k d", p=128), "", "xt4_%d" % i)
            loads(0)
            for b in range(NB):
                dma(g1b[:], gates_s[b, 0, :].partition_broadcast(128), "gates_s", "g1b")
                for tl in range(NTL):
                    t0 = b * S + tl * TT
                    gt = b * NTL + tl
                    i4 = gt % 2
                    ofT = ofTl[i4]; orT = orTl[i4]; gts = gtsl[i4]; xt = xtl[i4]
                    ofn, orn, gtn, xtn = "ofT4_%d" % i4, "orT4_%d" % i4, "gts_%d" % i4, "xt4_%d" % i4
                    if gt + 1 < NB * NTL:
                        loads(gt + 1)
                    for oc in range(8):
                        f_ = pF[oc % 2]; fn = "pF%d" % (oc % 2); r_ = pR[oc % 2]; rn = "pR%d" % (oc % 2)
                        for kc in range(4):
                            mm(f_[:, :], foxo[:, kc, oc * 128:(oc + 1) * 128], ofT[:, kc, :], kc == 0, kc == 3, ["foxo", ofn], [fn])
                        for kc in range(4):
                            mm(r_[:, :], rwo[:, kc, oc * 128:(oc + 1) * 128], orT[:, kc, :], kc == 0, kc == 3, ["rwo", orn], [rn])
                        m1 = m1l[oc % 2]; m2 = m2l[oc % 2]; m1n = "m1_%d" % (oc % 2); m2n = "m2_%d" % (oc % 2)
                        tt("dve", m1[:], f_[:, :], gts[:, oc, :], ALU.mult, [fn, gtn], [m1n])
                        tt("dve", m2[:], r_[:, :], gts[:, 8 + oc, :], ALU.mult, [rn, gtn], [m2n])
                        tt("pool", mg[:, oc, :], m1[:], m2[:], ALU.add, [m1n, m2n], ["mg"])
                    for blk in range(4):
                        for hf in range(2):
                            i = (blk * 2 + hf) % 2
                            for kc in range(8):
                                mm(pM[i][:, :], mg[:, kc, blk * 128:(blk + 1) * 128], wo[:, kc, hf * 512:(hf + 1) * 512], kc == 0, kc == 7, ["mg", "wo"], ["pM%d" % i])
                            m1 = m1l[i]; m1n = "m1_%d" % i
                            tt("dve", m1[:], pM[i][:, :], g1b[:, hf * 512:(hf + 1) * 512], ALU.mult, ["pM%d" % i, "g1b"], [m1n])
                            tt("pool", xt[:, blk, hf * 512:(hf + 1) * 512], xt[:, blk, hf * 512:(hf + 1) * 512], m1[:], ALU.add, [xtn, m1n], [xtn])
                    dma(h_s[t0:t0 + TT, :].rearrange("(k p) d -> p k d", p=128), xt[:], xtn, "h_s")

        SC.barrier()
        with ExitStack() as ph:
            sb = lambda n, shp, dt=F32: ph.enter_context(nc.sbuf_tensor(n, shp, dt))
            ps = lambda n, shp, dt=F32: ph.enter_context(nc.psum_tensor(n, shp, dt))
            T2 = 256
            wup = sb("wup", [128, 8, 2 * DFF], BF16); wdn = sb("wdn", [128, 22, D], BF16)
            vw = lambda d: d.rearrange("(kc p) n -> p kc n", p=128)
            dma(wup[:], vw(wup_d), "", "wup", queue="pool"); dma(wdn[:], vw(wdn_d), "", "wdn", queue="pool")
            cw = sb("cw", [128, 22, 3]); cb = sb("cb", [128, 22]); g2b_ = sb("g2bc", [128, D]); fgb = sb("fgb", [128, D])
            dma(cw[:], convw_d, "", "cw"); dma(cb[:], convb_d, "", "cb"); dma(fgb[:], fing_d.partition_broadcast(128), "", "fgb")
            htl = [sb("ht%d" % i, [128, 2, D]) for i in range(2)]; hx = sb("hx", [128, 2, D], BF16); u2l = [sb("u2T%d" % i, [128, 8, T2], BF16) for i in range(2)]; ssq3 = sb("ssq3", [128, 2]); rstd3 = sb("rstd3", [128, 2])
            ssq = sb("ssq2", [128, 2]); rstd = sb("rstd2", [128, 2]); acar = sb("acar", [128, 22, 2])
            araw_ = [sb("araw%d" % i, [128, T2 + 2]) for i in range(2)]; c1_ = [sb("c1%d" % i, [128, T2]) for i in range(2)]; gT = sb("gT", [128, 22, T2], BF16)
            m1bl = [sb("m1b_%d" % i, [128, 512]) for i in range(2)]; junk = sb("junk", [128, D], BF16); junk2 = sb("junk2", [128, D], BF16)
            pT = [ps("qT%d" % i, [128, D], BF16) for i in range(2)]
            pa = [ps("pa%d" % i, [128, T2]) for i in range(2)]; pb = [ps("pb%d" % i, [128, T2]) for i in range(2)]
            pD = [ps("pD%d" % i, [128, 512]) for i in range(2)]
            def hload(b, tl):
                i = tl % 2
                t0 = b * S + tl * T2
                dma(htl[i][:], h_s[t0:t0 + T2, :].rearrange("(k p) d -> p k d", p=128), "h_s", "ht%d" % i)
            def prologue(b, tl):
                i = tl % 2
                norm_T(htl[i], "ht%d" % i, 2, G2, modc[:, 3], b, u2l[i], "u2T%d" % i, junk[:], "junk", ssq, rstd, hx, pT, "b_")
            def main(b, tl):
                i = tl % 2
                u2T = u2l[i]; un = "u2T%d" % i
                for fc in range(22):
                    a_ = pa[fc % 2]; an = "pa%d" % (fc % 2); b_ = pb[fc % 2]; bn = "pb%d" % (fc % 2)
                    araw = araw_[fc % 2]; arn = "araw%d" % (fc % 2); c1 = c1_[fc % 2]; c1n = "c1%d" % (fc % 2)
                    for kc in range(8):
                        mm(a_[:, :], wup[:, kc, fc * 128:(fc + 1) * 128], u2T[:, kc, :], kc == 0, kc == 7, ["wup", un], [an])
                    for kc in range(8):
                        mm(b_[:, :], wup[:, kc, DFF + fc * 128:DFF + (fc + 1) * 128], u2T[:, kc, :], kc == 0, kc == 7, ["wup", un], [bn])
                    cp("act", araw[:, 2:T2 + 2], a_[:, :], [an], [arn])
                    cp("pool", araw[:, 0:2], acar[:, fc, :], ["acar"], [arn])
                    ts("dve", c1[:], araw[:, 2:T2 + 2], cw[:, fc, 2:3], cb[:, fc:fc + 1], ALU.mult, ALU.add, [arn, "cw", "cb"], [c1n])
                    stt(c1[:], araw[:, 1:T2 + 1], cw[:, fc, 1:2], c1[:], ALU.mult, ALU.add, [arn, "cw", c1n], [c1n])
                    stt(c1[:], araw[:, 0:T2], cw[:, fc, 0:1], c1[:], ALU.mult, ALU.add, [arn, "cw", c1n], [c1n])
                    cp("pool", acar[:, fc, :], araw[:, T2:T2 + 2], [arn], ["acar"])
                    act(c1[:], c1[:], AF.Silu, [c1n], [c1n])
                    tt("dve", gT[:, fc, :], c1[:], b_[:, :], ALU.mult, [c1n, bn], ["gT"])
            def epilogue(b, tl):
                i = tl % 2
                ht = htl[i]; hn = "ht%d" % i
                t0 = b * S + tl * T2
                for blk in range(2):
                    for hf in range(2):
                        for fc in range(22):
                            mm(pD[hf][:, :], gT[:, fc, blk * 128:(blk + 1) * 128], wdn[:, fc, hf * 512:(hf + 1) * 512], fc == 0, fc == 21, ["gT", "wdn"], ["pD%d" % hf])
                        m1 = m1bl[hf]; m1n = "m1b_%d" % hf
                        tt("dve", m1[:], pD[hf][:, :], g2b_[:, hf * 512:(hf + 1) * 512], ALU.mult, ["pD%d" % hf, "g2bc"], [m1n])
                        tt("pool", ht[:, blk, hf * 512:(hf + 1) * 512], ht[:, blk, hf * 512:(hf + 1) * 512], m1[:], ALU.add, [hn, m1n], [hn])
                for blk in range(2):
                    act(junk2[:], ht[:, blk, :], AF.Square, [hn], ["junk2", "ssq3"], accum=ssq3[:, blk:blk + 1])
                act(rstd3[:], ssq3[:], AF.Sqrt, ["ssq3", "cst"], ["rstd3"], bias=EPS, scale=1.0 / D)
                SC.op("dve", lambda E: E.reciprocal(out=rstd3[:], in_=rstd3[:]), bl("rstd3"), bl("rstd3"))
                for blk in range(2):
                    stt(ht[:, blk, :], ht[:, blk, :], rstd3[:, blk:blk + 1], fgb[:], ALU.mult, ALU.mult, [hn, "rstd3", "fgb"], [hn])
                dma(out_d[t0:t0 + T2, :].rearrange("(k p) d -> p k d", p=128), ht[:], hn, "out")
            NT2 = S // T2
            for b in range(NB):
                dma(g2b_[:], gates_s[b, 1, :].partition_broadcast(128), "gates_s", "g2bc")
                mset("pool", acar[:], 0.0, "acar")
                hload(b, 0)
                prologue(b, 0)
                for tl in range(NT2):
                    if tl + 1 < NT2:
                        hload(b, tl + 1)
                    main(b, tl)
                    if tl + 1 < NT2:
                        prologue(b, tl + 1)
                    epilogue(b, tl)
        SC.final_wait(bl("out"))
        if debug:
            SC.final_wait(list(B.values()))
        SC.emit()
    return nc


def host_inputs(inp, b0, NB):
    f = lambda a: np.ascontiguousarray(np.asarray(a, dtype=np.float32))
    S = inp["x"].shape[1]
    w_in = np.asarray(inp["w_in"])[0]
    rw = w_in[:, 1544:3336]
    z64 = np.zeros((D, 64), np.float32)
    w_rw = np.concatenate([rw[:, 0:1600], z64, rw[:, 1600:1664], z64, rw[:, 1664:1792]], axis=1)
    mu = np.asarray(inp["rwkv_mu"])[0]
    zz = np.zeros((64,), np.float32)
    mu_p = np.concatenate([mu[0:1600], zz, mu[1600:1664], zz, mu[1664:1792]])
    col = lambda v, n: f(np.asarray(v).reshape(n, 128).T)
    bm = np.asarray(inp["b_mod"])[0]
    par = [inp["rwkv_w0"], inp["rwkv_a0"], inp["rwkv_k_k"], inp["rwkv_k_a"], inp["rwkv_r_k"], inp["rwkv_ln_w"], inp["rwkv_ln_b"]]
    rw_par = np.stack([np.asarray(p)[0].reshape(4, 128).T for p in par], axis=1)
    return {
        "x": f(np.asarray(inp["x"])[b0:b0 + NB].reshape(NB * S, D)),
        "cT": f(np.asarray(inp["c"])[b0:b0 + NB].reshape(NB, 8, 128).transpose(2, 1, 0)),
        "w_mod": f(np.asarray(inp["w_mod"])[0]),
        "bmod_col": col(bm, 48),
        "bmod_g": f(np.broadcast_to(np.stack([bm[2048:3072], bm[5120:6144]])[None], (NB, 2, D))),
        "ln1_col": col(np.asarray(inp["ln1_g"])[0], 8), "ln2_col": col(np.asarray(inp["ln2_g"])[0], 8),
        "fin_g": f(inp["final_g"]),
        "w_qkv": f(w_in[:, 0:1536]), "w_f": f(w_in[:, 1536:1544]), "w_rw": f(w_rw), "w_gate": f(w_in[:, 3336:5384]),
        "fox_bf": f(np.asarray(inp["fox_b_f"])[0].reshape(8, 1)), "mu_col": col(mu_p, NRW),
        "rw_par": f(rw_par),
        "w2": f(np.asarray(inp["rwkv_w2"])[0]), "a2": f(np.asarray(inp["rwkv_a2"])[0]), "g2": f(np.asarray(inp["rwkv_g2"])[0]),
        "fox_w_out": f(np.asarray(inp["fox_w_out"])[0]), "rwkv_w_out": f(np.asarray(inp["rwkv_w_out"])[0]), "w_o": f(np.asarray(inp["w_o"])[0]),
        "w_up": f(np.asarray(inp["w_up"])[0]), "w_down": f(np.asarray(inp["w_down"])[0]),
        "conv_col": f(np.asarray(inp["conv_w"])[0].reshape(3, 22, 128).transpose(2, 1, 0)),
        "convb_col": col(np.asarray(inp["conv_b"])[0], 22),
    }


_NC_CACHE = {}


def kernel(**inputs):
    x = np.asarray(inputs["x"])
    Bt, S, _ = x.shape
    ncores = 8
    NB = Bt // ncores
    key = (NB, S)
    if key not in _NC_CACHE:
        _NC_CACHE[key] = build(NB, S)
    nc = _NC_CACHE[key]
    in_maps = [host_inputs(inputs, i * NB, NB) for i in range(ncores)]
    res = run_bass_kernel_spmd(nc, in_maps, core_ids=list(range(ncores)))
    out = np.concatenate([np.asarray(r["out"]).reshape(NB, S, D) for r in res.results], axis=0)
    return out.astype(np.float32)
```
